# Optimizing a Trainium2 kernel written in Bass

```python
import jax, jax.numpy as jnp
from jax import lax
import numpy as np

D_MODEL = 1024
BATCH = 16
SEQ = 4096
DEPTH = 4

FOURIER_GROUPS = 4
FOURIER_GROUP_DIM = 64
FOURIER_WIDTH = FOURIER_GROUPS * FOURIER_GROUP_DIM
RWKV_HEADS = 6
RWKV_HEAD_DIM = 64
RWKV_WIDTH = RWKV_HEADS * RWKV_HEAD_DIM
DECAY_LORA = 64
ICLR_LORA = 64
N_DIR = 2
RWKV_SHIFT_WIDTH = 3 * RWKV_WIDTH + N_DIR * (DECAY_LORA + ICLR_LORA)
GN_EPS = 64e-5
MLA_HEADS = 6
QK_NOPE = 64
QK_ROPE = 32
V_HEAD = 64
MLA_WIDTH = MLA_HEADS * V_HEAD
Q_LORA = 256
KV_LORA = 128
ROPE_THETA = 10000.0
Q_BLOCK = 128
N_BRANCH = 3
RMS_EPS = 1e-6

IN_SPLITS = (FOURIER_WIDTH, FOURIER_WIDTH,
             RWKV_SHIFT_WIDTH, RWKV_WIDTH,
             Q_LORA, KV_LORA, QK_ROPE, MLA_WIDTH,
             N_BRANCH * D_MODEL)
D_IN = sum(IN_SPLITS)

kernel_name = "fnet_rwkv7_mla_gated_hybrid_encoder"


def split_cols(z, sizes):
    out = []
    start = 0
    for s in sizes:
        out.append(z[..., start:start + s])
        start += s
    return out


def rms_norm(x, g):
    xf = x.astype(jnp.float32)
    y = xf * lax.rsqrt(jnp.mean(xf * xf, axis=-1, keepdims=True) + RMS_EPS)
    return (y * g.astype(jnp.float32)).astype(x.dtype)


def apply_rope(x, cos, sin):
    x1, x2 = jnp.split(x, 2, axis=-1)
    return jnp.concatenate([x1 * cos - x2 * sin, x1 * sin + x2 * cos], axis=-1)


def fourier_mix(za, w):
    B, T, _ = za.shape
    u = za.reshape(B, T, FOURIER_GROUPS, FOURIER_GROUP_DIM).astype(jnp.float32)
    f = jnp.fft.fft2(u, axes=(1, 3), norm="ortho").real.astype(za.dtype)
    return jnp.einsum('btgc,gce->btge', f, w).reshape(B, T, FOURIER_WIDTH)


def centred_shift(z, mu_prev, mu_next):
    prev = jnp.pad(z[:, :-1], ((0, 0), (1, 0), (0, 0)))
    nxt = jnp.pad(z[:, 1:], ((0, 0), (0, 1), (0, 0)))
    return z + mu_prev * (prev - z) + mu_next * (nxt - z)


def rwkv7_step(S, inp):
    r, w, k, v, a, b = inp
    sa = jnp.einsum('dbhvk,dbhk->dbhv', S, a)
    S = S * w[..., None, :] + sa[..., :, None] * b[..., None, :] + v[..., :, None] * k[..., None, :]
    y = jnp.einsum('dbhvk,dbhk->dbhv', S, r)
    return S, y


def rwkv7_bidir(zs, w0, w2, a0, a2, k_k, k_a, r_k, lnx_g, lnx_b):
    f32 = jnp.float32
    B, T, _ = zs.shape
    H, N, C = RWKV_HEADS, RWKV_HEAD_DIM, RWKV_WIDTH
    r, k, v, wd, ad = split_cols(zs.astype(f32), (C, C, C, N_DIR * DECAY_LORA, N_DIR * ICLR_LORA))
    wd = wd.reshape(B, T, N_DIR, DECAY_LORA)
    ad = ad.reshape(B, T, N_DIR, ICLR_LORA)
    w_log = -jax.nn.softplus(-(w0.astype(f32) + jnp.einsum('btdr,drc->btdc', jnp.tanh(wd), w2.astype(f32)))) - 0.5
    decay = jnp.exp(-jnp.exp(w_log))
    iclr = jax.nn.sigmoid(a0.astype(f32) + jnp.einsum('btdr,drc->btdc', ad, a2.astype(f32)))
    kk = (k * k_k.astype(f32)).reshape(B, T, H, N)
    kk = kk / jnp.maximum(jnp.sqrt(jnp.sum(kk * kk, axis=-1, keepdims=True)), 1e-12)
    kk = kk.reshape(B, T, C)
    k_dir = k[:, :, None, :] * (1.0 + (iclr - 1.0) * k_a.astype(f32))
    b_dir = kk[:, :, None, :] * iclr

    def shared(t):
        return jnp.stack([t, t[:, ::-1]], axis=0)

    def per_dir(t):
        return jnp.stack([t[:, :, 0], t[:, ::-1, 1]], axis=0)

    def time_major(t):
        return t.reshape(N_DIR, B, T, H, N).transpose(2, 0, 1, 3, 4)

    xs = (time_major(shared(r)), time_major(per_dir(decay)), time_major(per_dir(k_dir)),
          time_major(shared(v)), time_major(shared(-kk)), time_major(per_dir(b_dir)))
    S0 = jnp.zeros((N_DIR, B, H, N, N), f32)
    _, y = lax.scan(rwkv7_step, S0, xs)
    y = y.transpose(1, 2, 0, 3, 4)
    y = y[0] + y[1][:, ::-1]
    mu = jnp.mean(y, axis=-1, keepdims=True)
    var = jnp.mean(jnp.square(y - mu), axis=-1, keepdims=True)
    y = ((y - mu) * lax.rsqrt(var + GN_EPS)).reshape(B, T, C) * lnx_g.astype(f32) + lnx_b.astype(f32)
    bonus = jnp.einsum('bthn,btdhn,hn->bth', r.reshape(B, T, H, N),
                       k_dir.reshape(B, T, N_DIR, H, N), r_k.astype(f32))
    return y + (bonus[..., None] * v.reshape(B, T, H, N)).reshape(B, T, C)


def mla(zq, zkv, zkr, q_norm_g, w_uq, kv_norm_g, w_ukv, cos, sin):
    B, T, _ = zq.shape
    c_q = rms_norm(zq, q_norm_g)
    q = (c_q @ w_uq).reshape(B, T, MLA_HEADS, QK_NOPE + QK_ROPE)
    q_nope = q[..., :QK_NOPE]
    q_rope = apply_rope(q[..., QK_NOPE:], cos[:, :, None, :], sin[:, :, None, :])
    c_kv = rms_norm(zkv, kv_norm_g)
    kv = (c_kv @ w_ukv).reshape(B, T, MLA_HEADS, QK_NOPE + V_HEAD)
    k_nope = kv[..., :QK_NOPE]
    v = kv[..., QK_NOPE:]
    k_rope = apply_rope(zkr, cos, sin)
    nb = T // Q_BLOCK
    qn_blocks = q_nope.reshape(B, nb, Q_BLOCK, MLA_HEADS, QK_NOPE).transpose(1, 0, 2, 3, 4)
    qr_blocks = q_rope.reshape(B, nb, Q_BLOCK, MLA_HEADS, QK_ROPE).transpose(1, 0, 2, 3, 4)
    scale = (QK_NOPE + QK_ROPE) ** -0.5

    def attend(blk):
        qn, qr = blk
        s = jnp.einsum('bqhd,bkhd->bhqk', qn, k_nope) + jnp.einsum('bqhd,bkd->bhqk', qr, k_rope)
        p = jax.nn.softmax(s.astype(jnp.float32) * scale, axis=-1).astype(v.dtype)
        return jnp.einsum('bhqk,bkhd->bqhd', p, v)

    o = lax.map(attend, (qn_blocks, qr_blocks))
    return o.transpose(1, 0, 2, 3, 4).reshape(B, T, MLA_WIDTH)


def setup_inputs(seed: int = 0) -> dict:
    key = jax.random.key(seed)
    ks = jax.random.split(key, 32)
    f32 = jnp.float32
    nrm = lambda k, shape, s: jax.random.normal(k, shape, f32) * s
    x = jax.random.normal(ks[0], (BATCH, SEQ, D_MODEL), f32)
    offset = jax.random.randint(ks[1], (BATCH, 1), 0, 1024, dtype=jnp.int32)
    positions = offset + jnp.arange(SEQ, dtype=jnp.int32)[None, :]
    decay_base = jnp.linspace(-6.0, -1.0, RWKV_WIDTH, dtype=f32)
    return {
        "x": x,
        "positions": positions,
        "norm_g": 1.0 + nrm(ks[2], (DEPTH, D_MODEL), 0.05),
        "w_in": nrm(ks[3], (DEPTH, D_MODEL, D_IN), D_MODEL ** -0.5),
        "fourier_w": nrm(ks[4], (DEPTH, FOURIER_GROUPS, FOURIER_GROUP_DIM, FOURIER_GROUP_DIM), FOURIER_GROUP_DIM ** -0.5),
        "shift_mu_prev": jax.random.uniform(ks[5], (DEPTH, RWKV_SHIFT_WIDTH), f32, 0.0, 0.5),
        "shift_mu_next": jax.random.uniform(ks[6], (DEPTH, RWKV_SHIFT_WIDTH), f32, 0.0, 0.5),
        "decay_w0": decay_base + nrm(ks[7], (DEPTH, N_DIR, RWKV_WIDTH), 0.1),
        "decay_w2": nrm(ks[8], (DEPTH, N_DIR, DECAY_LORA, RWKV_WIDTH), 0.1 * DECAY_LORA ** -0.5),
        "iclr_a0": nrm(ks[9], (DEPTH, N_DIR, RWKV_WIDTH), 0.1),
        "iclr_a2": nrm(ks[10], (DEPTH, N_DIR, ICLR_LORA, RWKV_WIDTH), 0.1 * ICLR_LORA ** -0.5),
        "key_k": 0.85 + nrm(ks[11], (DEPTH, RWKV_WIDTH), 0.02),
        "key_a": 1.0 + nrm(ks[12], (DEPTH, RWKV_WIDTH), 0.02),
        "bonus_r_k": nrm(ks[13], (DEPTH, RWKV_HEADS, RWKV_HEAD_DIM), 0.1),
        "lnx_g": 1.0 + nrm(ks[14], (DEPTH, RWKV_WIDTH), 0.05),
        "lnx_b": nrm(ks[15], (DEPTH, RWKV_WIDTH), 0.01),
        "q_norm_g": 1.0 + nrm(ks[16], (DEPTH, Q_LORA), 0.05),
        "w_uq": nrm(ks[17], (DEPTH, Q_LORA, MLA_HEADS * (QK_NOPE + QK_ROPE)), Q_LORA ** -0.5),
        "kv_norm_g": 1.0 + nrm(ks[18], (DEPTH, KV_LORA), 0.05),
        "w_ukv": nrm(ks[19], (DEPTH, KV_LORA, MLA_HEADS * (QK_NOPE + V_HEAD)), KV_LORA ** -0.5),
        "proj_a": nrm(ks[20], (DEPTH, FOURIER_WIDTH, D_MODEL), FOURIER_WIDTH ** -0.5),
        "proj_b": nrm(ks[21], (DEPTH, RWKV_WIDTH, D_MODEL), RWKV_WIDTH ** -0.5),
        "proj_c": nrm(ks[22], (DEPTH, MLA_WIDTH, D_MODEL), MLA_WIDTH ** -0.5),
        "w_out": nrm(ks[23], (DEPTH, D_MODEL, D_MODEL), D_MODEL ** -0.5),
        "final_g": 1.0 + nrm(ks[24], (D_MODEL,), 0.05),
    }


def reference(x, positions, norm_g, w_in, fourier_w, shift_mu_prev, shift_mu_next, decay_w0, decay_w2,
              iclr_a0, iclr_a2, key_k, key_a, bonus_r_k, lnx_g, lnx_b, q_norm_g, w_uq, kv_norm_g, w_ukv,
              proj_a, proj_b, proj_c, w_out, final_g):
    B, T, D = x.shape
    f32 = jnp.float32
    inv_freq = ROPE_THETA ** (-jnp.arange(0, QK_ROPE, 2, dtype=f32) / QK_ROPE)
    ang = positions.astype(f32)[..., None] * inv_freq
    cos = jnp.cos(ang).astype(x.dtype)
    sin = jnp.sin(ang).astype(x.dtype)
    for l in range(DEPTH):
        h = rms_norm(x, norm_g[l])
        z = h @ w_in[l]
        za, za_gate, z_rwkv, zb_gate, zq, zkv, zkr, zc_gate, z_merge = split_cols(z, IN_SPLITS)
        ya = fourier_mix(za, fourier_w[l]) * jax.nn.silu(za_gate)
        zs = centred_shift(z_rwkv, shift_mu_prev[l], shift_mu_next[l])
        yb = rwkv7_bidir(zs, decay_w0[l], decay_w2[l], iclr_a0[l], iclr_a2[l], key_k[l], key_a[l],
                         bonus_r_k[l], lnx_g[l], lnx_b[l]).astype(x.dtype) * jax.nn.silu(zb_gate)
        yc = mla(zq, zkv, zkr, q_norm_g[l], w_uq[l], kv_norm_g[l], w_ukv[l], cos, sin) * jax.nn.silu(zc_gate)
        g = jax.nn.sigmoid(z_merge.reshape(B, T, N_BRANCH, D))
        merged = (g[:, :, 0] * (ya @ proj_a[l]) + g[:, :, 1] * (yb @ proj_b[l])
                  + g[:, :, 2] * (yc @ proj_c[l]))
        x = x + merged @ w_out[l]
    return rms_norm(x, final_g)
```

```python
import contextlib
import numpy as np
import ml_dtypes
import concourse.bass as bass
import concourse.mybir as mybir
from concourse.bass_utils import run_bass_kernel_spmd

F32 = mybir.dt.float32
BF = mybir.dt.bfloat16
I32 = mybir.dt.int32
AF = mybir.ActivationFunctionType
ALU = mybir.AluOpType
AX = mybir.AxisListType

D = 1024
NCORES = 8
CH = 128
RDT = BF
STOP = None


class StopBuild(Exception):
    pass


def chk(tag):
    if STOP == tag:
        raise StopBuild()


class Dep:
    __slots__ = ("w", "r")

    def __init__(self):
        self.w = {}
        self.r = {}


class Tl:
    def __init__(self, t, d=None, ps=False):
        self.t = t
        self.d = d or Dep()
        self.ps = ps

    def __getitem__(self, idx):
        return self.t[idx]


class K:
    NDMA = 24

    def __init__(self, nc):
        self.nc = nc
        self.es = contextlib.ExitStack()
        self.eng = dict(pe=nc.tensor, act=nc.scalar, dve=nc.vector, pool=nc.gpsimd, sp=nc.sync)
        self.sem = {}
        self.cnt = {}
        for e in ["pe", "act", "dve", "pool"]:
            self.sem[e] = self.es.enter_context(nc.semaphore("s_" + e))
            self.cnt[e] = 0
        for i in range(self.NDMA):
            key = ("d", i)
            self.sem[key] = self.es.enter_context(nc.semaphore("d%d" % i))
            self.cnt[key] = 0
        self.rr = 0
        self.waited = {}
        self.ninstr = 0

    def _wait(self, e, reads, writes):
        need = {}
        for t in reads:
            for key, v in t.d.w.items():
                need[key] = max(need.get(key, 0), v)
            if t.ps:
                for key, v in t.d.r.items():
                    need[key] = max(need.get(key, 0), v)
        for t in writes:
            for key, v in t.d.w.items():
                need[key] = max(need.get(key, 0), v)
            for key, v in t.d.r.items():
                need[key] = max(need.get(key, 0), v)
        for key, v in need.items():
            if key == e and e == "pe":
                continue
            if self.waited.get((e, key), 0) < v:
                self.eng[e].wait_ge(self.sem[key], v)
                self.waited[(e, key)] = v
                self.ninstr += 1

    def _done(self, key, v, reads, writes):
        for t in writes:
            t.d.w = {key: v}
            t.d.r = {}
        for t in reads:
            if t.d.r.get(key, 0) < v:
                t.d.r[key] = v

    def op(self, e, fn, r, w):
        self._wait(e, r, w)
        ins = fn()
        self.cnt[e] += 1
        ins.then_inc(self.sem[e], 1)
        self._done(e, self.cnt[e], r, w)
        self.ninstr += 1
        return ins

    def dma(self, q, out, in_, r, w):
        self._wait(q, r, w)
        key = ("d", self.rr)
        self.rr = (self.rr + 1) % self.NDMA
        if self.waited.get((q, key), 0) < self.cnt[key]:
            self.eng[q].wait_ge(self.sem[key], self.cnt[key])
            self.waited[(q, key)] = self.cnt[key]
        ins = self.eng[q].dma_start(out=out, in_=in_)
        self.cnt[key] += 16
        ins.then_inc(self.sem[key], 16)
        self._done(key, self.cnt[key], r, w)
        self.ninstr += 1

    def barrier(self, engines=("pe", "act", "dve", "pool", "sp")):
        for e in engines:
            for key, v in self.cnt.items():
                if key == e or v == 0:
                    continue
                if self.waited.get((e, key), 0) < v:
                    self.eng[e].wait_ge(self.sem[key], v)
                    self.waited[(e, key)] = v

    def mm(self, out, lhsT, rhs, start, stop, r, w):
        return self.op("pe", lambda: self.nc.tensor.matmul(out, lhsT, rhs, start=start, stop=stop), r, w)

    def tr(self, out, in_, ident, r, w):
        return self.op("pe", lambda: self.nc.tensor.transpose(out, in_, ident), r, w)

    def act(self, out, in_, func, r, w, **kw):
        return self.op("act", lambda: self.nc.scalar.activation(out=out, in_=in_, func=func, **kw), r, w)

    def ts(self, e, out, in0, s1, s2, op0, op1, r, w):
        eng = self.eng[e]
        if op1 is None:
            return self.op(e, lambda: eng.tensor_scalar(out=out, in0=in0, scalar1=s1, scalar2=None, op0=op0), r, w)
        return self.op(e, lambda: eng.tensor_scalar(out=out, in0=in0, scalar1=s1, scalar2=s2, op0=op0, op1=op1), r, w)

    def tt(self, e, out, in0, in1, op, r, w):
        eng = self.eng[e]
        return self.op(e, lambda: eng.tensor_tensor(out=out, in0=in0, in1=in1, op=op), r, w)

    def stt(self, out, in0, scalar, in1, op0, op1, r, w):
        return self.op("dve", lambda: self.nc.vector.scalar_tensor_tensor(out=out, in0=in0, scalar=scalar, in1=in1, op0=op0, op1=op1), r, w)

    def cp(self, e, out, in_, r, w):
        if e == "act":
            return self.op(e, lambda: self.nc.scalar.copy(out=out, in_=in_), r, w)
        eng = self.eng[e]
        return self.op(e, lambda: eng.tensor_copy(out=out, in_=in_), r, w)

    def recip(self, out, in_, r, w):
        return self.op("dve", lambda: self.nc.vector.reciprocal(out=out, in_=in_), r, w)

    def memset(self, e, ap, val, w):
        eng = self.eng[e]
        return self.op(e, lambda: eng.memset(ap, val), [], w)


O_ZA, O_ZAG, O_ZR, O_ZBG, O_ZQ, O_ZKV, O_ZKR, O_ZCG, O_ZM = 0, 256, 512, 1920, 2304, 2560, 2688, 2720, 3104
M_ZR, M_ZQ, M_ZKV, M_KR1, M_KR2, M_ZA, CM = 0, 1408, 1664, 1792, 1888, 1984, 2240
NPP = 60
C_ID, C_SF, C_IF, C_SB, C_IB, C_OB, C_OA, C_SEG, C_C64, C_S64, C_INVF, C_SGN, C_E96, NCST = (
    0, 128, 256, 384, 512, 640, 768, 896, 1408, 1472, 1536, 1537, 1538, 1538 + 97)


def build(T, NL, NS, debug=False):
    nc = bass.Bass("TRN2", target_bir_lowering=False)
    k = K(nc)
    try:
        return _build(nc, k, T, NL, NS, debug)
    except StopBuild:
        k.barrier()
        return nc, k


def _build(nc, k, T, NL, NS, debug):
    NSL = T // 512
    NCK = T // CH
    NT = T // 128
    okind = "ExternalOutput" if debug else "Internal"

    def din(name, shape, dt):
        return Tl(nc.dram_tensor(name, shape, dt, kind="ExternalInput").ap())

    x_in = din("x", [NS, T, D], F32)
    pos_in = din("pos", [NS, 1, T], I32)
    cst_in = din("cst", [128, NCST], F32)
    dftc_in = din("dftc", [T, T], BF)
    dfts_in = din("dfts", [T, T], BF)
    wmix_in = din("wmix", [NL, D, CM], F32)
    wgate_in = din("wgate", [NL, D, 4096], F32)
    wproj_in = din("wproj", [NL, D, D], F32)
    wout_in = din("wout", [NL, D, D], F32)
    pp_in = din("pp", [NL, 128, NPP], F32)
    fw_in = din("fw", [NL, 64, 256], F32)
    w2_in = din("w2", [NL, 128, 384], F32)
    a2_in = din("a2", [NL, 128, 384], F32)
    wuq_in = din("wuq", [NL, 256, 576], F32)
    wuqs_in = din("wuqs", [NL, 256, 576], F32)
    wukvk_in = din("wukvk", [NL, 128, 384], F32)
    wukvv_in = din("wukvv", [NL, 128, 384], F32)
    fg_in = din("fg", [1, D], F32)
    out_d = Tl(nc.dram_tensor("out", [NS, T, D], F32, kind="ExternalOutput").ap())

    def dscr(name, shape, dt):
        return Tl(nc.dram_tensor(name, shape, dt, kind=okind).ap())

    X = [dscr("X%d" % s, [T, D], F32) for s in range(NS)]
    HT = [dscr("HT%d" % s, [128, 8, T], BF) for s in range(NS)]
    ZR = [dscr("ZR%d" % s, [1408, T], F32) for s in range(NS)]
    ZA = [dscr("ZA%d" % s, [T, 256], BF) for s in range(NS)]
    QT = [dscr("QT%d" % s, [6, 97, T], BF) for s in range(NS)]
    KT = [dscr("KT%d" % s, [6, 97, T], BF) for s in range(NS)]
    VA = [dscr("VA%d" % s, [T, 6 * 65], BF) for s in range(NS)]
    YA = [dscr("YA%d" % s, [256, T], BF) for s in range(NS)]
    YB = [dscr("YB%d" % s, [384, T], F32) for s in range(NS)]
    YC = [dscr("YC%d" % s, [384, T], BF) for s in range(NS)]

    es0 = k.es

    uid = [0]

    def sb(es, name, shape, dt):
        uid[0] += 1
        return Tl(es.enter_context(nc.sbuf_tensor("sb%d_%s" % (uid[0], name), shape, dt)))

    cst = sb(es0, "cst", [128, NCST], F32)
    identb = sb(es0, "identb", [128, 128], BF)
    onesb = sb(es0, "onesb", [128, 128], BF)
    pp = sb(es0, "pp", [128, NPP], F32)
    ppx = sb(es0, "ppx", [128, 40], F32)
    CSD = [dscr("CSD%d" % s, [2, 32, T], F32) for s in range(NS)]
    PS = [Tl(es0.enter_context(nc.psum_tensor("ps%d" % i, [128, 512], F32)), ps=True) for i in range(7)]
    PSB = Tl(es0.enter_context(nc.psum_tensor("psb", [128, 1024], BF)), ps=True)

    k.dma("sp", cst[:], cst_in[:, :], [cst_in], [cst])
    k.cp("dve", identb[:], cst[:, C_ID:C_ID + 128], [cst], [identb])
    k.cp("dve", onesb[:], cst[:, C_OA:C_OA + 128], [cst], [onesb])
    ident = cst[:, C_ID:C_ID + 128]

    with contextlib.ExitStack() as es:
        posi = sb(es, "posi", [128, T], I32)
        ang = sb(es, "ang", [128, T], F32)
        kq = sb(es, "kq", [128, T], F32)
        rr_ = sb(es, "rr", [128, T], F32)
        cs1t = sb(es, "cs1t", [128, T], F32)
        cs2t = sb(es, "cs2t", [128, T], F32)
        CS1 = [cs1t] * NS
        CS2 = [cs2t] * NS
        P = slice(64, 96)
        for s in range(NS):
            for p in range(64, 96):
                k.dma("sp", posi[p:p + 1, :], pos_in[s, :, :], [pos_in], [posi])
            k.cp("dve", ang[P, :], posi[P, :], [posi], [ang])
            k.ts("dve", ang[P, :], ang[P, :], cst[P, C_INVF:C_INVF + 1], None, ALU.mult, None, [ang, cst], [ang])
            k.ts("dve", kq[P, :], ang[P, :], float(1.0 / (2 * np.pi)), None, ALU.mult, None, [ang], [kq])
            k.ts("dve", kq[P, :], kq[P, :], 12582912.0, None, ALU.add, None, [kq], [kq])
            k.ts("dve", kq[P, :], kq[P, :], -12582912.0, None, ALU.add, None, [kq], [kq])
            c1 = 6.28125
            c2 = float(np.float32(2 * np.pi - c1))
            c3 = float(2 * np.pi - c1 - np.float64(np.float32(2 * np.pi - c1)))
            k.stt(rr_[P, :], kq[P, :], -c1, ang[P, :], ALU.mult, ALU.add, [kq, ang], [rr_])
            k.stt(rr_[P, :], kq[P, :], -c2, rr_[P, :], ALU.mult, ALU.add, [kq, rr_], [rr_])
            k.stt(rr_[P, :], kq[P, :], -c3, rr_[P, :], ALU.mult, ALU.add, [kq, rr_], [rr_])
            k.ts("dve", rr_[P, :], rr_[P, :], 3.1415925, -3.1415925, ALU.min, ALU.max, [rr_], [rr_])
            k.act(CS2[s][P, :], rr_[P, :], AF.Sin, [rr_], [CS2[s]])
            k.ts("dve", CS2[s][P, :], CS2[s][P, :], cst[P, C_SGN:C_SGN + 1], None, ALU.mult, None, [CS2[s], cst], [CS2[s]])
            k.ts("dve", kq[P, :], rr_[P, :], -1.0, None, ALU.mult, None, [rr_], [kq])
            k.tt("dve", kq[P, :], kq[P, :], rr_[P, :], ALU.max, [kq, rr_], [kq])
            k.ts("dve", kq[P, :], kq[P, :], -1.0, float(np.pi / 2), ALU.mult, ALU.add, [kq], [kq])
            k.act(CS1[s][P, :], kq[P, :], AF.Sin, [kq], [CS1[s]])
            k.dma("sp", CSD[s][0], CS1[s][P, :], [CS1[s]], [CSD[s]])
            k.dma("sp", CSD[s][1], CS2[s][P, :], [CS2[s]], [CSD[s]])
        k.barrier()

    scale = float(96 ** -0.5)
    if STOP == "rope":
        return nc, k

    for l in range(NL):
        Xsrc = [Tl(x_in.t[s], x_in.d) for s in range(NS)] if l == 0 else X
        last = l == NL - 1
        k.dma("sp", pp[:], pp_in[l], [pp_in], [pp])
        k.tt("dve", ppx[:, 0:11], pp[:, 8:19], pp[:, 19:30], ALU.add, [pp], [ppx])
        k.ts("dve", ppx[:, 0:11], ppx[:, 0:11], -1.0, 1.0, ALU.mult, ALU.add, [ppx], [ppx])
        k.ts("dve", ppx[:, 11:14], pp[:, 45:48], -1.0, 1.0, ALU.mult, ALU.add, [pp], [ppx])
        k.ts("dve", ppx[:, 14:17], pp[:, 45:48], -2.0, 2.0, ALU.mult, ALU.add, [pp], [ppx])

        with contextlib.ExitStack() as es:
            wm = sb(es, "wm", [128, 8, CM], BF)
            stg = [sb(es, "stg%d" % i, [128, CM], F32) for i in range(2)]
            for c in range(8):
                st = stg[c % 2]
                k.dma("sp", st[:], wmix_in[l, c * 128:(c + 1) * 128, :], [wmix_in], [st])
                k.ts("pool" if c % 2 else "dve", wm[:, c, :], st[:], pp[:, c:c + 1], None, ALU.mult, None, [st, pp], [wm])
            wuq = sb(es, "wuq", [128, 2, 576], BF)
            wuqs = sb(es, "wuqs", [128, 2, 576], BF)
            wkk = sb(es, "wkk", [128, 384], BF)
            wkv = sb(es, "wkv", [128, 384], BF)
            for c in range(2):
                k.dma("sp", stg[0][:, 0:576], wuq_in[l, c * 128:(c + 1) * 128, :], [wuq_in], [stg[0]])
                k.ts("dve", wuq[:, c, :], stg[0][:, 0:576], pp[:, 57 + c:58 + c], None, ALU.mult, None, [stg[0], pp], [wuq])
                k.dma("sp", stg[1][:, 0:576], wuqs_in[l, c * 128:(c + 1) * 128, :], [wuqs_in], [stg[1]])
                k.ts("dve", wuqs[:, c, :], stg[1][:, 0:576], pp[:, 57 + c:58 + c], None, ALU.mult, None, [stg[1], pp], [wuqs])
            k.dma("sp", stg[0][:, 0:384], wukvk_in[l], [wukvk_in], [stg[0]])
            k.ts("dve", wkk[:], stg[0][:, 0:384], pp[:, 59:60], None, ALU.mult, None, [stg[0], pp], [wkk])
            k.dma("sp", stg[1][:, 0:384], wukvv_in[l], [wukvv_in], [stg[1]])
            k.ts("dve", wkv[:], stg[1][:, 0:384], pp[:, 59:60], None, ALU.mult, None, [stg[1], pp], [wkv])

            chk("p1a")
            xb = [sb(es, "xb%d" % i, [128, D], F32) for i in range(2)]
            junk = sb(es, "junk", [128, D], F32)
            st4 = sb(es, "st4", [128, 8], F32)
            hb = sb(es, "hb", [128, D], BF)
            hT = sb(es, "hT", [128, 8, 512], BF)
            zo = [sb(es, "zo%d" % i, [128, 512], F32) for i in range(3)]
            zq = sb(es, "zq", [128, 2, 512], F32)
            zkv = sb(es, "zkv", [128, 512], F32)
            sq = sb(es, "sq", [128, 2, 512], F32)
            rb = sb(es, "rb", [128, 512], F32)
            cq = sb(es, "cq", [128, 2, 512], BF)
            ckv = sb(es, "ckv", [128, 512], BF)
            t1 = sb(es, "t1", [128, 512], F32)
            t2 = sb(es, "t2", [128, 512], F32)
            qts = sb(es, "qts", [128, 6, 512], BF)
            kts = sb(es, "kts", [128, 6, 512], BF)
            q32 = sb(es, "q32", [128, 512], F32)
            vas = sb(es, "vas", [128, 4, 6, 65], BF)
            zas = sb(es, "zas", [128, 4, 256], BF)
            kmx = sb(es, "kmx", [128, 6, NSL * NS + 1], F32)
            csl = sb(es, "csl", [128, 2, 512], F32)
            e96 = cst[0:96, C_E96:C_E96 + 97]
            onesr = sb(es, "onesr", [128, 512], F32)
            k.memset("pool", onesr[:], 1.0, [onesr])
            k.memset("pool", vas[:], 1.0, [vas])
            k.memset("pool", kts[:], 1.0, [kts])
            k.memset("pool", qts[:], 0.0, [qts])
            zi = 0
            for s in range(NS):
                for sl in range(NSL):
                    S0 = sl * 512
                    SL = slice(S0, S0 + 512)
                    k.dma("sp", csl[64:96, 0, :], CSD[s][0, :, SL], [CSD[s]], [csl])
                    k.dma("sp", csl[64:96, 1, :], CSD[s][1, :, SL], [CSD[s]], [csl])
                    for tt in range(4):
                        tok0 = S0 + tt * 128
                        xt = xb[tt % 2]
                        k.dma("sp", xt[:], Xsrc[s][tok0:tok0 + 128, :], [Xsrc[s]], [xt])
                        if l == 0:
                            k.dma("sp", X[s][tok0:tok0 + 128, :], xt[:], [xt], [X[s]])
                        k.act(junk[:], xt[:], AF.Square, [xt], [junk, st4], accum_out=st4[:, 0:1])
                        k.ts("dve", st4[:, 1:2], st4[:, 0:1], 1.0 / D, 1e-6, ALU.mult, ALU.add, [st4], [st4])
                        k.act(st4[:, 2:3], st4[:, 1:2], AF.Sqrt, [st4], [st4])
                        k.recip(st4[:, 3:4], st4[:, 2:3], [st4], [st4])
                        k.ts("dve", hb[:], xt[:], st4[:, 3:4], None, ALU.mult, None, [xt, st4], [hb])
                        for c in range(8):
                            k.tr(PSB[:, c * 128:(c + 1) * 128], hb[:, c * 128:(c + 1) * 128], identb[:], [hb, identb], [PSB])
                        k.cp("act", hT[:, :, tt * 128:(tt + 1) * 128], PSB[:, :].rearrange("p (c t) -> p c t", c=8), [PSB], [hT])
                    k.dma("sp", HT[s][:, :, SL], hT[:], [hT], [HT[s]])
                    chk("p1b")
                    for cb in range(11):
                        ps = PS[cb % 3]
                        for c in range(8):
                            k.mm(ps[:], wm[:, c, M_ZR + cb * 128:M_ZR + (cb + 1) * 128], hT[:, c, :], c == 0, c == 7, [wm, hT], [ps])
                        z = zo[zi % 3]
                        zi += 1
                        k.cp("act" if cb % 2 else "dve", z[:], ps[:], [ps], [z])
                        k.dma("sp", ZR[s][cb * 128:(cb + 1) * 128, SL], z[:], [z], [ZR[s]])
                    chk("p1c")
                    for tt in range(4):
                        ps = PS[3]
                        for c in range(8):
                            k.mm(ps[:, 0:256], hT[:, c, tt * 128:(tt + 1) * 128], wm[:, c, M_ZA:M_ZA + 256], c == 0, c == 7, [wm, hT], [ps])
                        chk("p1c1")
                        k.cp("dve", zas[:, tt, :], ps[:, 0:256], [ps], [zas])
                        chk("p1c3")
                    chk("p1c2")
                    k.dma("sp", ZA[s][SL, :].rearrange("(a p) c -> p a c", p=128), zas[:], [zas], [ZA[s]])
                    chk("p1d")
                    for b in range(2):
                        ps = PS[4]
                        for c in range(8):
                            k.mm(ps[:], wm[:, c, M_ZQ + b * 128:M_ZQ + (b + 1) * 128], hT[:, c, :], c == 0, c == 7, [wm, hT], [ps])
                        k.cp("dve", zq[:, b, :], ps[:], [ps], [zq])
                        k.act(sq[:, b, :], zq[:, b, :], AF.Square, [zq], [sq])
                    ps = PS[4]
                    for b in range(2):
                        k.mm(ps[:], cst[:, C_OA:C_OA + 128], sq[:, b, :], b == 0, b == 1, [cst, sq], [ps])
                    k.ts("dve", rb[:], ps[:], 1.0 / 256, 1e-6, ALU.mult, ALU.add, [ps], [rb])
                    k.act(rb[:], rb[:], AF.Sqrt, [rb], [rb])
                    k.recip(rb[:], rb[:], [rb], [rb])
                    for b in range(2):
                        k.tt("dve", cq[:, b, :], zq[:, b, :], rb[:], ALU.mult, [zq, rb], [cq])
                    ps = PS[5]
                    for c in range(8):
                        k.mm(ps[:], wm[:, c, M_ZKV:M_ZKV + 128], hT[:, c, :], c == 0, c == 7, [wm, hT], [ps])
                    k.cp("dve", zkv[:], ps[:], [ps], [zkv])
                    k.act(sq[:, 0, :], zkv[:], AF.Square, [zkv], [sq])
                    ps = PS[5]
                    k.mm(ps[:], cst[:, C_OA:C_OA + 128], sq[:, 0, :], True, True, [cst, sq], [ps])
                    k.ts("dve", rb[:], ps[:], 1.0 / 128, 1e-6, ALU.mult, ALU.add, [ps], [rb])
                    k.act(rb[:], rb[:], AF.Sqrt, [rb], [rb])
                    k.recip(rb[:], rb[:], [rb], [rb])
                    k.tt("dve", ckv[:], zkv[:], rb[:], ALU.mult, [zkv, rb], [ckv])
                    chk("p1e")
                    R = slice(64, 96)
                    pa, pb = PS[3], PS[4]
                    for c in range(8):
                        k.mm(pa[0:96, :], wm[:, c, M_KR1:M_KR1 + 96], hT[:, c, :], c == 0, c == 7, [wm, hT], [pa])
                    for c in range(8):
                        k.mm(pb[0:96, :], wm[:, c, M_KR2:M_KR2 + 96], hT[:, c, :], c == 0, c == 7, [wm, hT], [pb])
                    k.tt("dve", t1[R, :], pa[R, :], csl[R, 0, :], ALU.mult, [pa, csl], [t1])
                    k.tt("dve", t2[R, :], pb[R, :], csl[R, 1, :], ALU.mult, [pb, csl], [t2])
                    k.tt("dve", t1[R, :], t1[R, :], t2[R, :], ALU.add, [t1, t2], [t1])
                    for h in range(6):
                        k.cp("pool", kts[R, h, :], t1[R, :], [t1], [kts])
                    chk("p1f")
                    for h in range(6):
                        pq, pqs, pk = PS[0], PS[1], PS[2]
                        for b in range(2):
                            k.mm(pq[0:96, :], wuq[:, b, h * 96:(h + 1) * 96], cq[:, b, :], b == 0, b == 1, [wuq, cq], [pq])
                        for b in range(2):
                            k.mm(pqs[0:96, :], wuqs[:, b, h * 96:(h + 1) * 96], cq[:, b, :], b == 0, b == 1, [wuqs, cq], [pqs])
                        k.mm(pk[0:64, :], wkk[:, h * 64:(h + 1) * 64], ckv[:], True, True, [wkk, ckv], [pk])
                        k.ts("dve", q32[0:64, :], pq[0:64, :], scale, None, ALU.mult, None, [pq], [q32])
                        k.tt("dve", t1[R, :], pq[R, :], csl[R, 0, :], ALU.mult, [pq, csl], [t1])
                        k.tt("dve", t2[R, :], pqs[R, :], csl[R, 1, :], ALU.mult, [pqs, csl], [t2])
                        k.stt(q32[R, :], t1[R, :], 1.0, t2[R, :], ALU.mult, ALU.add, [t1, t2], [q32])
                        k.ts("dve", q32[R, :], q32[R, :], scale, None, ALU.mult, None, [q32], [q32])
                        k.cp("act", qts[0:96, h, :], q32[0:96, :], [q32], [qts])
                        k.cp("act", kts[0:64, h, :], pk[0:64, :], [pk], [kts])
                        k.tt("pool", t2[0:96, :], qts[0:96, h, :], qts[0:96, h, :], ALU.mult, [qts], [t2])
                        pn = PS[5]
                        k.mm(pn[0:97, :], e96, t2[0:96, :], True, True, [cst, t2], [pn])
                        k.act(t1[96:97, :], pn[96:97, :], AF.Sqrt, [pn], [t1])
                        k.ts("dve", qts[96:97, h, :], t1[96:97, :], -1.0, None, ALU.mult, None, [t1], [qts])
                        k.tt("pool", t2[0:96, :], kts[0:96, h, :], kts[0:96, h, :], ALU.mult, [kts], [t2])
                        pn = PS[6]
                        k.mm(pn[0:97, :], e96, t2[0:96, :], True, True, [cst, t2], [pn])
                        k.op("dve", lambda pn=pn, h=h, s=s, sl=sl: nc.vector.tensor_reduce(
                            out=kmx[96:97, h, s * NSL + sl:s * NSL + sl + 1], in_=pn[96:97, :], axis=AX.X, op=ALU.max), [pn], [kmx])
                    chk("p1g")
                    k.dma("sp", QT[s][:, :, SL].rearrange("h p t -> p h t"), qts[0:97, :, :], [qts], [QT[s]])
                    k.dma("sp", KT[s][:, 0:96, SL].rearrange("h p t -> p h t"), kts[0:96, :, :], [kts], [KT[s]])
                    chk("p1h")
                    for tt in range(4):
                        ps = PS[3]
                        k.mm(ps[:, 0:384], ckv[:, tt * 128:(tt + 1) * 128], wkv[:], True, True, [ckv, wkv], [ps])
                        k.cp("act", vas[:, tt, :, 0:64], ps[:, 0:384].rearrange("p (h v) -> p h v", h=6), [ps], [vas])
                    k.dma("sp", VA[s][SL, :].rearrange("(a p) c -> p a c", p=128), vas[:].rearrange("p a h v -> p a (h v)"), [vas], [VA[s]])
                for h in range(6):
                    k.op("dve", lambda h=h, s=s: nc.vector.tensor_reduce(
                        out=kmx[96:97, h, NSL * NS:NSL * NS + 1], in_=kmx[96:97, h, s * NSL:(s + 1) * NSL], axis=AX.X, op=ALU.max), [kmx], [kmx])
                    k.act(kmx[96:97, h, NSL * NS:NSL * NS + 1], kmx[96:97, h, NSL * NS:NSL * NS + 1], AF.Sqrt, [kmx], [kmx])
                    k.ts("dve", kts[96:97, h, :], onesr[96:97, :], kmx[96:97, h, NSL * NS:NSL * NS + 1], None, ALU.mult, None, [kmx, onesr], [kts])
                for sl in range(NSL):
                    k.dma("sp", KT[s][:, 96:97, sl * 512:(sl + 1) * 512].rearrange("h p t -> p h t"), kts[96:97, :, :], [kts], [KT[s]])
            k.barrier()

        if STOP == "p1":
            return nc, k
        for s in range(NS):
            fourier_phase(nc, k, sb, l, s, T, cst, PS, ZA, YA, dftc_in, dfts_in, fw_in)
            if STOP == "fourier":
                return nc, k
            mla_phase(nc, k, sb, l, s, T, cst, PS, QT, KT, VA, YC)
            if STOP == "mla":
                return nc, k
            rwkv_phase(nc, k, sb, l, s, T, cst, PS, PSB, identb, pp, ppx, ZR, YB, w2_in, a2_in)
            if STOP == "rwkv":
                return nc, k

        out_phase(nc, k, sb, l, NS, T, cst, PS, PSB, identb, pp, HT, YA, YB, YC, X, out_d, wgate_in, wproj_in, wout_in, fg_in, last)

    k.barrier()
    return nc, k


def fourier_phase(nc, k, sb, l, s, T, cst, PS, ZA, YA, dftc_in, dfts_in, fw_in):
    NT = T // 128
    NB = T // 512
    with contextlib.ExitStack() as es:
        za = sb(es, "f_za", [128, NT, 256], BF)
        fw = sb(es, "f_fw", [64, 256], F32)
        wc = sb(es, "f_wc", [128, 2, 256], BF)
        ws = sb(es, "f_ws", [128, 2, 256], BF)
        mats = [sb(es, "f_m%d" % i, [128, NT, 512], BF) for i in range(2)]
        a1 = sb(es, "f_a1", [128, 2, 512], BF)
        a2 = sb(es, "f_a2", [128, 2, 512], BF)
        yo = sb(es, "f_yo", [128, 2, 512], BF)
        k.dma("sp", za[:], ZA[s][:, :].rearrange("(a p) c -> p a c", p=128), [ZA[s]], [za])
        k.dma("sp", fw[:], fw_in[l], [fw_in], [fw])
        k.memset("pool", wc[:], 0.0, [wc])
        k.memset("pool", ws[:], 0.0, [ws])
        nrm = float(1.0 / np.sqrt(T * 64.0))
        pc, psn = PS[0], PS[1]
        k.mm(pc[0:64, 0:256], cst[0:64, C_C64:C_C64 + 64], fw[:], True, True, [cst, fw], [pc])
        k.mm(psn[0:64, 0:256], cst[0:64, C_S64:C_S64 + 64], fw[:], True, True, [cst, fw], [psn])
        for g in range(4):
            rows = slice((g % 2) * 64, (g % 2) * 64 + 64)
            cols = slice(g * 64, (g + 1) * 64)
            k.ts("dve", wc[rows, g // 2, cols], pc[0:64, cols], nrm, None, ALU.mult, None, [pc], [wc])
            k.ts("dve", ws[rows, g // 2, cols], psn[0:64, cols], -nrm, None, ALU.mult, None, [psn], [ws])
        for tb in range(NB):
            TB = slice(tb * 512, (tb + 1) * 512)
            for mi, src in enumerate((dftc_in, dfts_in)):
                m = mats[mi]
                for half in range(2):
                    hs = slice(half * (NT // 2), (half + 1) * (NT // 2)) if NT >= 2 else slice(0, NT)
                    if NT < 2 and half == 1:
                        continue
                    k.dma("sp", m[:, hs, :], src[:, TB].rearrange("(a p) n -> p a n", p=128)[:, hs, :], [src], [m])
                dst = a1 if mi == 0 else a2
                for cb in range(2):
                    ps = PS[2 + cb]
                    for c in range(NT):
                        k.mm(ps[:], za[:, c, cb * 128:(cb + 1) * 128], m[:, c, :], c == 0, c == NT - 1, [za, m], [ps])
                    k.cp("act" if cb else "dve", dst[:, cb, :], ps[:], [ps], [dst])
            for eb in range(2):
                ps = PS[4 + eb]
                i = 0
                for (w_, a_) in ((wc, a1), (ws, a2)):
                    for cb in range(2):
                        k.mm(ps[:], w_[:, cb, eb * 128:(eb + 1) * 128], a_[:, cb, :], i == 0, i == 3, [w_, a_], [ps])
                        i += 1
                k.cp("act", yo[:, eb, :], ps[:], [ps], [yo])
            k.dma("sp", YA[s][:, TB].rearrange("(e p) t -> p e t", p=128), yo[:], [yo], [YA[s]])
        k.barrier()


def mla_phase(nc, k, sb, l, s, T, cst, PS, QT, KT, VA, YC):
    NT = T // 128
    NB = T // 512
    with contextlib.ExitStack() as es:
        qt = [sb(es, "m_qt%d" % i, [128, T], BF) for i in range(2)]
        kt = [sb(es, "m_kt%d" % i, [128, T], BF) for i in range(2)]
        va = [sb(es, "m_va%d" % i, [128, NT, 65], BF) for i in range(2)]
        pt = [sb(es, "m_pt%d" % i, [128, 512], BF) for i in range(4)]
        osb = sb(es, "m_o", [128, 512], F32)
        rc = sb(es, "m_rc", [128, 512], F32)
        yo = [sb(es, "m_yo%d" % i, [64, 512], BF) for i in range(2)]
        pi = 0
        for h in range(6):
            q_, k_, v_ = qt[h % 2], kt[h % 2], va[h % 2]
            k.dma("sp", q_[0:97, :], QT[s][h], [QT[s]], [q_])
            k.dma("sp", k_[0:97, :], KT[s][h], [KT[s]], [k_])
            k.dma("sp", v_[:], VA[s][:, h * 65:(h + 1) * 65].rearrange("(a p) c -> p a c", p=128), [VA[s]], [v_])
            for qb in range(NB):
                QB = slice(qb * 512, (qb + 1) * 512)
                po = PS[4 + qb % 2]
                pq_ = {}
                for kc in range(NT + 2):
                    if kc < NT:
                        ps = PS[kc % 4]
                        k.mm(ps[:], k_[0:97, kc * 128:(kc + 1) * 128], q_[0:97, QB], True, True, [k_, q_], [ps])
                        p_ = pt[pi % 4]
                        pi += 1
                        k.act(p_[:], ps[:], AF.Exp, [ps], [p_])
                        pq_[kc] = p_
                    if kc >= 2:
                        j = kc - 2
                        k.mm(po[0:65, :], v_[:, j, :], pq_[j][:], j == 0, j == NT - 1, [v_, pq_[j]], [po])
                k.cp("dve", osb[0:65, :], po[0:65, :], [po], [osb])
                k.recip(rc[64:65, :], osb[64:65, :], [osb], [rc])
                pbc = PS[6]
                k.mm(pbc[0:64, :], cst[64:65, C_OA:C_OA + 64], rc[64:65, :], True, True, [cst, rc], [pbc])
                y_ = yo[(h * NB + qb) % 2]
                k.tt("dve", y_[:], osb[0:64, :], pbc[0:64, :], ALU.mult, [osb, pbc], [y_])
                k.dma("sp", YC[s][h * 64:(h + 1) * 64, QB], y_[:], [y_], [YC[s]])
        k.barrier()


def rwkv_phase(nc, k, sb, l, s, T, cst, PS, PSB, identb, pp, ppx, ZR, YB, w2_in, a2_in):
    NSL = T // 512
    ident = cst[:, C_ID:C_ID + 128]
    with contextlib.ExitStack() as es:
        w2 = sb(es, "r_w2", [128, 384], F32)
        a2w = sb(es, "r_a2", [128, 384], F32)
        k.dma("sp", w2[:], w2_in[l], [w2_in], [w2])
        k.dma("sp", a2w[:], a2_in[l], [a2_in], [a2w])
        zr = sb(es, "r_zr", [128, 11, 514], F32)
        zs = sb(es, "r_zs", [128, 11, 512], F32)
        th = sb(es, "r_th", [128, 512], F32)
        lw = sb(es, "r_lw", [128, 3, 512], F32)
        ic = sb(es, "r_ic", [128, 3, 512], F32)
        kk = sb(es, "r_kk", [128, 3, 512], F32)
        tA = sb(es, "r_tA", [128, 3, 512], F32)
        tB = sb(es, "r_tB", [128, 3, 512], F32)
        cum = sb(es, "r_cum", [128, 3, 512], F32)
        epos = sb(es, "r_ep", [128, 3, 512], F32)
        eneg = sb(es, "r_en", [128, 3, 512], F32)
        AR = sb(es, "r_AR", [128, 3, 2, 512], RDT)
        BK = sb(es, "r_BK", [128, 3, 2, 512], RDT)
        VV = sb(es, "r_VV", [128, 3, 512], RDT)
        Y = sb(es, "r_Y", [128, 3, 512], F32)
        Y0 = sb(es, "r_Y0", [128, 3, 512], F32)
        ST = sb(es, "r_ST", [128, 3, 64], RDT)
        TOKB = [[sb(es, "r_TOK%d_%d" % (i, b), [128, 384], RDT) for b in range(3)] for i in range(3)]
        NU = 6
        NAr = [sb(es, "r_NAr%d" % i, [128, 256], RDT) for i in range(NU)]
        KA = [sb(es, "r_KA%d" % i, [128, 256], RDT) for i in range(NU)]
        NN = [[sb(es, "r_NN%d_%d" % (i, j), [128, 256], RDT) for j in range(2)] for i in range(NU)]
        TM = [[sb(es, "r_TM%d_%d" % (i, j), [128, 128], RDT) for j in range(2)] for i in range(NU)]
        X0 = [sb(es, "r_X0%d" % i, [128, 64], RDT) for i in range(NU)]
        UT = [sb(es, "r_UT%d" % i, [128, 64], RDT) for i in range(NU)]
        ST32 = sb(es, "r_ST32", [128, 3, 64], F32)
        if RDT == BF:
            ptile, idn, idt = PSB, identb[:], identb
        else:
            ptile, idn, idt = PS[6], ident, cst
        for d in range(2):
            k.memset("pool", ST[:], 0.0, [ST])
            k.memset("pool", ST32[:], 0.0, [ST32])
            slabs = range(NSL) if d == 0 else range(NSL - 1, -1, -1)
            m2 = cst[:, C_SF:C_SF + 256] if d == 0 else cst[:, C_SB:C_SB + 256]
            mT = cst[:, C_SB:C_SB + 128] if d == 0 else cst[:, C_SF:C_SF + 128]
            for sl in slabs:
                S0 = sl * 512
                lo = 1 if sl == 0 else 0
                hi = 1 if sl == NSL - 1 else 0
                if lo:
                    k.memset("pool", zr[:, :, 0:1], 0.0, [zr])
                if hi:
                    k.memset("pool", zr[:, :, 513:514], 0.0, [zr])
                k.dma("sp", zr[:, :, lo:514 - hi], ZR[s][:, S0 - 1 + lo:S0 + 513 - hi].rearrange("(b p) t -> p b t", p=128), [ZR[s]], [zr])
                for b in range(11):
                    e = "dve" if b % 2 == 0 else "pool"
                    k.ts(e, zs[:, b, :], zr[:, b, 1:513], ppx[:, b:b + 1], None, ALU.mult, None, [zr, ppx], [zs])
                    k.stt(zs[:, b, :], zr[:, b, 0:512], pp[:, 8 + b:9 + b], zs[:, b, :], ALU.mult, ALU.add, [zr, pp, zs], [zs])
                    k.stt(zs[:, b, :], zr[:, b, 2:514], pp[:, 19 + b:20 + b], zs[:, b, :], ALU.mult, ALU.add, [zr, pp, zs], [zs])
                DR = slice(d * 64, d * 64 + 64)
                k.act(th[DR, :], zs[DR, 9, :], AF.Tanh, [zs], [th])
                for b in range(3):
                    ps = PS[b % 2]
                    k.mm(ps[:], w2[DR, b * 128:(b + 1) * 128], th[DR, :], True, True, [w2, th], [ps])
                    k.act(lw[:, b, :], ps[:], AF.Sigmoid, [ps, pp], [lw], bias=pp[:, 30 + d * 3 + b:31 + d * 3 + b])
                    k.ts("dve", lw[:, b, :], lw[:, b, :], -float(np.exp(-0.5)), None, ALU.mult, None, [lw], [lw])
                for b in range(3):
                    ps = PS[2 + b % 2]
                    k.mm(ps[:], a2w[DR, b * 128:(b + 1) * 128], zs[DR, 10, :], True, True, [a2w, zs], [ps])
                    k.act(ic[:, b, :], ps[:], AF.Sigmoid, [ps, pp], [ic], bias=pp[:, 36 + d * 3 + b:37 + d * 3 + b])
                for b in range(3):
                    k.ts("dve", kk[:, b, :], zs[:, 3 + b, :], pp[:, 42 + b:43 + b], None, ALU.mult, None, [zs, pp], [kk])
                    k.tt("pool", tA[:, b, :], kk[:, b, :], kk[:, b, :], ALU.mult, [kk], [tA])
                    ps = PS[4 + b % 2]
                    k.mm(ps[:], cst[:, C_OB:C_OB + 128], tA[:, b, :], True, True, [cst, tA], [ps])
                    k.act(tB[:, b, :], ps[:], AF.Sqrt, [ps], [tB])
                    k.ts("dve", tB[:, b, :], tB[:, b, :], 1e-12, None, ALU.max, None, [tB], [tB])
                    k.recip(tB[:, b, :], tB[:, b, :], [tB], [tB])
                    k.tt("dve", kk[:, b, :], kk[:, b, :], tB[:, b, :], ALU.mult, [kk, tB], [kk])
                for b in range(3):
                    k.op("dve", lambda b=b: nc.vector.tensor_tensor_scan(
                        out=cum[:, b, :], data0=cst[:, C_SEG:C_SEG + 512], data1=lw[:, b, :], initial=0.0,
                        op0=ALU.mult, op1=ALU.add), [cst, lw], [cum])
                    if d == 1:
                        for c in range(4):
                            cs_ = slice(c * 128, (c + 1) * 128)
                            k.stt(tA[:, b, cs_], cum[:, b, cs_], cum[:, b, c * 128 + 127:c * 128 + 128], lw[:, b, cs_],
                                  ALU.subtract, ALU.subtract, [cum, lw], [tA])
                        k.ts("dve", cum[:, b, :], tA[:, b, :], -1.0, None, ALU.mult, None, [tA], [cum])
                    k.act(epos[:, b, :], cum[:, b, :], AF.Exp, [cum], [epos])
                    k.act(eneg[:, b, :], cum[:, b, :], AF.Exp, [cum], [eneg], scale=-1.0)
                    k.tt("pool", tA[:, b, :], cum[:, b, :], lw[:, b, :], ALU.subtract, [cum, lw], [tA])
                    k.act(tA[:, b, :], tA[:, b, :], AF.Exp, [tA], [tA])
                    k.stt(AR[:, b, 0, :], kk[:, b, :], -1.0, tA[:, b, :], ALU.mult, ALU.mult, [kk, tA], [AR])
                    k.tt("pool", AR[:, b, 1, :], zs[:, b, :], epos[:, b, :], ALU.mult, [zs, epos], [AR])
                    k.tt("dve", tB[:, b, :], kk[:, b, :], ic[:, b, :], ALU.mult, [kk, ic], [tB])
                    k.tt("dve", BK[:, b, 0, :], tB[:, b, :], eneg[:, b, :], ALU.mult, [tB, eneg], [BK])
                    k.ts("dve", tB[:, b, :], ic[:, b, :], pp[:, 45 + b:46 + b], ppx[:, 11 + b:12 + b], ALU.mult, ALU.add, [ic, pp, ppx], [tB])
                    k.tt("pool", tB[:, b, :], tB[:, b, :], zs[:, 3 + b, :], ALU.mult, [tB, zs], [tB])
                    k.tt("dve", BK[:, b, 1, :], tB[:, b, :], eneg[:, b, :], ALU.mult, [tB, eneg], [BK])
                    k.cp("pool", VV[:, b, :], zs[:, 6 + b, :], [zs], [VV])
                if d == 1:
                    k.dma("sp", Y0[:], YB[s][:, S0:S0 + 512].rearrange("(b p) t -> p b t", p=128), [YB[s]], [Y0])
                chunks = range(4) if d == 0 else range(3, -1, -1)
                for c in chunks:
                    cs_ = slice(c * 128, (c + 1) * 128)
                    gcol = c * 128 + 127 if d == 0 else c * 128
                    tkl = TOKB[c % 3]
                    for b in range(3):
                        k.tr(ptile[:, 0:128], BK[:, b, 0, cs_], idn, [BK, idt], [ptile])
                        k.tr(ptile[:, 128:256], BK[:, b, 1, cs_], idn, [BK, idt], [ptile])
                        k.tr(ptile[:, 256:384], VV[:, b, cs_], idn, [VV, idt], [ptile])
                        k.cp("act", tkl[b][:], ptile[:, 0:384], [ptile], [tkl[b]])
                    HS = range(6)
                    HR = [slice((h % 2) * 64, (h % 2) * 64 + 64) for h in HS]
                    HB = [h // 2 for h in HS]
                    for h in HS:
                        b, bank = HB[h], PS[h]
                        k.mm(bank[:, 0:256], BK[HR[h], b, 0, cs_], AR[HR[h], b, :, cs_], True, True, [BK, AR], [bank])
                        k.mm(bank[:, 256:512], BK[HR[h], b, 1, cs_], AR[HR[h], b, :, cs_], True, True, [BK, AR], [bank])
                    for h in HS:
                        bank = PS[h]
                        k.tt("dve", NAr[h][:], bank[:, 0:256], m2, ALU.mult, [bank, cst], [NAr[h]])
                        k.tt("dve", KA[h][:], bank[:, 256:512], m2, ALU.mult, [bank, cst], [KA[h]])
                    for h in HS:
                        b, bank = HB[h], PS[h]
                        k.mm(bank[:, 0:128], AR[HR[h], b, 0, cs_], BK[HR[h], b, 0, cs_], True, True, [BK, AR], [bank])
                    cur = {}
                    tmc = {}
                    for h in HS:
                        bank = PS[h]
                        nn0 = NN[h][0]
                        k.tt("dve", nn0[:, 128:256], bank[:, 0:128], mT, ALU.mult, [bank, cst], [nn0])
                        k.cp("pool", nn0[:, 0:128], NAr[h][:, 0:128], [NAr[h]], [nn0])
                        k.tt("pool", TM[h][0][:], NAr[h][:, 0:128], ident, ALU.add, [NAr[h], cst], [TM[h][0]])
                        cur[h] = nn0
                        tmc[h] = TM[h][0]
                    for lev in range(6):
                        for h in HS:
                            bank = PS[h]
                            k.mm(bank[:, 0:128], cur[h][:, 128:256], cur[h][:, 0:128], True, True, [cur[h]], [bank])
                            k.mm(bank[:, 128:256], cur[h][:, 0:128], cur[h][:, 128:256], True, True, [cur[h]], [bank])
                        for h in HS:
                            nxt = NN[h][(lev + 1) % 2]
                            k.cp("act", nxt[:], PS[h][:, 0:256], [PS[h]], [nxt])
                            cur[h] = nxt
                        for h in HS:
                            k.mm(PS[h][:, 256:384], cur[h][:, 128:256], tmc[h][:], True, True, [cur[h], tmc[h]], [PS[h]])
                        for h in HS:
                            tm2 = TM[h][(lev + 1) % 2]
                            k.tt("dve", tm2[:], PS[h][:, 256:384], tmc[h][:], ALU.add, [PS[h], tmc[h]], [tm2])
                            tmc[h] = tm2
                    tv = [tkl[HB[h]][:, 256 + (h % 2) * 64:256 + (h % 2) * 64 + 64] for h in HS]
                    tb_ = [tkl[HB[h]][:, (h % 2) * 64:(h % 2) * 64 + 64] for h in HS]
                    tk_ = [tkl[HB[h]][:, 128 + (h % 2) * 64:128 + (h % 2) * 64 + 64] for h in HS]
                    for h in HS:
                        b, bank = HB[h], PS[h]
                        k.mm(bank[:, 0:64], AR[HR[h], b, 0, cs_], ST[HR[h], b, :], True, False, [AR, ST], [bank])
                        k.mm(bank[:, 0:64], KA[h][:, 0:128], tv[h], False, True, [KA[h], tkl[b]], [bank])
                    for h in HS:
                        k.cp("act", X0[h][:], PS[h][:, 0:64], [PS[h]], [X0[h]])
                    for h in HS:
                        k.mm(PS[h][:, 64:128], tmc[h][:], X0[h][:], True, True, [tmc[h], X0[h]], [PS[h]])
                    for h in HS:
                        k.cp("act", UT[h][:], PS[h][:, 64:128], [PS[h]], [UT[h]])
                    for h in HS:
                        b, bank = HB[h], PS[h]
                        k.mm(bank[0:64, 128:256], ST[HR[h], b, :], AR[HR[h], b, 1, cs_], True, False, [ST, AR], [bank])
                        k.mm(bank[0:64, 128:256], UT[h][:], NAr[h][:, 128:256], False, False, [UT[h], NAr[h]], [bank])
                        k.mm(bank[0:64, 128:256], tv[h], KA[h][:, 128:256], False, True, [tkl[b], KA[h]], [bank])
                        k.mm(bank[0:64, 256:320], tb_[h], UT[h][:], True, False, [tkl[b], UT[h]], [bank])
                        k.mm(bank[0:64, 256:320], tk_[h], tv[h], False, True, [tkl[b]], [bank])
                    for h in HS:
                        b, bank = HB[h], PS[h]
                        if d == 0:
                            k.cp("act", Y[HR[h], b, cs_], bank[0:64, 128:256], [bank], [Y])
                        else:
                            k.tt("dve", Y[HR[h], b, cs_], bank[0:64, 128:256], Y0[HR[h], b, cs_], ALU.add, [bank, Y0], [Y])
                        k.tt("dve", ST32[HR[h], b, :], bank[0:64, 256:320], ST32[HR[h], b, :], ALU.add, [bank, ST32], [ST32])
                    for b in range(3):
                        k.ts("pool", ST32[:, b, :], ST32[:, b, :], epos[:, b, gcol:gcol + 1], None, ALU.mult, None, [ST32, epos], [ST32])
                    k.cp("act", ST[:], ST32[:], [ST32], [ST])
                if d == 1:
                    OB = cst[:, C_OB:C_OB + 128]
                    for b in range(3):
                        ps = PS[b % 2]
                        k.mm(ps[:], a2w[0:64, b * 128:(b + 1) * 128], zs[0:64, 10, :], True, True, [a2w, zs], [ps])
                        k.act(tA[:, b, :], ps[:], AF.Sigmoid, [ps, pp], [tA], bias=pp[:, 36 + b:37 + b])
                        k.tt("dve", tA[:, b, :], tA[:, b, :], ic[:, b, :], ALU.add, [tA, ic], [tA])
                        k.ts("dve", tA[:, b, :], tA[:, b, :], pp[:, 45 + b:46 + b], ppx[:, 14 + b:15 + b], ALU.mult, ALU.add, [tA, pp, ppx], [tA])
                        k.tt("dve", tA[:, b, :], tA[:, b, :], zs[:, 3 + b, :], ALU.mult, [tA, zs], [tA])
                        k.stt(tA[:, b, :], zs[:, b, :], pp[:, 48 + b:49 + b], tA[:, b, :], ALU.mult, ALU.mult, [zs, pp, tA], [tA])
                        ps = PS[2 + b % 2]
                        k.mm(ps[:], OB, tA[:, b, :], True, True, [cst, tA], [ps])
                        k.tt("dve", tB[:, b, :], ps[:], zs[:, 6 + b, :], ALU.mult, [ps, zs], [tB])
                        ps2 = PS[4 + b % 2]
                        k.mm(ps2[:], OB, Y[:, b, :], True, True, [cst, Y], [ps2])
                        k.stt(cum[:, b, :], ps2[:], -1.0 / 64, Y[:, b, :], ALU.mult, ALU.add, [ps2, Y], [cum])
                        k.tt("pool", tA[:, b, :], cum[:, b, :], cum[:, b, :], ALU.mult, [cum], [tA])
                        ps3 = PS[b % 2]
                        k.mm(ps3[:], OB, tA[:, b, :], True, True, [cst, tA], [ps3])
                        k.ts("dve", epos[:, b, :], ps3[:], 1.0 / 64, 64e-5, ALU.mult, ALU.add, [ps3], [epos])
                        k.act(epos[:, b, :], epos[:, b, :], AF.Sqrt, [epos], [epos])
                        k.recip(epos[:, b, :], epos[:, b, :], [epos], [epos])
                        k.tt("dve", cum[:, b, :], cum[:, b, :], epos[:, b, :], ALU.mult, [cum, epos], [cum])
                        k.ts("dve", cum[:, b, :], cum[:, b, :], pp[:, 51 + b:52 + b], pp[:, 54 + b:55 + b], ALU.mult, ALU.add, [cum, pp], [cum])
                        k.tt("dve", Y[:, b, :], cum[:, b, :], tB[:, b, :], ALU.add, [cum, tB], [Y])
                k.dma("sp", YB[s][:, S0:S0 + 512].rearrange("(b p) t -> p b t", p=128), Y[:], [Y], [YB[s]])
        k.barrier()


def out_phase(nc, k, sb, l, NS, T, cst, PS, PSB, identb, pp, HT, YA, YB, YC, X, out_d, wgate_in, wproj_in, wout_in, fg_in, last):
    NSL = T // 512
    with contextlib.ExitStack() as es:
        wg = sb(es, "o_wg", [128, 8, 4096], BF)
        wp = sb(es, "o_wp", [128, 8, D], BF)
        wo = sb(es, "o_wo", [128, 8, D], BF)
        stg = [sb(es, "o_stg%d" % i, [128, 2048], F32) for i in range(2)]
        si = 0
        for c in range(8):
            for hf in range(2):
                st = stg[si % 2]
                si += 1
                k.dma("sp", st[:], wgate_in[l, c * 128:(c + 1) * 128, hf * 2048:(hf + 1) * 2048], [wgate_in], [st])
                k.ts("pool" if si % 2 else "dve", wg[:, c, hf * 2048:(hf + 1) * 2048], st[:], pp[:, c:c + 1], None, ALU.mult, None, [st, pp], [wg])
        for c in range(8):
            st = stg[si % 2]
            si += 1
            k.dma("sp", st[:, 0:D], wproj_in[l, c * 128:(c + 1) * 128, :], [wproj_in], [st])
            k.dma("sp", st[:, D:2 * D], wout_in[l, c * 128:(c + 1) * 128, :], [wout_in], [st])
            k.cp("dve", wp[:, c, :], st[:, 0:D], [st], [wp])
            k.cp("pool", wo[:, c, :], st[:, D:2 * D], [st], [wo])
        hT = sb(es, "o_hT", [128, 8, 512], BF)
        ya = sb(es, "o_ya", [128, 2, 512], BF)
        yb = sb(es, "o_yb", [128, 3, 512], F32)
        yc = sb(es, "o_yc", [128, 3, 512], BF)
        gs = [sb(es, "o_gs%d" % i, [128, 512], F32) for i in range(3)]
        yg = sb(es, "o_yg", [128, 8, 512], BF)
        mg = sb(es, "o_mg", [128, 8, 512], BF)
        tm_ = sb(es, "o_tm", [128, 512], F32)
        tm2 = sb(es, "o_tm2", [128, 512], F32)
        xb = [sb(es, "o_xb%d" % i, [128, D], F32) for i in range(2)]
        xn = [sb(es, "o_xn%d" % i, [128, D], F32) for i in range(2)]
        junk = sb(es, "o_junk", [128, D], F32)
        st4 = sb(es, "o_st4", [128, 8], F32)
        fgb = sb(es, "o_fgb", [128, D], F32)
        if last:
            for p in range(128):
                k.dma("sp", fgb[p:p + 1, :], fg_in[:, :], [fg_in], [fgb])
        gi = 0
        xi = 0
        for s in range(NS):
            for sl in range(NSL):
                SL = slice(sl * 512, (sl + 1) * 512)
                k.dma("sp", hT[:], HT[s][:, :, SL], [HT[s]], [hT])
                k.dma("sp", ya[:], YA[s][:, SL].rearrange("(b p) t -> p b t", p=128), [YA[s]], [ya])
                k.dma("sp", yb[:], YB[s][:, SL].rearrange("(b p) t -> p b t", p=128), [YB[s]], [yb])
                k.dma("sp", yc[:], YC[s][:, SL].rearrange("(b p) t -> p b t", p=128), [YC[s]], [yc])
                for gb in range(8):
                    ps = PS[gb % 2]
                    for c in range(8):
                        k.mm(ps[:], wg[:, c, gb * 128:(gb + 1) * 128], hT[:, c, :], c == 0, c == 7, [wg, hT], [ps])
                    g_ = gs[gi % 3]
                    gi += 1
                    k.act(g_[:], ps[:], AF.Sigmoid, [ps], [g_])
                    k.tt("dve", g_[:], g_[:], ps[:], ALU.mult, [g_, ps], [g_])
                    if gb < 2:
                        src, srct = ya[:, gb, :], ya
                    elif gb < 5:
                        src, srct = yb[:, gb - 2, :], yb
                    else:
                        src, srct = yc[:, gb - 5, :], yc
                    k.tt("dve", yg[:, gb, :], g_[:], src, ALU.mult, [g_, srct], [yg])
                for ob in range(8):
                    OBS = slice(ob * 128, (ob + 1) * 128)
                    pa, pb, pc = PS[2], PS[3], PS[4]
                    for i, cb in enumerate((0, 1)):
                        k.mm(pa[:], wp[:, cb, OBS], yg[:, cb, :], i == 0, i == 1, [wp, yg], [pa])
                    for i, cb in enumerate((2, 3, 4)):
                        k.mm(pb[:], wp[:, cb, OBS], yg[:, cb, :], i == 0, i == 2, [wp, yg], [pb])
                    for i, cb in enumerate((5, 6, 7)):
                        k.mm(pc[:], wp[:, cb, OBS], yg[:, cb, :], i == 0, i == 2, [wp, yg], [pc])
                    sg = []
                    for j in range(3):
                        ps = PS[j % 2]
                        col = 1024 + j * 1024 + ob * 128
                        for c in range(8):
                            k.mm(ps[:], wg[:, c, col:col + 128], hT[:, c, :], c == 0, c == 7, [wg, hT], [ps])
                        g_ = gs[gi % 3]
                        gi += 1
                        k.act(g_[:], ps[:], AF.Sigmoid, [ps], [g_])
                        sg.append(g_)
                    k.tt("dve", tm_[:], pa[:], sg[0][:], ALU.mult, [pa, sg[0]], [tm_])
                    k.tt("dve", tm2[:], pb[:], sg[1][:], ALU.mult, [pb, sg[1]], [tm2])
                    k.tt("pool", tm_[:], tm_[:], tm2[:], ALU.add, [tm_, tm2], [tm_])
                    k.tt("dve", tm2[:], pc[:], sg[2][:], ALU.mult, [pc, sg[2]], [tm2])
                    k.tt("pool", mg[:, ob, :], tm_[:], tm2[:], ALU.add, [tm_, tm2], [mg])
                for tt in range(4):
                    tok0 = sl * 512 + tt * 128
                    xt = xb[xi % 2]
                    xo = xn[xi % 2]
                    xi += 1
                    k.dma("sp", xt[:], X[s][tok0:tok0 + 128, :], [X[s]], [xt])
                    for hf in range(2):
                        ps = PS[5 + hf]
                        for ob in range(8):
                            k.mm(ps[:], mg[:, ob, tt * 128:(tt + 1) * 128], wo[:, ob, hf * 512:(hf + 1) * 512], ob == 0, ob == 7, [mg, wo], [ps])
                        k.tt("dve", xo[:, hf * 512:(hf + 1) * 512], ps[:], xt[:, hf * 512:(hf + 1) * 512], ALU.add, [ps, xt], [xo])
                    if not last:
                        k.dma("sp", X[s][tok0:tok0 + 128, :], xo[:], [xo], [X[s]])
                    else:
                        k.act(junk[:], xo[:], AF.Square, [xo], [junk, st4], accum_out=st4[:, 0:1])
                        k.ts("dve", st4[:, 1:2], st4[:, 0:1], 1.0 / D, 1e-6, ALU.mult, ALU.add, [st4], [st4])
                        k.act(st4[:, 2:3], st4[:, 1:2], AF.Sqrt, [st4], [st4])
                        k.recip(st4[:, 3:4], st4[:, 2:3], [st4], [st4])
                        k.stt(xo[:], xo[:], st4[:, 3:4], fgb[:], ALU.mult, ALU.mult, [xo, st4, fgb], [xo])
                        k.dma("sp", out_d[s, tok0:tok0 + 128, :], xo[:], [xo], [out_d])
        k.barrier()


def _consts():
    c = np.zeros((128, NCST), np.float32)
    c[:, C_ID:C_ID + 128] = np.eye(128)
    j = np.arange(128)[:, None]
    t = np.arange(128)[None, :]
    c[:, C_SF:C_SF + 128] = j < t
    c[:, C_IF:C_IF + 128] = j <= t
    c[:, C_SB:C_SB + 128] = j > t
    c[:, C_IB:C_IB + 128] = j >= t
    c[:, C_OB:C_OB + 128] = (j // 64) == (t // 64)
    c[:, C_OA:C_OA + 128] = 1.0
    seg = np.ones(512, np.float32)
    seg[::128] = 0.0
    c[:, C_SEG:C_SEG + 512] = seg[None, :]
    a = np.arange(64)
    ang = 2 * np.pi * np.outer(a, a) / 64.0
    c[0:64, C_C64:C_C64 + 64] = np.cos(ang)
    c[0:64, C_S64:C_S64 + 64] = np.sin(ang)
    inv = (10000.0 ** (-np.arange(0, 32, 2, dtype=np.float32) / np.float32(32))).astype(np.float32)
    for p in range(64, 96):
        c[p, C_INVF] = inv[(p - 64) % 16]
        c[p, C_SGN] = -1.0 if p < 80 else 1.0
    c[0:96, C_E96 + 96] = 1.0
    return c


def _dft(T):
    n = np.arange(T, dtype=np.int64)
    m = (np.outer(n, n) % T).astype(np.float64) * (2 * np.pi / T)
    return np.cos(m).astype(ml_dtypes.bfloat16), np.sin(m).astype(ml_dtypes.bfloat16)


def host_inputs(inp, T, NL, NS, ncores):
    f = lambda a: np.ascontiguousarray(np.asarray(a), dtype=np.float32)
    w_in = f(inp["w_in"])
    wmix = np.concatenate([
        w_in[:, :, O_ZR:O_ZR + 1408], w_in[:, :, O_ZQ:O_ZQ + 256], w_in[:, :, O_ZKV:O_ZKV + 128],
        w_in[:, :, O_ZKV:O_ZKV + 64], w_in[:, :, O_ZKR:O_ZKR + 32],
        w_in[:, :, O_ZKV:O_ZKV + 64], w_in[:, :, O_ZKR + 16:O_ZKR + 32], w_in[:, :, O_ZKR:O_ZKR + 16],
        w_in[:, :, O_ZA:O_ZA + 256]], axis=2)
    assert wmix.shape[2] == CM
    wgate = np.concatenate([w_in[:, :, O_ZAG:O_ZAG + 256], w_in[:, :, O_ZBG:O_ZBG + 384],
                            w_in[:, :, O_ZCG:O_ZCG + 384], w_in[:, :, O_ZM:O_ZM + 3072]], axis=2)
    wproj = np.concatenate([f(inp["proj_a"]), f(inp["proj_b"]), f(inp["proj_c"])], axis=1)
    pp = np.zeros((NL, 128, NPP), np.float32)

    def blk(v, n):
        return np.transpose(v.reshape(NL, n, 128), (0, 2, 1))

    pp[:, :, 0:8] = blk(f(inp["norm_g"]), 8)
    pp[:, :, 8:19] = blk(f(inp["shift_mu_prev"]), 11)
    pp[:, :, 19:30] = blk(f(inp["shift_mu_next"]), 11)
    pp[:, :, 30:36] = blk(f(inp["decay_w0"]).reshape(NL, 768), 6)
    pp[:, :, 36:42] = blk(f(inp["iclr_a0"]).reshape(NL, 768), 6)
    pp[:, :, 42:45] = blk(f(inp["key_k"]), 3)
    pp[:, :, 45:48] = blk(f(inp["key_a"]), 3)
    pp[:, :, 48:51] = blk(f(inp["bonus_r_k"]).reshape(NL, 384), 3)
    pp[:, :, 51:54] = blk(f(inp["lnx_g"]), 3)
    pp[:, :, 54:57] = blk(f(inp["lnx_b"]), 3)
    pp[:, :, 57:59] = blk(f(inp["q_norm_g"]), 2)
    pp[:, :, 59:60] = blk(f(inp["kv_norm_g"]), 1)
    fw = np.transpose(f(inp["fourier_w"]), (0, 2, 1, 3)).reshape(NL, 64, 256)
    w2 = f(inp["decay_w2"]).reshape(NL, 128, 384)
    a2 = f(inp["iclr_a2"]).reshape(NL, 128, 384)
    wuq = f(inp["w_uq"])
    wq4 = wuq.reshape(NL, 256, 6, 96)
    wuqs = np.concatenate([wq4[..., 0:64], wq4[..., 80:96], wq4[..., 64:80]], axis=-1).reshape(NL, 256, 576)
    wkv4 = f(inp["w_ukv"]).reshape(NL, 128, 6, 128)
    wukvk = np.ascontiguousarray(wkv4[..., 0:64]).reshape(NL, 128, 384)
    wukvv = np.ascontiguousarray(wkv4[..., 64:128]).reshape(NL, 128, 384)
    dc, ds = _dft(T)
    shared = dict(cst=_consts(), dftc=dc, dfts=ds, wmix=np.ascontiguousarray(wmix), wgate=np.ascontiguousarray(wgate),
                  wproj=np.ascontiguousarray(wproj), wout=f(inp["w_out"]), pp=pp, fw=np.ascontiguousarray(fw), w2=w2, a2=a2,
                  wuq=np.ascontiguousarray(wuq), wuqs=np.ascontiguousarray(wuqs), wukvk=wukvk, wukvv=wukvv,
                  fg=f(inp["final_g"]).reshape(1, D))
    x = f(inp["x"])
    pos = np.ascontiguousarray(np.asarray(inp["positions"]), dtype=np.int32)
    maps = []
    for c in range(ncores):
        m = dict(shared)
        m["x"] = np.ascontiguousarray(x[c * NS:(c + 1) * NS])
        m["pos"] = np.ascontiguousarray(pos[c * NS:(c + 1) * NS]).reshape(NS, 1, T)
        maps.append(m)
    return maps


def kernel(**inputs):
    x = np.asarray(inputs["x"])
    B, T, _ = x.shape
    NL = np.asarray(inputs["w_in"]).shape[0]
    NS = B // NCORES
    nc, _ = build(T, NL, NS)
    maps = host_inputs(inputs, T, NL, NS, NCORES)
    res = run_bass_kernel_spmd(nc, maps, core_ids=list(range(NCORES)))
    return np.concatenate([np.asarray(r["out"], dtype=np.float32) for r in res.results], axis=0)
```

```python
import contextlib
import numpy as np
import ml_dtypes
import concourse.bass as bass
import concourse.mybir as mybir
from concourse.bass_utils import run_bass_kernel_spmd

F32 = mybir.dt.float32
BF = mybir.dt.bfloat16
I32 = mybir.dt.int32
AF = mybir.ActivationFunctionType
ALU = mybir.AluOpType
AX = mybir.AxisListType

D = 1024
NCORES = 8
CH = 128
RDT = BF
STOP = None


class StopBuild(Exception):
    pass


def chk(tag):
    if STOP == tag:
        raise StopBuild()


class Dep:
    __slots__ = ("w", "r")

    def __init__(self):
        self.w = {}
        self.r = {}


class Tl:
    def __init__(self, t, d=None, ps=False):
        self.t = t
        self.d = d or Dep()
        self.ps = ps

    def __getitem__(self, idx):
        return self.t[idx]


class K:
    NDMA = 24

    def __init__(self, nc):
        self.nc = nc
        self.es = contextlib.ExitStack()
        self.eng = dict(pe=nc.tensor, act=nc.scalar, dve=nc.vector, pool=nc.gpsimd, sp=nc.sync)
        self.sem = {}
        self.cnt = {}
        for e in ["pe", "act", "dve", "pool"]:
            self.sem[e] = self.es.enter_context(nc.semaphore("s_" + e))
            self.cnt[e] = 0
        for i in range(self.NDMA):
            key = ("d", i)
            self.sem[key] = self.es.enter_context(nc.semaphore("d%d" % i))
            self.cnt[key] = 0
        self.rr = 0
        self.waited = {}
        self.ninstr = 0

    def _wait(self, e, reads, writes):
        need = {}
        for t in reads:
            for key, v in t.d.w.items():
                need[key] = max(need.get(key, 0), v)
            if t.ps:
                for key, v in t.d.r.items():
                    need[key] = max(need.get(key, 0), v)
        for t in writes:
            for key, v in t.d.w.items():
                need[key] = max(need.get(key, 0), v)
            for key, v in t.d.r.items():
                need[key] = max(need.get(key, 0), v)
        for key, v in need.items():
            if key == e and e == "pe":
                continue
            if self.waited.get((e, key), 0) < v:
                self.eng[e].wait_ge(self.sem[key], v)
                self.waited[(e, key)] = v
                self.ninstr += 1

    def _done(self, key, v, reads, writes):
        for t in writes:
            t.d.w = {key: v}
            t.d.r = {}
        for t in reads:
            if t.d.r.get(key, 0) < v:
                t.d.r[key] = v

    def op(self, e, fn, r, w):
        self._wait(e, r, w)
        ins = fn()
        self.cnt[e] += 1
        ins.then_inc(self.sem[e], 1)
        self._done(e, self.cnt[e], r, w)
        self.ninstr += 1
        return ins

    def dma(self, q, out, in_, r, w):
        self._wait(q, r, w)
        key = ("d", self.rr)
        self.rr = (self.rr + 1) % self.NDMA
        if self.waited.get((q, key), 0) < self.cnt[key]:
            self.eng[q].wait_ge(self.sem[key], self.cnt[key])
            self.waited[(q, key)] = self.cnt[key]
        ins = self.eng[q].dma_start(out=out, in_=in_)
        self.cnt[key] += 16
        ins.then_inc(self.sem[key], 16)
        self._done(key, self.cnt[key], r, w)
        self.ninstr += 1

    def barrier(self, engines=("pe", "act", "dve", "pool", "sp")):
        for e in engines:
            for key, v in self.cnt.items():
                if key == e or v == 0:
                    continue
                if self.waited.get((e, key), 0) < v:
                    self.eng[e].wait_ge(self.sem[key], v)
                    self.waited[(e, key)] = v

    def mm(self, out, lhsT, rhs, start, stop, r, w):
        return self.op("pe", lambda: self.nc.tensor.matmul(out, lhsT, rhs, start=start, stop=stop), r, w)

    def tr(self, out, in_, ident, r, w):
        return self.op("pe", lambda: self.nc.tensor.transpose(out, in_, ident), r, w)

    def act(self, out, in_, func, r, w, **kw):
        return self.op("act", lambda: self.nc.scalar.activation(out=out, in_=in_, func=func, **kw), r, w)

    def ts(self, e, out, in0, s1, s2, op0, op1, r, w):
        eng = self.eng[e]
        if op1 is None:
            return self.op(e, lambda: eng.tensor_scalar(out=out, in0=in0, scalar1=s1, scalar2=None, op0=op0), r, w)
        return self.op(e, lambda: eng.tensor_scalar(out=out, in0=in0, scalar1=s1, scalar2=s2, op0=op0, op1=op1), r, w)

    def tt(self, e, out, in0, in1, op, r, w):
        eng = self.eng[e]
        return self.op(e, lambda: eng.tensor_tensor(out=out, in0=in0, in1=in1, op=op), r, w)

    def stt(self, out, in0, scalar, in1, op0, op1, r, w):
        return self.op("dve", lambda: self.nc.vector.scalar_tensor_tensor(out=out, in0=in0, scalar=scalar, in1=in1, op0=op0, op1=op1), r, w)

    def cp(self, e, out, in_, r, w):
        if e == "act":
            return self.op(e, lambda: self.nc.scalar.copy(out=out, in_=in_), r, w)
        eng = self.eng[e]
        return self.op(e, lambda: eng.tensor_copy(out=out, in_=in_), r, w)

    def recip(self, out, in_, r, w):
        return self.op("dve", lambda: self.nc.vector.reciprocal(out=out, in_=in_), r, w)

    def memset(self, e, ap, val, w):
        eng = self.eng[e]
        return self.op(e, lambda: eng.memset(ap, val), [], w)


O_ZA, O_ZAG, O_ZR, O_ZBG, O_ZQ, O_ZKV, O_ZKR, O_ZCG, O_ZM = 0, 256, 512, 1920, 2304, 2560, 2688, 2720, 3104
M_ZR, M_ZQ, M_ZKV, M_KR1, M_KR2, M_ZA, CM = 0, 1408, 1664, 1792, 1888, 1984, 2240
NPP = 60
C_ID, C_SF, C_IF, C_SB, C_IB, C_OB, C_OA, C_SEG, C_C64, C_S64, C_INVF, C_SGN, C_E96, NCST = (
    0, 128, 256, 384, 512, 640, 768, 896, 1408, 1472, 1536, 1537, 1538, 1538 + 97)


def build(T, NL, NS, debug=False):
    nc = bass.Bass("TRN2", target_bir_lowering=False)
    k = K(nc)
    try:
        return _build(nc, k, T, NL, NS, debug)
    except StopBuild:
        k.barrier()
        return nc, k


def _build(nc, k, T, NL, NS, debug):
    NSL = T // 512
    NCK = T // CH
    NT = T // 128
    okind = "ExternalOutput" if debug else "Internal"

    def din(name, shape, dt):
        return Tl(nc.dram_tensor(name, shape, dt, kind="ExternalInput").ap())

    x_in = din("x", [NS, T, D], F32)
    pos_in = din("pos", [NS, 1, T], I32)
    cst_in = din("cst", [128, NCST], F32)
    dftc_in = din("dftc", [T, T], BF)
    dfts_in = din("dfts", [T, T], BF)
    wmix_in = din("wmix", [NL, D, CM], F32)
    wgate_in = din("wgate", [NL, D, 4096], F32)
    wproj_in = din("wproj", [NL, D, D], F32)
    wout_in = din("wout", [NL, D, D], F32)
    pp_in = din("pp", [NL, 128, NPP], F32)
    fw_in = din("fw", [NL, 64, 256], F32)
    w2_in = din("w2", [NL, 128, 384], F32)
    a2_in = din("a2", [NL, 128, 384], F32)
    wuq_in = din("wuq", [NL, 256, 576], F32)
    wuqs_in = din("wuqs", [NL, 256, 576], F32)
    wukvk_in = din("wukvk", [NL, 128, 384], F32)
    wukvv_in = din("wukvv", [NL, 128, 384], F32)
    fg_in = din("fg", [1, D], F32)
    out_d = Tl(nc.dram_tensor("out", [NS, T, D], F32, kind="ExternalOutput").ap())

    def dscr(name, shape, dt):
        return Tl(nc.dram_tensor(name, shape, dt, kind=okind).ap())

    X = [dscr("X%d" % s, [T, D], F32) for s in range(NS)]
    HT = [dscr("HT%d" % s, [128, 8, T], BF) for s in range(NS)]
    ZR = [dscr("ZR%d" % s, [1408, T], F32) for s in range(NS)]
    ZA = [dscr("ZA%d" % s, [T, 256], BF) for s in range(NS)]
    QT = [dscr("QT%d" % s, [6, 97, T], BF) for s in range(NS)]
    KT = [dscr("KT%d" % s, [6, 97, T], BF) for s in range(NS)]
    VA = [dscr("VA%d" % s, [T, 6 * 65], BF) for s in range(NS)]
    YA = [dscr("YA%d" % s, [256, T], BF) for s in range(NS)]
    YB = [dscr("YB%d" % s, [384, T], F32) for s in range(NS)]
    YC = [dscr("YC%d" % s, [384, T], BF) for s in range(NS)]

    es0 = k.es

    uid = [0]

    def sb(es, name, shape, dt):
        uid[0] += 1
        return Tl(es.enter_context(nc.sbuf_tensor("sb%d_%s" % (uid[0], name), shape, dt)))

    cst = sb(es0, "cst", [128, NCST], F32)
    identb = sb(es0, "identb", [128, 128], BF)
    onesb = sb(es0, "onesb", [128, 128], BF)
    pp = sb(es0, "pp", [128, NPP], F32)
    ppx = sb(es0, "ppx", [128, 40], F32)
    CSD = [dscr("CSD%d" % s, [2, 32, T], F32) for s in range(NS)]
    PS = [Tl(es0.enter_context(nc.psum_tensor("ps%d" % i, [128, 512], F32)), ps=True) for i in range(7)]
    PSB = Tl(es0.enter_context(nc.psum_tensor("psb", [128, 1024], BF)), ps=True)

    k.dma("sp", cst[:], cst_in[:, :], [cst_in], [cst])
    k.cp("dve", identb[:], cst[:, C_ID:C_ID + 128], [cst], [identb])
    k.cp("dve", onesb[:], cst[:, C_OA:C_OA + 128], [cst], [onesb])
    ident = cst[:, C_ID:C_ID + 128]

    with contextlib.ExitStack() as es:
        posi = sb(es, "posi", [128, T], I32)
        ang = sb(es, "ang", [128, T], F32)
        kq = sb(es, "kq", [128, T], F32)
        rr_ = sb(es, "rr", [128, T], F32)
        cs1t = sb(es, "cs1t", [128, T], F32)
        cs2t = sb(es, "cs2t", [128, T], F32)
        CS1 = [cs1t] * NS
        CS2 = [cs2t] * NS
        P = slice(64, 96)
        for s in range(NS):
            for p in range(64, 96):
                k.dma("sp", posi[p:p + 1, :], pos_in[s, :, :], [pos_in], [posi])
            k.cp("dve", ang[P, :], posi[P, :], [posi], [ang])
            k.ts("dve", ang[P, :], ang[P, :], cst[P, C_INVF:C_INVF + 1], None, ALU.mult, None, [ang, cst], [ang])
            k.ts("dve", kq[P, :], ang[P, :], float(1.0 / (2 * np.pi)), None, ALU.mult, None, [ang], [kq])
            k.ts("dve", kq[P, :], kq[P, :], 12582912.0, None, ALU.add, None, [kq], [kq])
            k.ts("dve", kq[P, :], kq[P, :], -12582912.0, None, ALU.add, None, [kq], [kq])
            c1 = 6.28125
            c2 = float(np.float32(2 * np.pi - c1))
            c3 = float(2 * np.pi - c1 - np.float64(np.float32(2 * np.pi - c1)))
            k.stt(rr_[P, :], kq[P, :], -c1, ang[P, :], ALU.mult, ALU.add, [kq, ang], [rr_])
            k.stt(rr_[P, :], kq[P, :], -c2, rr_[P, :], ALU.mult, ALU.add, [kq, rr_], [rr_])
            k.stt(rr_[P, :], kq[P, :], -c3, rr_[P, :], ALU.mult, ALU.add, [kq, rr_], [rr_])
            k.ts("dve", rr_[P, :], rr_[P, :], 3.1415925, -3.1415925, ALU.min, ALU.max, [rr_], [rr_])
            k.act(CS2[s][P, :], rr_[P, :], AF.Sin, [rr_], [CS2[s]])
            k.ts("dve", CS2[s][P, :], CS2[s][P, :], cst[P, C_SGN:C_SGN + 1], None, ALU.mult, None, [CS2[s], cst], [CS2[s]])
            k.ts("dve", kq[P, :], rr_[P, :], -1.0, None, ALU.mult, None, [rr_], [kq])
            k.tt("dve", kq[P, :], kq[P, :], rr_[P, :], ALU.max, [kq, rr_], [kq])
            k.ts("dve", kq[P, :], kq[P, :], -1.0, float(np.pi / 2), ALU.mult, ALU.add, [kq], [kq])
            k.act(CS1[s][P, :], kq[P, :], AF.Sin, [kq], [CS1[s]])
            k.dma("sp", CSD[s][0], CS1[s][P, :], [CS1[s]], [CSD[s]])
            k.dma("sp", CSD[s][1], CS2[s][P, :], [CS2[s]], [CSD[s]])
        k.barrier()

    scale = float(96 ** -0.5)
    if STOP == "rope":
        return nc, k

    for l in range(NL):
        Xsrc = [Tl(x_in.t[s], x_in.d) for s in range(NS)] if l == 0 else X
        last = l == NL - 1
        k.dma("sp", pp[:], pp_in[l], [pp_in], [pp])
        k.tt("dve", ppx[:, 0:11], pp[:, 8:19], pp[:, 19:30], ALU.add, [pp], [ppx])
        k.ts("dve", ppx[:, 0:11], ppx[:, 0:11], -1.0, 1.0, ALU.mult, ALU.add, [ppx], [ppx])
        k.ts("dve", ppx[:, 11:14], pp[:, 45:48], -1.0, 1.0, ALU.mult, ALU.add, [pp], [ppx])
        k.ts("dve", ppx[:, 14:17], pp[:, 45:48], -2.0, 2.0, ALU.mult, ALU.add, [pp], [ppx])

        with contextlib.ExitStack() as es:
            wm = sb(es, "wm", [128, 8, CM], BF)
            stg = [sb(es, "stg%d" % i, [128, CM], F32) for i in range(2)]
            for c in range(8):
                st = stg[c % 2]
                k.dma("sp", st[:], wmix_in[l, c * 128:(c + 1) * 128, :], [wmix_in], [st])
                k.ts("pool" if c % 2 else "dve", wm[:, c, :], st[:], pp[:, c:c + 1], None, ALU.mult, None, [st, pp], [wm])
            wuq = sb(es, "wuq", [128, 2, 576], BF)
            wuqs = sb(es, "wuqs", [128, 2, 576], BF)
            wkk = sb(es, "wkk", [128, 384], BF)
            wkv = sb(es, "wkv", [128, 384], BF)
            for c in range(2):
                k.dma("sp", stg[0][:, 0:576], wuq_in[l, c * 128:(c + 1) * 128, :], [wuq_in], [stg[0]])
                k.ts("dve", wuq[:, c, :], stg[0][:, 0:576], pp[:, 57 + c:58 + c], None, ALU.mult, None, [stg[0], pp], [wuq])
                k.dma("sp", stg[1][:, 0:576], wuqs_in[l, c * 128:(c + 1) * 128, :], [wuqs_in], [stg[1]])
                k.ts("dve", wuqs[:, c, :], stg[1][:, 0:576], pp[:, 57 + c:58 + c], None, ALU.mult, None, [stg[1], pp], [wuqs])
            k.dma("sp", stg[0][:, 0:384], wukvk_in[l], [wukvk_in], [stg[0]])
            k.ts("dve", wkk[:], stg[0][:, 0:384], pp[:, 59:60], None, ALU.mult, None, [stg[0], pp], [wkk])
            k.dma("sp", stg[1][:, 0:384], wukvv_in[l], [wukvv_in], [stg[1]])
            k.ts("dve", wkv[:], stg[1][:, 0:384], pp[:, 59:60], None, ALU.mult, None, [stg[1], pp], [wkv])

            chk("p1a")
            xb = [sb(es, "xb%d" % i, [128, D], F32) for i in range(2)]
            junk = sb(es, "junk", [128, D], F32)
            st4 = sb(es, "st4", [128, 8], F32)
            hb = sb(es, "hb", [128, D], BF)
            hT = sb(es, "hT", [128, 8, 512], BF)
            zo = [sb(es, "zo%d" % i, [128, 512], F32) for i in range(3)]
            zq = sb(es, "zq", [128, 2, 512], F32)
            zkv = sb(es, "zkv", [128, 512], F32)
            sq = sb(es, "sq", [128, 2, 512], F32)
            rb = sb(es, "rb", [128, 512], F32)
            cq = sb(es, "cq", [128, 2, 512], BF)
            ckv = sb(es, "ckv", [128, 512], BF)
            t1 = sb(es, "t1", [128, 512], F32)
            t2 = sb(es, "t2", [128, 512], F32)
            qts = sb(es, "qts", [128, 6, 512], BF)
            kts = sb(es, "kts", [128, 6, 512], BF)
            q32 = sb(es, "q32", [128, 512], F32)
            vas = sb(es, "vas", [128, 4, 6, 65], BF)
            zas = sb(es, "zas", [128, 4, 256], BF)
            kmx = sb(es, "kmx", [128, 6, NSL * NS + 1], F32)
            csl = sb(es, "csl", [128, 2, 512], F32)
            e96 = cst[0:96, C_E96:C_E96 + 97]
            onesr = sb(es, "onesr", [128, 512], F32)
            k.memset("pool", onesr[:], 1.0, [onesr])
            k.memset("pool", vas[:], 1.0, [vas])
            k.memset("pool", kts[:], 1.0, [kts])
            k.memset("pool", qts[:], 0.0, [qts])
            zi = 0
            hTs = [hT, sb(es, "hT2", [128, 8, 512], BF)]
            csls = [csl, sb(es, "csl2", [128, 2, 512], F32)]
            work = [(s, sl) for s in range(NS) for sl in range(NSL)]

            def front(wi):
                s, sl = work[wi]
                hT, csl = hTs[wi % 2], csls[wi % 2]
                S0 = sl * 512
                SL = slice(S0, S0 + 512)
                k.dma("sp", csl[64:96, 0, :], CSD[s][0, :, SL], [CSD[s]], [csl])
                k.dma("sp", csl[64:96, 1, :], CSD[s][1, :, SL], [CSD[s]], [csl])
                for tt in range(4):
                    tok0 = S0 + tt * 128
                    xt = xb[tt % 2]
                    k.dma("sp", xt[:], Xsrc[s][tok0:tok0 + 128, :], [Xsrc[s]], [xt])
                    if l == 0:
                        k.dma("sp", X[s][tok0:tok0 + 128, :], xt[:], [xt], [X[s]])
                    k.act(junk[:], xt[:], AF.Square, [xt], [junk, st4], accum_out=st4[:, 0:1])
                    k.ts("dve", st4[:, 1:2], st4[:, 0:1], 1.0 / D, 1e-6, ALU.mult, ALU.add, [st4], [st4])
                    k.act(st4[:, 2:3], st4[:, 1:2], AF.Sqrt, [st4], [st4])
                    k.recip(st4[:, 3:4], st4[:, 2:3], [st4], [st4])
                    k.ts("dve", hb[:], xt[:], st4[:, 3:4], None, ALU.mult, None, [xt, st4], [hb])
                    for c in range(8):
                        k.tr(PSB[:, c * 128:(c + 1) * 128], hb[:, c * 128:(c + 1) * 128], identb[:], [hb, identb], [PSB])
                    k.cp("act", hT[:, :, tt * 128:(tt + 1) * 128], PSB[:, :].rearrange("p (c t) -> p c t", c=8), [PSB], [hT])
                k.dma("sp", HT[s][:, :, SL], hT[:], [hT], [HT[s]])
                chk("p1b")

            def back(wi):
                nonlocal zi
                s, sl = work[wi]
                hT, csl = hTs[wi % 2], csls[wi % 2]
                S0 = sl * 512
                SL = slice(S0, S0 + 512)
                R = slice(64, 96)
                for cb in range(11):
                    ps = PS[cb % 3]
                    for c in range(8):
                        k.mm(ps[:], wm[:, c, M_ZR + cb * 128:M_ZR + (cb + 1) * 128], hT[:, c, :], c == 0, c == 7, [wm, hT], [ps])
                    z = zo[zi % 3]
                    zi += 1
                    k.cp("act" if cb % 2 else "dve", z[:], ps[:], [ps], [z])
                    k.dma("sp", ZR[s][cb * 128:(cb + 1) * 128, SL], z[:], [z], [ZR[s]])
                chk("p1c")
                for tt in range(4):
                    ps = PS[3]
                    for c in range(8):
                        k.mm(ps[:, 0:256], hT[:, c, tt * 128:(tt + 1) * 128], wm[:, c, M_ZA:M_ZA + 256], c == 0, c == 7, [wm, hT], [ps])
                    chk("p1c1")
                    k.cp("dve", zas[:, tt, :], ps[:, 0:256], [ps], [zas])
                    chk("p1c3")
                chk("p1c2")
                k.dma("sp", ZA[s][SL, :].rearrange("(a p) c -> p a c", p=128), zas[:], [zas], [ZA[s]])
                chk("p1d")
                for b in range(2):
                    ps = PS[4]
                    for c in range(8):
                        k.mm(ps[:], wm[:, c, M_ZQ + b * 128:M_ZQ + (b + 1) * 128], hT[:, c, :], c == 0, c == 7, [wm, hT], [ps])
                    k.cp("dve", zq[:, b, :], ps[:], [ps], [zq])
                    k.act(sq[:, b, :], zq[:, b, :], AF.Square, [zq], [sq])
                ps = PS[4]
                for b in range(2):
                    k.mm(ps[:], cst[:, C_OA:C_OA + 128], sq[:, b, :], b == 0, b == 1, [cst, sq], [ps])
                k.ts("dve", rb[:], ps[:], 1.0 / 256, 1e-6, ALU.mult, ALU.add, [ps], [rb])
                k.act(rb[:], rb[:], AF.Sqrt, [rb], [rb])
                k.recip(rb[:], rb[:], [rb], [rb])
                for b in range(2):
                    k.tt("dve", cq[:, b, :], zq[:, b, :], rb[:], ALU.mult, [zq, rb], [cq])
                ps = PS[5]
                for c in range(8):
                    k.mm(ps[:], wm[:, c, M_ZKV:M_ZKV + 128], hT[:, c, :], c == 0, c == 7, [wm, hT], [ps])
                k.cp("dve", zkv[:], ps[:], [ps], [zkv])
                k.act(sq[:, 0, :], zkv[:], AF.Square, [zkv], [sq])
                ps = PS[5]
                k.mm(ps[:], cst[:, C_OA:C_OA + 128], sq[:, 0, :], True, True, [cst, sq], [ps])
                k.ts("dve", rb[:], ps[:], 1.0 / 128, 1e-6, ALU.mult, ALU.add, [ps], [rb])
                k.act(rb[:], rb[:], AF.Sqrt, [rb], [rb])
                k.recip(rb[:], rb[:], [rb], [rb])
                k.tt("dve", ckv[:], zkv[:], rb[:], ALU.mult, [zkv, rb], [ckv])
                chk("p1e")
                R = slice(64, 96)
                pa, pb = PS[3], PS[4]
                for c in range(8):
                    k.mm(pa[0:96, :], wm[:, c, M_KR1:M_KR1 + 96], hT[:, c, :], c == 0, c == 7, [wm, hT], [pa])
                for c in range(8):
                    k.mm(pb[0:96, :], wm[:, c, M_KR2:M_KR2 + 96], hT[:, c, :], c == 0, c == 7, [wm, hT], [pb])
                k.tt("dve", t1[R, :], pa[R, :], csl[R, 0, :], ALU.mult, [pa, csl], [t1])
                k.tt("dve", t2[R, :], pb[R, :], csl[R, 1, :], ALU.mult, [pb, csl], [t2])
                k.tt("dve", t1[R, :], t1[R, :], t2[R, :], ALU.add, [t1, t2], [t1])
                for h in range(6):
                    k.cp("pool", kts[R, h, :], t1[R, :], [t1], [kts])
                chk("p1f")
                for h in range(6):
                    pq, pqs, pk = PS[0], PS[1], PS[2]
                    for b in range(2):
                        k.mm(pq[0:96, :], wuq[:, b, h * 96:(h + 1) * 96], cq[:, b, :], b == 0, b == 1, [wuq, cq], [pq])
                    for b in range(2):
                        k.mm(pqs[0:96, :], wuqs[:, b, h * 96:(h + 1) * 96], cq[:, b, :], b == 0, b == 1, [wuqs, cq], [pqs])
                    k.mm(pk[0:64, :], wkk[:, h * 64:(h + 1) * 64], ckv[:], True, True, [wkk, ckv], [pk])
                    k.ts("dve", q32[0:64, :], pq[0:64, :], scale, None, ALU.mult, None, [pq], [q32])
                    k.tt("dve", t1[R, :], pq[R, :], csl[R, 0, :], ALU.mult, [pq, csl], [t1])
                    k.tt("dve", t2[R, :], pqs[R, :], csl[R, 1, :], ALU.mult, [pqs, csl], [t2])
                    k.stt(q32[R, :], t1[R, :], 1.0, t2[R, :], ALU.mult, ALU.add, [t1, t2], [q32])
                    k.ts("dve", q32[R, :], q32[R, :], scale, None, ALU.mult, None, [q32], [q32])
                    k.cp("act", qts[0:96, h, :], q32[0:96, :], [q32], [qts])
                    k.cp("act", kts[0:64, h, :], pk[0:64, :], [pk], [kts])
                    k.tt("pool", t2[0:96, :], qts[0:96, h, :], qts[0:96, h, :], ALU.mult, [qts], [t2])
                    pn = PS[5]
                    k.mm(pn[0:97, :], e96, t2[0:96, :], True, True, [cst, t2], [pn])
                    k.act(t1[96:97, :], pn[96:97, :], AF.Sqrt, [pn], [t1])
                    k.ts("dve", qts[96:97, h, :], t1[96:97, :], -1.0, None, ALU.mult, None, [t1], [qts])
                    k.tt("pool", t2[0:96, :], kts[0:96, h, :], kts[0:96, h, :], ALU.mult, [kts], [t2])
                    pn = PS[6]
                    k.mm(pn[0:97, :], e96, t2[0:96, :], True, True, [cst, t2], [pn])
                    k.op("dve", lambda pn=pn, h=h, s=s, sl=sl: nc.vector.tensor_reduce(
                        out=kmx[96:97, h, s * NSL + sl:s * NSL + sl + 1], in_=pn[96:97, :], axis=AX.X, op=ALU.max), [pn], [kmx])
                chk("p1g")
                k.dma("sp", QT[s][:, :, SL].rearrange("h p t -> p h t"), qts[0:97, :, :], [qts], [QT[s]])
                k.dma("sp", KT[s][:, 0:96, SL].rearrange("h p t -> p h t"), kts[0:96, :, :], [kts], [KT[s]])
                chk("p1h")
                for tt in range(4):
                    ps = PS[3]
                    k.mm(ps[:, 0:384], ckv[:, tt * 128:(tt + 1) * 128], wkv[:], True, True, [ckv, wkv], [ps])
                    k.cp("act", vas[:, tt, :, 0:64], ps[:, 0:384].rearrange("p (h v) -> p h v", h=6), [ps], [vas])
                k.dma("sp", VA[s][SL, :].rearrange("(a p) c -> p a c", p=128), vas[:].rearrange("p a h v -> p a (h v)"), [vas], [VA[s]])

            def kbound(s):
                for h in range(6):
                    k.op("dve", lambda h=h, s=s: nc.vector.tensor_reduce(
                        out=kmx[96:97, h, NSL * NS:NSL * NS + 1], in_=kmx[96:97, h, s * NSL:(s + 1) * NSL], axis=AX.X, op=ALU.max), [kmx], [kmx])
                    k.act(kmx[96:97, h, NSL * NS:NSL * NS + 1], kmx[96:97, h, NSL * NS:NSL * NS + 1], AF.Sqrt, [kmx], [kmx])
                    k.ts("dve", kts[96:97, h, :], onesr[96:97, :], kmx[96:97, h, NSL * NS:NSL * NS + 1], None, ALU.mult, None, [kmx, onesr], [kts])
                for sl in range(NSL):
                    k.dma("sp", KT[s][:, 96:97, sl * 512:(sl + 1) * 512].rearrange("h p t -> p h t"), kts[96:97, :, :], [kts], [KT[s]])

            front(0)
            for wi in range(len(work)):
                if wi + 1 < len(work):
                    front(wi + 1)
                back(wi)
                if work[wi][1] == NSL - 1:
                    kbound(work[wi][0])
            k.barrier()

        if STOP == "p1":
            return nc, k
        for s in range(NS):
            fourier_phase(nc, k, sb, l, s, T, cst, PS, ZA, YA, dftc_in, dfts_in, fw_in)
            if STOP == "fourier":
                return nc, k
            mla_phase(nc, k, sb, l, s, T, cst, PS, QT, KT, VA, YC)
            if STOP == "mla":
                return nc, k
            rwkv_phase(nc, k, sb, l, s, T, cst, PS, PSB, identb, pp, ppx, ZR, YB, w2_in, a2_in)
            if STOP == "rwkv":
                return nc, k

        out_phase(nc, k, sb, l, NS, T, cst, PS, PSB, identb, pp, HT, YA, YB, YC, X, out_d, wgate_in, wproj_in, wout_in, fg_in, last)

    k.barrier()
    return nc, k


def fourier_phase(nc, k, sb, l, s, T, cst, PS, ZA, YA, dftc_in, dfts_in, fw_in):
    NT = T // 128
    NB = T // 512
    with contextlib.ExitStack() as es:
        za = sb(es, "f_za", [128, NT, 256], BF)
        fw = sb(es, "f_fw", [64, 256], F32)
        wc = sb(es, "f_wc", [128, 2, 256], BF)
        ws = sb(es, "f_ws", [128, 2, 256], BF)
        mats = [sb(es, "f_m%d" % i, [128, NT, 512], BF) for i in range(2)]
        a1 = sb(es, "f_a1", [128, 2, 512], BF)
        a2 = sb(es, "f_a2", [128, 2, 512], BF)
        yo = sb(es, "f_yo", [128, 2, 512], BF)
        k.dma("sp", za[:], ZA[s][:, :].rearrange("(a p) c -> p a c", p=128), [ZA[s]], [za])
        k.dma("sp", fw[:], fw_in[l], [fw_in], [fw])
        k.memset("pool", wc[:], 0.0, [wc])
        k.memset("pool", ws[:], 0.0, [ws])
        nrm = float(1.0 / np.sqrt(T * 64.0))
        pc, psn = PS[0], PS[1]
        k.mm(pc[0:64, 0:256], cst[0:64, C_C64:C_C64 + 64], fw[:], True, True, [cst, fw], [pc])
        k.mm(psn[0:64, 0:256], cst[0:64, C_S64:C_S64 + 64], fw[:], True, True, [cst, fw], [psn])
        for g in range(4):
            rows = slice((g % 2) * 64, (g % 2) * 64 + 64)
            cols = slice(g * 64, (g + 1) * 64)
            k.ts("dve", wc[rows, g // 2, cols], pc[0:64, cols], nrm, None, ALU.mult, None, [pc], [wc])
            k.ts("dve", ws[rows, g // 2, cols], psn[0:64, cols], -nrm, None, ALU.mult, None, [psn], [ws])
        for tb in range(NB):
            TB = slice(tb * 512, (tb + 1) * 512)
            for mi, src in enumerate((dftc_in, dfts_in)):
                m = mats[mi]
                for half in range(2):
                    hs = slice(half * (NT // 2), (half + 1) * (NT // 2)) if NT >= 2 else slice(0, NT)
                    if NT < 2 and half == 1:
                        continue
                    k.dma("sp", m[:, hs, :], src[:, TB].rearrange("(a p) n -> p a n", p=128)[:, hs, :], [src], [m])
                dst = a1 if mi == 0 else a2
                for cb in range(2):
                    ps = PS[2 + cb]
                    for c in range(NT):
                        k.mm(ps[:], za[:, c, cb * 128:(cb + 1) * 128], m[:, c, :], c == 0, c == NT - 1, [za, m], [ps])
                    k.cp("act" if cb else "dve", dst[:, cb, :], ps[:], [ps], [dst])
            for eb in range(2):
                ps = PS[4 + eb]
                i = 0
                for (w_, a_) in ((wc, a1), (ws, a2)):
                    for cb in range(2):
                        k.mm(ps[:], w_[:, cb, eb * 128:(eb + 1) * 128], a_[:, cb, :], i == 0, i == 3, [w_, a_], [ps])
                        i += 1
                k.cp("act", yo[:, eb, :], ps[:], [ps], [yo])
            k.dma("sp", YA[s][:, TB].rearrange("(e p) t -> p e t", p=128), yo[:], [yo], [YA[s]])
        k.barrier()


def mla_phase(nc, k, sb, l, s, T, cst, PS, QT, KT, VA, YC):
    NT = T // 128
    NB = T // 512
    with contextlib.ExitStack() as es:
        qt = [sb(es, "m_qt%d" % i, [128, T], BF) for i in range(2)]
        kt = [sb(es, "m_kt%d" % i, [128, T], BF) for i in range(2)]
        va = [sb(es, "m_va%d" % i, [128, NT, 65], BF) for i in range(2)]
        pt = [sb(es, "m_pt%d" % i, [128, 512], BF) for i in range(4)]
        osb = sb(es, "m_o", [128, 512], F32)
        rc = sb(es, "m_rc", [128, 512], F32)
        yo = [sb(es, "m_yo%d" % i, [64, 512], BF) for i in range(2)]
        pi = 0
        def ld(h):
            k.dma("sp", qt[h % 2][0:97, :], QT[s][h], [QT[s]], [qt[h % 2]])
            k.dma("sp", kt[h % 2][0:97, :], KT[s][h], [KT[s]], [kt[h % 2]])
            k.dma("sp", va[h % 2][:], VA[s][:, h * 65:(h + 1) * 65].rearrange("(a p) c -> p a c", p=128), [VA[s]], [va[h % 2]])

        ld(0)
        for h in range(6):
            q_, k_, v_ = qt[h % 2], kt[h % 2], va[h % 2]
            if h + 1 < 6:
                ld(h + 1)
            for qb in range(NB):
                QB = slice(qb * 512, (qb + 1) * 512)
                po = PS[4 + qb % 2]
                pq_ = {}
                for kc in range(NT + 2):
                    if kc < NT:
                        ps = PS[kc % 4]
                        k.mm(ps[:], k_[0:97, kc * 128:(kc + 1) * 128], q_[0:97, QB], True, True, [k_, q_], [ps])
                        p_ = pt[pi % 4]
                        pi += 1
                        k.act(p_[:], ps[:], AF.Exp, [ps], [p_])
                        pq_[kc] = p_
                    if kc >= 2:
                        j = kc - 2
                        k.mm(po[0:65, :], v_[:, j, :], pq_[j][:], j == 0, j == NT - 1, [v_, pq_[j]], [po])
                k.cp("dve", osb[0:65, :], po[0:65, :], [po], [osb])
                k.recip(rc[64:65, :], osb[64:65, :], [osb], [rc])
                pbc = PS[6]
                k.mm(pbc[0:64, :], cst[64:65, C_OA:C_OA + 64], rc[64:65, :], True, True, [cst, rc], [pbc])
                y_ = yo[(h * NB + qb) % 2]
                k.tt("dve", y_[:], osb[0:64, :], pbc[0:64, :], ALU.mult, [osb, pbc], [y_])
                k.dma("sp", YC[s][h * 64:(h + 1) * 64, QB], y_[:], [y_], [YC[s]])
        k.barrier()


def rwkv_phase(nc, k, sb, l, s, T, cst, PS, PSB, identb, pp, ppx, ZR, YB, w2_in, a2_in):
    NSL = T // 512
    ident = cst[:, C_ID:C_ID + 128]
    with contextlib.ExitStack() as es:
        w2 = sb(es, "r_w2", [128, 384], F32)
        a2w = sb(es, "r_a2", [128, 384], F32)
        k.dma("sp", w2[:], w2_in[l], [w2_in], [w2])
        k.dma("sp", a2w[:], a2_in[l], [a2_in], [a2w])
        zr = sb(es, "r_zr", [128, 11, 514], F32)
        zs = sb(es, "r_zs", [128, 11, 512], F32)
        th = sb(es, "r_th", [128, 512], F32)
        lw = sb(es, "r_lw", [128, 3, 512], F32)
        ic = sb(es, "r_ic", [128, 3, 512], F32)
        kk = sb(es, "r_kk", [128, 3, 512], F32)
        tA = sb(es, "r_tA", [128, 3, 512], F32)
        tB = sb(es, "r_tB", [128, 3, 512], F32)
        cum = sb(es, "r_cum", [128, 3, 512], F32)
        epos = sb(es, "r_ep", [128, 3, 512], F32)
        eneg = sb(es, "r_en", [128, 3, 512], F32)
        AR = sb(es, "r_AR", [128, 3, 2, 512], RDT)
        BK = sb(es, "r_BK", [128, 3, 2, 512], RDT)
        VV = sb(es, "r_VV", [128, 3, 512], RDT)
        Y = sb(es, "r_Y", [128, 3, 512], F32)
        Y0 = sb(es, "r_Y0", [128, 3, 512], F32)
        ST = sb(es, "r_ST", [128, 3, 64], RDT)
        TOKB = [[sb(es, "r_TOK%d_%d" % (i, b), [128, 384], RDT) for b in range(3)] for i in range(3)]
        NU = 6
        NAr = [[sb(es, "r_NAr%d_%d" % (j, i), [128, 256], RDT) for i in range(NU)] for j in range(2)]
        KA = [[sb(es, "r_KA%d_%d" % (j, i), [128, 256], RDT) for i in range(NU)] for j in range(2)]
        NN = [[[sb(es, "r_NN%d_%d_%d" % (q, i, j), [128, 256], RDT) for j in range(2)] for i in range(NU)] for q in range(2)]
        TM = [[[sb(es, "r_TM%d_%d_%d" % (q, i, j), [128, 128], RDT) for j in range(2)] for i in range(NU)] for q in range(2)]
        X0 = [sb(es, "r_X0%d" % i, [128, 64], RDT) for i in range(NU)]
        UT = [sb(es, "r_UT%d" % i, [128, 64], RDT) for i in range(NU)]
        ST32 = sb(es, "r_ST32", [128, 3, 64], F32)
        if RDT == BF:
            ptile, idn, idt = PSB, identb[:], identb
        else:
            ptile, idn, idt = PS[6], ident, cst
        for d in range(2):
            k.memset("pool", ST[:], 0.0, [ST])
            k.memset("pool", ST32[:], 0.0, [ST32])
            slabs = list(range(NSL)) if d == 0 else list(range(NSL - 1, -1, -1))

            def load_zr(sl_):
                S0_ = sl_ * 512
                lo = 1 if sl_ == 0 else 0
                hi = 1 if sl_ == NSL - 1 else 0
                if lo:
                    k.memset("pool", zr[:, :, 0:1], 0.0, [zr])
                if hi:
                    k.memset("pool", zr[:, :, 513:514], 0.0, [zr])
                k.dma("sp", zr[:, :, lo:514 - hi], ZR[s][:, S0_ - 1 + lo:S0_ + 513 - hi].rearrange("(b p) t -> p b t", p=128), [ZR[s]], [zr])
            m2 = cst[:, C_SF:C_SF + 256] if d == 0 else cst[:, C_SB:C_SB + 256]
            mT = cst[:, C_SB:C_SB + 128] if d == 0 else cst[:, C_SF:C_SF + 128]
            for sl in slabs:
                S0 = sl * 512
                if sl == slabs[0]:
                    load_zr(sl)
                for b in range(11):
                    k.act(zs[:, b, :], zr[:, b, 1:513], AF.Identity, [zr, ppx], [zs], scale=ppx[:, b:b + 1])
                    k.stt(zs[:, b, :], zr[:, b, 0:512], pp[:, 8 + b:9 + b], zs[:, b, :], ALU.mult, ALU.add, [zr, pp, zs], [zs])
                    k.stt(zs[:, b, :], zr[:, b, 2:514], pp[:, 19 + b:20 + b], zs[:, b, :], ALU.mult, ALU.add, [zr, pp, zs], [zs])
                si_ = slabs.index(sl)
                if si_ + 1 < len(slabs):
                    load_zr(slabs[si_ + 1])
                DR = slice(d * 64, d * 64 + 64)
                B3 = range(3)
                k.act(th[DR, :], zs[DR, 9, :], AF.Tanh, [zs], [th])
                for b in B3:
                    k.mm(PS[b][:], w2[DR, b * 128:(b + 1) * 128], th[DR, :], True, True, [w2, th], [PS[b]])
                for b in B3:
                    k.mm(PS[3 + b][:], a2w[DR, b * 128:(b + 1) * 128], zs[DR, 10, :], True, True, [a2w, zs], [PS[3 + b]])
                for b in B3:
                    k.ts("dve", kk[:, b, :], zs[:, 3 + b, :], pp[:, 42 + b:43 + b], None, ALU.mult, None, [zs, pp], [kk])
                for b in B3:
                    k.act(lw[:, b, :], PS[b][:], AF.Sigmoid, [PS[b], pp], [lw], bias=pp[:, 30 + d * 3 + b:31 + d * 3 + b])
                for b in B3:
                    k.act(ic[:, b, :], PS[3 + b][:], AF.Sigmoid, [PS[3 + b], pp], [ic], bias=pp[:, 36 + d * 3 + b:37 + d * 3 + b])
                for b in B3:
                    k.tt("pool", tA[:, b, :], kk[:, b, :], kk[:, b, :], ALU.mult, [kk], [tA])
                for b in B3:
                    k.mm(PS[b][:], cst[:, C_OB:C_OB + 128], tA[:, b, :], True, True, [cst, tA], [PS[b]])
                for b in B3:
                    k.ts("dve", lw[:, b, :], lw[:, b, :], -float(np.exp(-0.5)), None, ALU.mult, None, [lw], [lw])
                for b in B3:
                    k.op("dve", lambda b=b: nc.vector.tensor_tensor_scan(
                        out=cum[:, b, :], data0=cst[:, C_SEG:C_SEG + 512], data1=lw[:, b, :], initial=0.0,
                        op0=ALU.mult, op1=ALU.add), [cst, lw], [cum])
                for b in B3:
                    k.act(tB[:, b, :], PS[b][:], AF.Sqrt, [PS[b]], [tB])
                if d == 1:
                    for b in B3:
                        for c in range(4):
                            cs_ = slice(c * 128, (c + 1) * 128)
                            k.stt(tA[:, b, cs_], cum[:, b, cs_], cum[:, b, c * 128 + 127:c * 128 + 128], lw[:, b, cs_],
                                  ALU.subtract, ALU.subtract, [cum, lw], [tA])
                    for b in B3:
                        k.ts("dve", cum[:, b, :], tA[:, b, :], -1.0, None, ALU.mult, None, [tA], [cum])
                for b in B3:
                    k.ts("dve", tB[:, b, :], tB[:, b, :], 1e-12, None, ALU.max, None, [tB], [tB])
                for b in B3:
                    k.act(epos[:, b, :], cum[:, b, :], AF.Exp, [cum], [epos])
                for b in B3:
                    k.act(eneg[:, b, :], cum[:, b, :], AF.Exp, [cum], [eneg], scale=-1.0)
                for b in B3:
                    k.tt("pool", tA[:, b, :], cum[:, b, :], lw[:, b, :], ALU.subtract, [cum, lw], [tA])
                for b in B3:
                    k.recip(tB[:, b, :], tB[:, b, :], [tB], [tB])
                for b in B3:
                    k.act(tA[:, b, :], tA[:, b, :], AF.Exp, [tA], [tA])
                for b in B3:
                    k.tt("dve", kk[:, b, :], kk[:, b, :], tB[:, b, :], ALU.mult, [kk, tB], [kk])
                for b in B3:
                    k.tt("pool", AR[:, b, 1, :], zs[:, b, :], epos[:, b, :], ALU.mult, [zs, epos], [AR])
                for b in B3:
                    k.stt(AR[:, b, 0, :], kk[:, b, :], -1.0, tA[:, b, :], ALU.mult, ALU.mult, [kk, tA], [AR])
                for b in B3:
                    k.tt("dve", tB[:, b, :], kk[:, b, :], ic[:, b, :], ALU.mult, [kk, ic], [tB])
                for b in B3:
                    k.cp("pool", VV[:, b, :], zs[:, 6 + b, :], [zs], [VV])
                for b in B3:
                    k.tt("dve", BK[:, b, 0, :], tB[:, b, :], eneg[:, b, :], ALU.mult, [tB, eneg], [BK])
                for b in B3:
                    k.ts("dve", tB[:, b, :], ic[:, b, :], pp[:, 45 + b:46 + b], ppx[:, 11 + b:12 + b], ALU.mult, ALU.add, [ic, pp, ppx], [tB])
                for b in B3:
                    k.tt("pool", tB[:, b, :], tB[:, b, :], zs[:, 3 + b, :], ALU.mult, [tB, zs], [tB])
                for b in B3:
                    k.tt("dve", BK[:, b, 1, :], tB[:, b, :], eneg[:, b, :], ALU.mult, [tB, eneg], [BK])
                if d == 1:
                    k.dma("sp", Y0[:], YB[s][:, S0:S0 + 512].rearrange("(b p) t -> p b t", p=128), [YB[s]], [Y0])
                chunks = range(4) if d == 0 else range(3, -1, -1)
                TMC = {}

                def stage12(c):
                        cs_ = slice(c * 128, (c + 1) * 128)
                        gcol = c * 128 + 127 if d == 0 else c * 128
                        tkl = TOKB[c % 3]
                        HS = range(6)
                        HR = [slice((h % 2) * 64, (h % 2) * 64 + 64) for h in HS]
                        HB = [h // 2 for h in HS]
                        cs_ = slice(c * 128, (c + 1) * 128)
                        gcol = c * 128 + 127 if d == 0 else c * 128
                        tkl = TOKB[c % 3]
                        yield
                        for b in range(3):
                            k.tr(ptile[:, 0:128], BK[:, b, 0, cs_], idn, [BK, idt], [ptile])
                            k.tr(ptile[:, 128:256], BK[:, b, 1, cs_], idn, [BK, idt], [ptile])
                            k.tr(ptile[:, 256:384], VV[:, b, cs_], idn, [VV, idt], [ptile])
                            k.cp("act", tkl[b][:], ptile[:, 0:384], [ptile], [tkl[b]])
                        HS = range(6)
                        HR = [slice((h % 2) * 64, (h % 2) * 64 + 64) for h in HS]
                        HB = [h // 2 for h in HS]
                        yield
                        for h in HS:
                            b, bank = HB[h], PS[h]
                            k.mm(bank[:, 0:256], BK[HR[h], b, 0, cs_], AR[HR[h], b, :, cs_], True, True, [BK, AR], [bank])
                            k.mm(bank[:, 256:512], BK[HR[h], b, 1, cs_], AR[HR[h], b, :, cs_], True, True, [BK, AR], [bank])
                        yield
                        for h in HS:
                            bank = PS[h]
                            k.tt("dve", NAr[c % 2][h][:], bank[:, 0:256], m2, ALU.mult, [bank, cst], [NAr[c % 2][h]])
                            k.tt("dve", KA[c % 2][h][:], bank[:, 256:512], m2, ALU.mult, [bank, cst], [KA[c % 2][h]])
                        yield
                        for h in HS:
                            b, bank = HB[h], PS[h]
                            k.mm(bank[:, 0:128], AR[HR[h], b, 0, cs_], BK[HR[h], b, 0, cs_], True, True, [BK, AR], [bank])
                        cur = {}
                        tmc = {}
                        TMC[c] = tmc
                        yield
                        for h in HS:
                            bank = PS[h]
                            nn0 = NN[c % 2][h][0]
                            k.tt("dve", nn0[:, 128:256], bank[:, 0:128], mT, ALU.mult, [bank, cst], [nn0])
                            k.cp("pool", nn0[:, 0:128], NAr[c % 2][h][:, 0:128], [NAr[c % 2][h]], [nn0])
                            k.tt("pool", TM[c % 2][h][0][:], NAr[c % 2][h][:, 0:128], ident, ALU.add, [NAr[c % 2][h], cst], [TM[c % 2][h][0]])
                            cur[h] = nn0
                            tmc[h] = TM[c % 2][h][0]
                        yield
                        for lev in range(6):
                            yield
                            for h in HS:
                                bank = PS[h]
                                k.mm(bank[:, 0:128], cur[h][:, 128:256], cur[h][:, 0:128], True, True, [cur[h]], [bank])
                                k.mm(bank[:, 128:256], cur[h][:, 0:128], cur[h][:, 128:256], True, True, [cur[h]], [bank])
                            yield
                            for h in HS:
                                nxt = NN[c % 2][h][(lev + 1) % 2]
                                k.cp("act", nxt[:], PS[h][:, 0:256], [PS[h]], [nxt])
                                cur[h] = nxt
                            yield
                            for h in HS:
                                k.mm(PS[h][:, 256:384], cur[h][:, 128:256], tmc[h][:], True, True, [cur[h], tmc[h]], [PS[h]])
                            yield
                            for h in HS:
                                tm2 = TM[c % 2][h][(lev + 1) % 2]
                                k.tt("dve", tm2[:], PS[h][:, 256:384], tmc[h][:], ALU.add, [PS[h], tmc[h]], [tm2])
                                tmc[h] = tm2

                        yield

                def stage3(c):
                    cs_ = slice(c * 128, (c + 1) * 128)
                    gcol = c * 128 + 127 if d == 0 else c * 128
                    tkl = TOKB[c % 3]
                    HS = range(6)
                    HR = [slice((h % 2) * 64, (h % 2) * 64 + 64) for h in HS]
                    HB = [h // 2 for h in HS]
                    bank = PS[6]
                    XR = [slice(h * 64, (h + 1) * 64) for h in HS]
                    tv = [tkl[HB[h]][:, 256 + (h % 2) * 64:256 + (h % 2) * 64 + 64] for h in HS]
                    tb_ = [tkl[HB[h]][:, (h % 2) * 64:(h % 2) * 64 + 64] for h in HS]
                    tk_ = [tkl[HB[h]][:, 128 + (h % 2) * 64:128 + (h % 2) * 64 + 64] for h in HS]
                    yield
                    for h in HS:
                        b = HB[h]
                        k.mm(bank[:, XR[h]], AR[HR[h], b, 0, cs_], ST[HR[h], b, :], True, False, [AR, ST], [bank])
                        k.mm(bank[:, XR[h]], KA[c % 2][h][:, 0:128], tv[h], False, True, [KA[c % 2][h], tkl[b]], [bank])
                    yield
                    for h in HS:
                        k.cp("act", X0[h][:], bank[:, XR[h]], [bank], [X0[h]])
                    yield
                    for h in HS:
                        k.mm(bank[:, XR[h]], TMC[c][h][:], X0[h][:], True, True, [TMC[c][h], X0[h]], [bank])
                    yield
                    for h in HS:
                        k.cp("act", UT[h][:], bank[:, XR[h]], [bank], [UT[h]])
                    for pr in range(3):
                        yield
                        for h in (2 * pr, 2 * pr + 1):
                            b = HB[h]
                            o = (h % 2) * 192
                            k.mm(bank[0:64, o:o + 128], ST[HR[h], b, :], AR[HR[h], b, 1, cs_], True, False, [ST, AR], [bank])
                            k.mm(bank[0:64, o:o + 128], UT[h][:], NAr[c % 2][h][:, 128:256], False, False, [UT[h], NAr[c % 2][h]], [bank])
                            k.mm(bank[0:64, o:o + 128], tv[h], KA[c % 2][h][:, 128:256], False, True, [tkl[b], KA[c % 2][h]], [bank])
                            k.mm(bank[0:64, o + 128:o + 192], tb_[h], UT[h][:], True, False, [tkl[b], UT[h]], [bank])
                            k.mm(bank[0:64, o + 128:o + 192], tk_[h], tv[h], False, True, [tkl[b]], [bank])
                        yield
                        for h in (2 * pr, 2 * pr + 1):
                            b = HB[h]
                            o = (h % 2) * 192
                            if d == 0:
                                k.cp("act", Y[HR[h], b, cs_], bank[0:64, o:o + 128], [bank], [Y])
                            else:
                                k.tt("dve", Y[HR[h], b, cs_], bank[0:64, o:o + 128], Y0[HR[h], b, cs_], ALU.add, [bank, Y0], [Y])
                            k.tt("dve", ST32[HR[h], b, :], bank[0:64, o + 128:o + 192], ST32[HR[h], b, :], ALU.add, [bank, ST32], [ST32])
                    yield
                    for b in range(3):
                        k.ts("pool", ST32[:, b, :], ST32[:, b, :], epos[:, b, gcol:gcol + 1], None, ALU.mult, None, [ST32, epos], [ST32])
                    k.cp("act", ST[:], ST32[:], [ST32], [ST])
                    yield

                clist = list(chunks)
                for _ in stage12(clist[0]):
                    pass
                for ci, c in enumerate(clist):
                    g3 = stage3(c)
                    g12 = stage12(clist[ci + 1]) if ci + 1 < len(clist) else iter(())
                    done12 = done3 = False
                    while not (done12 and done3):
                        if not done12:
                            try:
                                next(g12)
                            except StopIteration:
                                done12 = True
                        if not done3:
                            try:
                                next(g3)
                            except StopIteration:
                                done3 = True
                if d == 1:
                    OB = cst[:, C_OB:C_OB + 128]
                    B3 = range(3)
                    for b in B3:
                        k.mm(PS[b][:], a2w[0:64, b * 128:(b + 1) * 128], zs[0:64, 10, :], True, True, [a2w, zs], [PS[b]])
                    for b in B3:
                        k.mm(PS[3 + b][:], OB, Y[:, b, :], True, True, [cst, Y], [PS[3 + b]])
                    for b in B3:
                        k.act(tA[:, b, :], PS[b][:], AF.Sigmoid, [PS[b], pp], [tA], bias=pp[:, 36 + b:37 + b])
                    for b in B3:
                        k.stt(cum[:, b, :], PS[3 + b][:], -1.0 / 64, Y[:, b, :], ALU.mult, ALU.add, [PS[3 + b], Y], [cum])
                    for b in B3:
                        k.tt("pool", eneg[:, b, :], cum[:, b, :], cum[:, b, :], ALU.mult, [cum], [eneg])
                    for b in B3:
                        k.mm(PS[3 + b][:], OB, eneg[:, b, :], True, True, [cst, eneg], [PS[3 + b]])
                    for b in B3:
                        k.tt("dve", tA[:, b, :], tA[:, b, :], ic[:, b, :], ALU.add, [tA, ic], [tA])
                    for b in B3:
                        k.ts("dve", tA[:, b, :], tA[:, b, :], pp[:, 45 + b:46 + b], ppx[:, 14 + b:15 + b], ALU.mult, ALU.add, [tA, pp, ppx], [tA])
                    for b in B3:
                        k.ts("dve", epos[:, b, :], PS[3 + b][:], 1.0 / 64, 64e-5, ALU.mult, ALU.add, [PS[3 + b]], [epos])
                    for b in B3:
                        k.act(epos[:, b, :], epos[:, b, :], AF.Sqrt, [epos], [epos])
                    for b in B3:
                        k.tt("dve", tA[:, b, :], tA[:, b, :], zs[:, 3 + b, :], ALU.mult, [tA, zs], [tA])
                    for b in B3:
                        k.stt(tA[:, b, :], zs[:, b, :], pp[:, 48 + b:49 + b], tA[:, b, :], ALU.mult, ALU.mult, [zs, pp, tA], [tA])
                    for b in B3:
                        k.mm(PS[b][:], OB, tA[:, b, :], True, True, [cst, tA], [PS[b]])
                    for b in B3:
                        k.recip(epos[:, b, :], epos[:, b, :], [epos], [epos])
                    for b in B3:
                        k.tt("dve", tB[:, b, :], PS[b][:], zs[:, 6 + b, :], ALU.mult, [PS[b], zs], [tB])
                    for b in B3:
                        k.tt("dve", cum[:, b, :], cum[:, b, :], epos[:, b, :], ALU.mult, [cum, epos], [cum])
                    for b in B3:
                        k.ts("dve", cum[:, b, :], cum[:, b, :], pp[:, 51 + b:52 + b], pp[:, 54 + b:55 + b], ALU.mult, ALU.add, [cum, pp], [cum])
                    for b in B3:
                        k.tt("pool", Y[:, b, :], cum[:, b, :], tB[:, b, :], ALU.add, [cum, tB], [Y])
                k.dma("sp", YB[s][:, S0:S0 + 512].rearrange("(b p) t -> p b t", p=128), Y[:], [Y], [YB[s]])
        k.barrier()


def out_phase(nc, k, sb, l, NS, T, cst, PS, PSB, identb, pp, HT, YA, YB, YC, X, out_d, wgate_in, wproj_in, wout_in, fg_in, last):
    NSL = T // 512
    with contextlib.ExitStack() as es:
        wg = sb(es, "o_wg", [128, 8, 4096], BF)
        wp = sb(es, "o_wp", [128, 8, D], BF)
        wo = sb(es, "o_wo", [128, 8, D], BF)
        stg = [sb(es, "o_stg%d" % i, [128, 2048], F32) for i in range(2)]
        si = 0
        for c in range(8):
            for hf in range(2):
                st = stg[si % 2]
                si += 1
                k.dma("sp", st[:], wgate_in[l, c * 128:(c + 1) * 128, hf * 2048:(hf + 1) * 2048], [wgate_in], [st])
                k.ts("pool" if si % 2 else "dve", wg[:, c, hf * 2048:(hf + 1) * 2048], st[:], pp[:, c:c + 1], None, ALU.mult, None, [st, pp], [wg])
        for c in range(8):
            st = stg[si % 2]
            si += 1
            k.dma("sp", st[:, 0:D], wproj_in[l, c * 128:(c + 1) * 128, :], [wproj_in], [st])
            k.dma("sp", st[:, D:2 * D], wout_in[l, c * 128:(c + 1) * 128, :], [wout_in], [st])
            k.cp("dve", wp[:, c, :], st[:, 0:D], [st], [wp])
            k.cp("pool", wo[:, c, :], st[:, D:2 * D], [st], [wo])
        hTs = [sb(es, "o_hT%d" % i, [128, 8, 512], BF) for i in range(2)]
        yas = [sb(es, "o_ya%d" % i, [128, 2, 512], BF) for i in range(2)]
        ybs = [sb(es, "o_yb%d" % i, [128, 3, 512], F32) for i in range(2)]
        ycs = [sb(es, "o_yc%d" % i, [128, 3, 512], BF) for i in range(2)]
        gs = [sb(es, "o_gs%d" % i, [128, 512], F32) for i in range(3)]
        yg = sb(es, "o_yg", [128, 8, 512], BF)
        mg = sb(es, "o_mg", [128, 8, 512], BF)
        tm_ = sb(es, "o_tm", [128, 512], F32)
        tm2 = sb(es, "o_tm2", [128, 512], F32)
        xb = [sb(es, "o_xb%d" % i, [128, D], F32) for i in range(2)]
        xn = [sb(es, "o_xn%d" % i, [128, D], F32) for i in range(2)]
        junk = sb(es, "o_junk", [128, D], F32)
        st4 = sb(es, "o_st4", [128, 8], F32)
        fgb = sb(es, "o_fgb", [128, D], F32)
        if last:
            for p in range(128):
                k.dma("sp", fgb[p:p + 1, :], fg_in[:, :], [fg_in], [fgb])
        gi = 0
        xi = 0
        work = [(s, sl) for s in range(NS) for sl in range(NSL)]

        def ld(wi):
            s_, sl_ = work[wi]
            j = wi % 2
            SL_ = slice(sl_ * 512, (sl_ + 1) * 512)
            k.dma("sp", hTs[j][:], HT[s_][:, :, SL_], [HT[s_]], [hTs[j]])
            k.dma("sp", yas[j][:], YA[s_][:, SL_].rearrange("(b p) t -> p b t", p=128), [YA[s_]], [yas[j]])
            k.dma("sp", ybs[j][:], YB[s_][:, SL_].rearrange("(b p) t -> p b t", p=128), [YB[s_]], [ybs[j]])
            k.dma("sp", ycs[j][:], YC[s_][:, SL_].rearrange("(b p) t -> p b t", p=128), [YC[s_]], [ycs[j]])

        ld(0)
        for wi, (s, sl) in enumerate(work):
            if True:
                SL = slice(sl * 512, (sl + 1) * 512)
                hT, ya, yb, yc = hTs[wi % 2], yas[wi % 2], ybs[wi % 2], ycs[wi % 2]
                if wi + 1 < len(work):
                    ld(wi + 1)
                for gb in range(8):
                    ps = PS[gb % 2]
                    for c in range(8):
                        k.mm(ps[:], wg[:, c, gb * 128:(gb + 1) * 128], hT[:, c, :], c == 0, c == 7, [wg, hT], [ps])
                    g_ = gs[gi % 3]
                    gi += 1
                    k.act(g_[:], ps[:], AF.Sigmoid, [ps], [g_])
                    k.tt("dve", g_[:], g_[:], ps[:], ALU.mult, [g_, ps], [g_])
                    if gb < 2:
                        src, srct = ya[:, gb, :], ya
                    elif gb < 5:
                        src, srct = yb[:, gb - 2, :], yb
                    else:
                        src, srct = yc[:, gb - 5, :], yc
                    k.tt("dve", yg[:, gb, :], g_[:], src, ALU.mult, [g_, srct], [yg])
                for ob in range(8):
                    OBS = slice(ob * 128, (ob + 1) * 128)
                    pa, pb, pc = PS[2], PS[3], PS[4]
                    for i, cb in enumerate((0, 1)):
                        k.mm(pa[:], wp[:, cb, OBS], yg[:, cb, :], i == 0, i == 1, [wp, yg], [pa])
                    for i, cb in enumerate((2, 3, 4)):
                        k.mm(pb[:], wp[:, cb, OBS], yg[:, cb, :], i == 0, i == 2, [wp, yg], [pb])
                    for i, cb in enumerate((5, 6, 7)):
                        k.mm(pc[:], wp[:, cb, OBS], yg[:, cb, :], i == 0, i == 2, [wp, yg], [pc])
                    sg = []
                    for j in range(3):
                        ps = PS[j % 2]
                        col = 1024 + j * 1024 + ob * 128
                        for c in range(8):
                            k.mm(ps[:], wg[:, c, col:col + 128], hT[:, c, :], c == 0, c == 7, [wg, hT], [ps])
                        g_ = gs[gi % 3]
                        gi += 1
                        k.act(g_[:], ps[:], AF.Sigmoid, [ps], [g_])
                        sg.append(g_)
                    k.tt("dve", tm_[:], pa[:], sg[0][:], ALU.mult, [pa, sg[0]], [tm_])
                    k.tt("dve", tm2[:], pb[:], sg[1][:], ALU.mult, [pb, sg[1]], [tm2])
                    k.tt("pool", tm_[:], tm_[:], tm2[:], ALU.add, [tm_, tm2], [tm_])
                    k.tt("dve", tm2[:], pc[:], sg[2][:], ALU.mult, [pc, sg[2]], [tm2])
                    k.tt("pool", mg[:, ob, :], tm_[:], tm2[:], ALU.add, [tm_, tm2], [mg])
                for tt in range(4):
                    tok0 = sl * 512 + tt * 128
                    xt = xb[xi % 2]
                    xo = xn[xi % 2]
                    xi += 1
                    k.dma("sp", xt[:], X[s][tok0:tok0 + 128, :], [X[s]], [xt])
                    for hf in range(2):
                        ps = PS[5 + hf]
                        for ob in range(8):
                            k.mm(ps[:], mg[:, ob, tt * 128:(tt + 1) * 128], wo[:, ob, hf * 512:(hf + 1) * 512], ob == 0, ob == 7, [mg, wo], [ps])
                        k.tt("dve", xo[:, hf * 512:(hf + 1) * 512], ps[:], xt[:, hf * 512:(hf + 1) * 512], ALU.add, [ps, xt], [xo])
                    if not last:
                        k.dma("sp", X[s][tok0:tok0 + 128, :], xo[:], [xo], [X[s]])
                    else:
                        k.act(junk[:], xo[:], AF.Square, [xo], [junk, st4], accum_out=st4[:, 0:1])
                        k.ts("dve", st4[:, 1:2], st4[:, 0:1], 1.0 / D, 1e-6, ALU.mult, ALU.add, [st4], [st4])
                        k.act(st4[:, 2:3], st4[:, 1:2], AF.Sqrt, [st4], [st4])
                        k.recip(st4[:, 3:4], st4[:, 2:3], [st4], [st4])
                        k.stt(xo[:], xo[:], st4[:, 3:4], fgb[:], ALU.mult, ALU.mult, [xo, st4, fgb], [xo])
                        k.dma("sp", out_d[s, tok0:tok0 + 128, :], xo[:], [xo], [out_d])
        k.barrier()


def _consts():
    c = np.zeros((128, NCST), np.float32)
    c[:, C_ID:C_ID + 128] = np.eye(128)
    j = np.arange(128)[:, None]
    t = np.arange(128)[None, :]
    c[:, C_SF:C_SF + 128] = j < t
    c[:, C_IF:C_IF + 128] = j <= t
    c[:, C_SB:C_SB + 128] = j > t
    c[:, C_IB:C_IB + 128] = j >= t
    c[:, C_OB:C_OB + 128] = (j // 64) == (t // 64)
    c[:, C_OA:C_OA + 128] = 1.0
    seg = np.ones(512, np.float32)
    seg[::128] = 0.0
    c[:, C_SEG:C_SEG + 512] = seg[None, :]
    a = np.arange(64)
    ang = 2 * np.pi * np.outer(a, a) / 64.0
    c[0:64, C_C64:C_C64 + 64] = np.cos(ang)
    c[0:64, C_S64:C_S64 + 64] = np.sin(ang)
    inv = (10000.0 ** (-np.arange(0, 32, 2, dtype=np.float32) / np.float32(32))).astype(np.float32)
    for p in range(64, 96):
        c[p, C_INVF] = inv[(p - 64) % 16]
        c[p, C_SGN] = -1.0 if p < 80 else 1.0
    c[0:96, C_E96 + 96] = 1.0
    return c


def _dft(T):
    n = np.arange(T, dtype=np.int64)
    m = (np.outer(n, n) % T).astype(np.float64) * (2 * np.pi / T)
    return np.cos(m).astype(ml_dtypes.bfloat16), np.sin(m).astype(ml_dtypes.bfloat16)


def host_inputs(inp, T, NL, NS, ncores):
    f = lambda a: np.ascontiguousarray(np.asarray(a), dtype=np.float32)
    w_in = f(inp["w_in"])
    wmix = np.concatenate([
        w_in[:, :, O_ZR:O_ZR + 1408], w_in[:, :, O_ZQ:O_ZQ + 256], w_in[:, :, O_ZKV:O_ZKV + 128],
        w_in[:, :, O_ZKV:O_ZKV + 64], w_in[:, :, O_ZKR:O_ZKR + 32],
        w_in[:, :, O_ZKV:O_ZKV + 64], w_in[:, :, O_ZKR + 16:O_ZKR + 32], w_in[:, :, O_ZKR:O_ZKR + 16],
        w_in[:, :, O_ZA:O_ZA + 256]], axis=2)
    assert wmix.shape[2] == CM
    wgate = np.concatenate([w_in[:, :, O_ZAG:O_ZAG + 256], w_in[:, :, O_ZBG:O_ZBG + 384],
                            w_in[:, :, O_ZCG:O_ZCG + 384], w_in[:, :, O_ZM:O_ZM + 3072]], axis=2)
    wproj = np.concatenate([f(inp["proj_a"]), f(inp["proj_b"]), f(inp["proj_c"])], axis=1)
    pp = np.zeros((NL, 128, NPP), np.float32)

    def blk(v, n):
        return np.transpose(v.reshape(NL, n, 128), (0, 2, 1))

    pp[:, :, 0:8] = blk(f(inp["norm_g"]), 8)
    pp[:, :, 8:19] = blk(f(inp["shift_mu_prev"]), 11)
    pp[:, :, 19:30] = blk(f(inp["shift_mu_next"]), 11)
    pp[:, :, 30:36] = blk(f(inp["decay_w0"]).reshape(NL, 768), 6)
    pp[:, :, 36:42] = blk(f(inp["iclr_a0"]).reshape(NL, 768), 6)
    pp[:, :, 42:45] = blk(f(inp["key_k"]), 3)
    pp[:, :, 45:48] = blk(f(inp["key_a"]), 3)
    pp[:, :, 48:51] = blk(f(inp["bonus_r_k"]).reshape(NL, 384), 3)
    pp[:, :, 51:54] = blk(f(inp["lnx_g"]), 3)
    pp[:, :, 54:57] = blk(f(inp["lnx_b"]), 3)
    pp[:, :, 57:59] = blk(f(inp["q_norm_g"]), 2)
    pp[:, :, 59:60] = blk(f(inp["kv_norm_g"]), 1)
    fw = np.transpose(f(inp["fourier_w"]), (0, 2, 1, 3)).reshape(NL, 64, 256)
    w2 = f(inp["decay_w2"]).reshape(NL, 128, 384)
    a2 = f(inp["iclr_a2"]).reshape(NL, 128, 384)
    wuq = f(inp["w_uq"])
    wq4 = wuq.reshape(NL, 256, 6, 96)
    wuqs = np.concatenate([wq4[..., 0:64], wq4[..., 80:96], wq4[..., 64:80]], axis=-1).reshape(NL, 256, 576)
    wkv4 = f(inp["w_ukv"]).reshape(NL, 128, 6, 128)
    wukvk = np.ascontiguousarray(wkv4[..., 0:64]).reshape(NL, 128, 384)
    wukvv = np.ascontiguousarray(wkv4[..., 64:128]).reshape(NL, 128, 384)
    dc, ds = _dft(T)
    shared = dict(cst=_consts(), dftc=dc, dfts=ds, wmix=np.ascontiguousarray(wmix), wgate=np.ascontiguousarray(wgate),
                  wproj=np.ascontiguousarray(wproj), wout=f(inp["w_out"]), pp=pp, fw=np.ascontiguousarray(fw), w2=w2, a2=a2,
                  wuq=np.ascontiguousarray(wuq), wuqs=np.ascontiguousarray(wuqs), wukvk=wukvk, wukvv=wukvv,
                  fg=f(inp["final_g"]).reshape(1, D))
    x = f(inp["x"])
    pos = np.ascontiguousarray(np.asarray(inp["positions"]), dtype=np.int32)
    maps = []
    for c in range(ncores):
        m = dict(shared)
        m["x"] = np.ascontiguousarray(x[c * NS:(c + 1) * NS])
        m["pos"] = np.ascontiguousarray(pos[c * NS:(c + 1) * NS]).reshape(NS, 1, T)
        maps.append(m)
    return maps


def kernel(**inputs):
    x = np.asarray(inputs["x"])
    B, T, _ = x.shape
    NL = np.asarray(inputs["w_in"]).shape[0]
    NS = B // NCORES
    nc, _ = build(T, NL, NS)
    maps = host_inputs(inputs, T, NL, NS, NCORES)
    res = run_bass_kernel_spmd(nc, maps, core_ids=list(range(NCORES)))
    return np.concatenate([np.asarray(r["out"], dtype=np.float32) for r in res.results], axis=0)
```

```python
import contextlib
import numpy as np
import ml_dtypes
import concourse.bass as bass
import concourse.mybir as mybir
from concourse.bass_utils import run_bass_kernel_spmd

F32 = mybir.dt.float32
BF = mybir.dt.bfloat16
I32 = mybir.dt.int32
AF = mybir.ActivationFunctionType
ALU = mybir.AluOpType
AX = mybir.AxisListType

D = 1024
NCORES = 8
CH = 128
RDT = BF
STOP = None


class StopBuild(Exception):
    pass


def chk(tag):
    if STOP == tag:
        raise StopBuild()


class Dep:
    __slots__ = ("w", "r")

    def __init__(self):
        self.w = {}
        self.r = {}


class Tl:
    def __init__(self, t, d=None, ps=False):
        self.t = t
        self.d = d or Dep()
        self.ps = ps

    def __getitem__(self, idx):
        return self.t[idx]


class K:
    NDMA = 24

    def __init__(self, nc):
        self.nc = nc
        self.es = contextlib.ExitStack()
        self.eng = dict(pe=nc.tensor, act=nc.scalar, dve=nc.vector, pool=nc.gpsimd, sp=nc.sync)
        self.sem = {}
        self.cnt = {}
        for e in ["pe", "act", "dve", "pool"]:
            self.sem[e] = self.es.enter_context(nc.semaphore("s_" + e))
            self.cnt[e] = 0
        for i in range(self.NDMA):
            key = ("d", i)
            self.sem[key] = self.es.enter_context(nc.semaphore("d%d" % i))
            self.cnt[key] = 0
        self.rr = 0
        self.waited = {}
        self.ninstr = 0

    def _wait(self, e, reads, writes):
        need = {}
        for t in reads:
            for key, v in t.d.w.items():
                need[key] = max(need.get(key, 0), v)
            if t.ps:
                for key, v in t.d.r.items():
                    need[key] = max(need.get(key, 0), v)
        for t in writes:
            for key, v in t.d.w.items():
                need[key] = max(need.get(key, 0), v)
            for key, v in t.d.r.items():
                need[key] = max(need.get(key, 0), v)
        for key, v in need.items():
            if key == e and e == "pe":
                continue
            if self.waited.get((e, key), 0) < v:
                self.eng[e].wait_ge(self.sem[key], v)
                self.waited[(e, key)] = v
                self.ninstr += 1

    def _done(self, key, v, reads, writes):
        for t in writes:
            t.d.w = {key: v}
            t.d.r = {}
        for t in reads:
            if t.d.r.get(key, 0) < v:
                t.d.r[key] = v

    def op(self, e, fn, r, w):
        self._wait(e, r, w)
        ins = fn()
        self.cnt[e] += 1
        ins.then_inc(self.sem[e], 1)
        self._done(e, self.cnt[e], r, w)
        self.ninstr += 1
        return ins

    def dma(self, q, out, in_, r, w):
        self._wait(q, r, w)
        key = ("d", self.rr)
        self.rr = (self.rr + 1) % self.NDMA
        if self.waited.get((q, key), 0) < self.cnt[key]:
            self.eng[q].wait_ge(self.sem[key], self.cnt[key])
            self.waited[(q, key)] = self.cnt[key]
        ins = self.eng[q].dma_start(out=out, in_=in_)
        self.cnt[key] += 16
        ins.then_inc(self.sem[key], 16)
        self._done(key, self.cnt[key], r, w)
        self.ninstr += 1

    def barrier(self, engines=("pe", "act", "dve", "pool", "sp")):
        for e in engines:
            for key, v in self.cnt.items():
                if key == e or v == 0:
                    continue
                if self.waited.get((e, key), 0) < v:
                    self.eng[e].wait_ge(self.sem[key], v)
                    self.waited[(e, key)] = v

    def mm(self, out, lhsT, rhs, start, stop, r, w):
        return self.op("pe", lambda: self.nc.tensor.matmul(out, lhsT, rhs, start=start, stop=stop), r, w)

    def tr(self, out, in_, ident, r, w):
        return self.op("pe", lambda: self.nc.tensor.transpose(out, in_, ident), r, w)

    def act(self, out, in_, func, r, w, **kw):
        return self.op("act", lambda: self.nc.scalar.activation(out=out, in_=in_, func=func, **kw), r, w)

    def ts(self, e, out, in0, s1, s2, op0, op1, r, w):
        eng = self.eng[e]
        if op1 is None:
            return self.op(e, lambda: eng.tensor_scalar(out=out, in0=in0, scalar1=s1, scalar2=None, op0=op0), r, w)
        return self.op(e, lambda: eng.tensor_scalar(out=out, in0=in0, scalar1=s1, scalar2=s2, op0=op0, op1=op1), r, w)

    def tt(self, e, out, in0, in1, op, r, w):
        eng = self.eng[e]
        return self.op(e, lambda: eng.tensor_tensor(out=out, in0=in0, in1=in1, op=op), r, w)

    def stt(self, out, in0, scalar, in1, op0, op1, r, w):
        return self.op("dve", lambda: self.nc.vector.scalar_tensor_tensor(out=out, in0=in0, scalar=scalar, in1=in1, op0=op0, op1=op1), r, w)

    def cp(self, e, out, in_, r, w):
        if e == "act":
            return self.op(e, lambda: self.nc.scalar.copy(out=out, in_=in_), r, w)
        eng = self.eng[e]
        return self.op(e, lambda: eng.tensor_copy(out=out, in_=in_), r, w)

    def recip(self, out, in_, r, w):
        return self.op("dve", lambda: self.nc.vector.reciprocal(out=out, in_=in_), r, w)

    def memset(self, e, ap, val, w):
        eng = self.eng[e]
        return self.op(e, lambda: eng.memset(ap, val), [], w)


O_ZA, O_ZAG, O_ZR, O_ZBG, O_ZQ, O_ZKV, O_ZKR, O_ZCG, O_ZM = 0, 256, 512, 1920, 2304, 2560, 2688, 2720, 3104
M_ZR, M_ZQ, M_ZKV, M_KR1, M_KR2, M_ZA, CM = 0, 1408, 1664, 1792, 1888, 1984, 2240
NPP = 60
C_ID, C_SF, C_IF, C_SB, C_IB, C_OB, C_OA, C_SEG, C_C64, C_S64, C_INVF, C_SGN, C_E96, NCST = (
    0, 128, 256, 384, 512, 640, 768, 896, 1408, 1472, 1536, 1537, 1538, 1538 + 97)


def build(T, NL, NS, debug=False):
    nc = bass.Bass("TRN2", target_bir_lowering=False)
    k = K(nc)
    try:
        return _build(nc, k, T, NL, NS, debug)
    except StopBuild:
        k.barrier()
        return nc, k


def _build(nc, k, T, NL, NS, debug):
    NSL = T // 512
    NCK = T // CH
    NT = T // 128
    okind = "ExternalOutput" if debug else "Internal"

    def din(name, shape, dt):
        return Tl(nc.dram_tensor(name, shape, dt, kind="ExternalInput").ap())

    x_in = din("x", [NS, T, D], F32)
    pos_in = din("pos", [NS, 1, T], I32)
    cst_in = din("cst", [128, NCST], F32)
    dftc_in = din("dftc", [T, T], BF)
    dfts_in = din("dfts", [T, T], BF)
    wmix_in = din("wmix", [NL, D, CM], F32)
    wgate_in = din("wgate", [NL, D, 4096], F32)
    wproj_in = din("wproj", [NL, D, D], F32)
    wout_in = din("wout", [NL, D, D], F32)
    pp_in = din("pp", [NL, 128, NPP], F32)
    fw_in = din("fw", [NL, 64, 256], F32)
    w2_in = din("w2", [NL, 128, 384], F32)
    a2_in = din("a2", [NL, 128, 384], F32)
    wuq_in = din("wuq", [NL, 256, 576], F32)
    wuqs_in = din("wuqs", [NL, 256, 576], F32)
    wukvk_in = din("wukvk", [NL, 128, 384], F32)
    wukvv_in = din("wukvv", [NL, 128, 384], F32)
    fg_in = din("fg", [1, D], F32)
    out_d = Tl(nc.dram_tensor("out", [NS, T, D], F32, kind="ExternalOutput").ap())

    def dscr(name, shape, dt):
        return Tl(nc.dram_tensor(name, shape, dt, kind=okind).ap())

    X = [dscr("X%d" % s, [T, D], F32) for s in range(NS)]
    HT = [dscr("HT%d" % s, [128, 8, T], BF) for s in range(NS)]
    ZR = [dscr("ZR%d" % s, [1408, T], F32) for s in range(NS)]
    ZA = [dscr("ZA%d" % s, [T, 256], BF) for s in range(NS)]
    QT = [dscr("QT%d" % s, [6, 97, T], BF) for s in range(NS)]
    KT = [dscr("KT%d" % s, [6, 97, T], BF) for s in range(NS)]
    VA = [dscr("VA%d" % s, [T, 6 * 65], BF) for s in range(NS)]
    YA = [dscr("YA%d" % s, [256, T], BF) for s in range(NS)]
    YB = [dscr("YB%d" % s, [384, T], F32) for s in range(NS)]
    YC = [dscr("YC%d" % s, [384, T], BF) for s in range(NS)]

    es0 = k.es

    uid = [0]

    def sb(es, name, shape, dt):
        uid[0] += 1
        return Tl(es.enter_context(nc.sbuf_tensor("sb%d_%s" % (uid[0], name), shape, dt)))

    cst = sb(es0, "cst", [128, NCST], F32)
    identb = sb(es0, "identb", [128, 128], BF)
    onesb = sb(es0, "onesb", [128, 128], BF)
    pp = sb(es0, "pp", [128, NPP], F32)
    ppx = sb(es0, "ppx", [128, 40], F32)
    CSD = [dscr("CSD%d" % s, [2, 32, T], F32) for s in range(NS)]
    PS = [Tl(es0.enter_context(nc.psum_tensor("ps%d" % i, [128, 512], F32)), ps=True) for i in range(7)]
    PSB = Tl(es0.enter_context(nc.psum_tensor("psb", [128, 1024], BF)), ps=True)

    k.dma("sp", cst[:], cst_in[:, :], [cst_in], [cst])
    k.cp("dve", identb[:], cst[:, C_ID:C_ID + 128], [cst], [identb])
    k.cp("dve", onesb[:], cst[:, C_OA:C_OA + 128], [cst], [onesb])
    ident = cst[:, C_ID:C_ID + 128]

    with contextlib.ExitStack() as es:
        posi = sb(es, "posi", [128, T], I32)
        ang = sb(es, "ang", [128, T], F32)
        kq = sb(es, "kq", [128, T], F32)
        rr_ = sb(es, "rr", [128, T], F32)
        cs1t = sb(es, "cs1t", [128, T], F32)
        cs2t = sb(es, "cs2t", [128, T], F32)
        CS1 = [cs1t] * NS
        CS2 = [cs2t] * NS
        P = slice(64, 96)
        for s in range(NS):
            for p in range(64, 96):
                k.dma("sp", posi[p:p + 1, :], pos_in[s, :, :], [pos_in], [posi])
            k.cp("dve", ang[P, :], posi[P, :], [posi], [ang])
            k.ts("dve", ang[P, :], ang[P, :], cst[P, C_INVF:C_INVF + 1], None, ALU.mult, None, [ang, cst], [ang])
            k.ts("dve", kq[P, :], ang[P, :], float(1.0 / (2 * np.pi)), None, ALU.mult, None, [ang], [kq])
            k.ts("dve", kq[P, :], kq[P, :], 12582912.0, None, ALU.add, None, [kq], [kq])
            k.ts("dve", kq[P, :], kq[P, :], -12582912.0, None, ALU.add, None, [kq], [kq])
            c1 = 6.28125
            c2 = float(np.float32(2 * np.pi - c1))
            c3 = float(2 * np.pi - c1 - np.float64(np.float32(2 * np.pi - c1)))
            k.stt(rr_[P, :], kq[P, :], -c1, ang[P, :], ALU.mult, ALU.add, [kq, ang], [rr_])
            k.stt(rr_[P, :], kq[P, :], -c2, rr_[P, :], ALU.mult, ALU.add, [kq, rr_], [rr_])
            k.stt(rr_[P, :], kq[P, :], -c3, rr_[P, :], ALU.mult, ALU.add, [kq, rr_], [rr_])
            k.ts("dve", rr_[P, :], rr_[P, :], 3.1415925, -3.1415925, ALU.min, ALU.max, [rr_], [rr_])
            k.act(CS2[s][P, :], rr_[P, :], AF.Sin, [rr_], [CS2[s]])
            k.ts("dve", CS2[s][P, :], CS2[s][P, :], cst[P, C_SGN:C_SGN + 1], None, ALU.mult, None, [CS2[s], cst], [CS2[s]])
            k.ts("dve", kq[P, :], rr_[P, :], -1.0, None, ALU.mult, None, [rr_], [kq])
            k.tt("dve", kq[P, :], kq[P, :], rr_[P, :], ALU.max, [kq, rr_], [kq])
            k.ts("dve", kq[P, :], kq[P, :], -1.0, float(np.pi / 2), ALU.mult, ALU.add, [kq], [kq])
            k.act(CS1[s][P, :], kq[P, :], AF.Sin, [kq], [CS1[s]])
            k.dma("sp", CSD[s][0], CS1[s][P, :], [CS1[s]], [CSD[s]])
            k.dma("sp", CSD[s][1], CS2[s][P, :], [CS2[s]], [CSD[s]])
        k.barrier()

    scale = float(96 ** -0.5)
    if STOP == "rope":
        return nc, k

    for l in range(NL):
        Xsrc = [Tl(x_in.t[s], x_in.d) for s in range(NS)] if l == 0 else X
        last = l == NL - 1
        k.dma("sp", pp[:], pp_in[l], [pp_in], [pp])
        k.tt("dve", ppx[:, 0:11], pp[:, 8:19], pp[:, 19:30], ALU.add, [pp], [ppx])
        k.ts("dve", ppx[:, 0:11], ppx[:, 0:11], -1.0, 1.0, ALU.mult, ALU.add, [ppx], [ppx])
        k.ts("dve", ppx[:, 11:14], pp[:, 45:48], -1.0, 1.0, ALU.mult, ALU.add, [pp], [ppx])
        k.ts("dve", ppx[:, 14:17], pp[:, 45:48], -2.0, 2.0, ALU.mult, ALU.add, [pp], [ppx])

        with contextlib.ExitStack() as es:
            wm = sb(es, "wm", [128, 8, CM], BF)
            stg = [sb(es, "stg%d" % i, [128, CM], F32) for i in range(2)]
            for c in range(8):
                st = stg[c % 2]
                k.dma("sp", st[:], wmix_in[l, c * 128:(c + 1) * 128, :], [wmix_in], [st])
                k.ts("pool" if c % 2 else "dve", wm[:, c, :], st[:], pp[:, c:c + 1], None, ALU.mult, None, [st, pp], [wm])
            wuq = sb(es, "wuq", [128, 2, 576], BF)
            wuqs = sb(es, "wuqs", [128, 2, 576], BF)
            wkk = sb(es, "wkk", [128, 384], BF)
            wkv = sb(es, "wkv", [128, 384], BF)
            for c in range(2):
                k.dma("sp", stg[0][:, 0:576], wuq_in[l, c * 128:(c + 1) * 128, :], [wuq_in], [stg[0]])
                k.ts("dve", wuq[:, c, :], stg[0][:, 0:576], pp[:, 57 + c:58 + c], None, ALU.mult, None, [stg[0], pp], [wuq])
                k.dma("sp", stg[1][:, 0:576], wuqs_in[l, c * 128:(c + 1) * 128, :], [wuqs_in], [stg[1]])
                k.ts("dve", wuqs[:, c, :], stg[1][:, 0:576], pp[:, 57 + c:58 + c], None, ALU.mult, None, [stg[1], pp], [wuqs])
            k.dma("sp", stg[0][:, 0:384], wukvk_in[l], [wukvk_in], [stg[0]])
            k.ts("dve", wkk[:], stg[0][:, 0:384], pp[:, 59:60], None, ALU.mult, None, [stg[0], pp], [wkk])
            k.dma("sp", stg[1][:, 0:384], wukvv_in[l], [wukvv_in], [stg[1]])
            k.ts("dve", wkv[:], stg[1][:, 0:384], pp[:, 59:60], None, ALU.mult, None, [stg[1], pp], [wkv])

            chk("p1a")
            xb = [sb(es, "xb%d" % i, [128, D], F32) for i in range(2)]
            junk = sb(es, "junk", [128, D], F32)
            st4 = sb(es, "st4", [128, 8], F32)
            hb = sb(es, "hb", [128, D], BF)
            hT = sb(es, "hT", [128, 8, 512], BF)
            zo = [sb(es, "zo%d" % i, [128, 512], F32) for i in range(3)]
            zq = sb(es, "zq", [128, 2, 512], F32)
            zkv = sb(es, "zkv", [128, 512], F32)
            sq = sb(es, "sq", [128, 2, 512], F32)
            rb = sb(es, "rb", [128, 512], F32)
            cq = sb(es, "cq", [128, 2, 512], BF)
            ckv = sb(es, "ckv", [128, 512], BF)
            t1 = sb(es, "t1", [128, 512], F32)
            t2 = sb(es, "t2", [128, 512], F32)
            qts = sb(es, "qts", [128, 6, 512], BF)
            kts = sb(es, "kts", [128, 6, 512], BF)
            q32 = sb(es, "q32", [128, 512], F32)
            vas = sb(es, "vas", [128, 4, 6, 65], BF)
            zas = sb(es, "zas", [128, 4, 256], BF)
            kmx = sb(es, "kmx", [128, 6, NSL * NS + 1], F32)
            csl = sb(es, "csl", [128, 2, 512], F32)
            e96 = cst[0:96, C_E96:C_E96 + 97]
            onesr = sb(es, "onesr", [128, 512], F32)
            k.memset("pool", onesr[:], 1.0, [onesr])
            k.memset("pool", vas[:], 1.0, [vas])
            k.memset("pool", kts[:], 1.0, [kts])
            k.memset("pool", qts[:], 0.0, [qts])
            zi = 0
            hTs = [hT, sb(es, "hT2", [128, 8, 512], BF)]
            csls = [csl, sb(es, "csl2", [128, 2, 512], F32)]
            work = [(s, sl) for s in range(NS) for sl in range(NSL)]

            def front(wi):
                s, sl = work[wi]
                hT, csl = hTs[wi % 2], csls[wi % 2]
                S0 = sl * 512
                SL = slice(S0, S0 + 512)
                k.dma("sp", csl[64:96, 0, :], CSD[s][0, :, SL], [CSD[s]], [csl])
                k.dma("sp", csl[64:96, 1, :], CSD[s][1, :, SL], [CSD[s]], [csl])
                for tt in range(4):
                    tok0 = S0 + tt * 128
                    xt = xb[tt % 2]
                    k.dma("sp", xt[:], Xsrc[s][tok0:tok0 + 128, :], [Xsrc[s]], [xt])
                    if l == 0:
                        k.dma("sp", X[s][tok0:tok0 + 128, :], xt[:], [xt], [X[s]])
                    k.act(junk[:], xt[:], AF.Square, [xt], [junk, st4], accum_out=st4[:, 0:1])
                    k.ts("dve", st4[:, 1:2], st4[:, 0:1], 1.0 / D, 1e-6, ALU.mult, ALU.add, [st4], [st4])
                    k.act(st4[:, 2:3], st4[:, 1:2], AF.Sqrt, [st4], [st4])
                    k.recip(st4[:, 3:4], st4[:, 2:3], [st4], [st4])
                    k.ts("dve", hb[:], xt[:], st4[:, 3:4], None, ALU.mult, None, [xt, st4], [hb])
                    for c in range(8):
                        k.tr(PSB[:, c * 128:(c + 1) * 128], hb[:, c * 128:(c + 1) * 128], identb[:], [hb, identb], [PSB])
                    k.cp("act", hT[:, :, tt * 128:(tt + 1) * 128], PSB[:, :].rearrange("p (c t) -> p c t", c=8), [PSB], [hT])
                k.dma("sp", HT[s][:, :, SL], hT[:], [hT], [HT[s]])
                chk("p1b")

            def back(wi):
                nonlocal zi
                s, sl = work[wi]
                hT, csl = hTs[wi % 2], csls[wi % 2]
                S0 = sl * 512
                SL = slice(S0, S0 + 512)
                R = slice(64, 96)
                for cb in range(11):
                    ps = PS[cb % 3]
                    for c in range(8):
                        k.mm(ps[:], wm[:, c, M_ZR + cb * 128:M_ZR + (cb + 1) * 128], hT[:, c, :], c == 0, c == 7, [wm, hT], [ps])
                    z = zo[zi % 3]
                    zi += 1
                    k.cp("act" if cb % 2 else "dve", z[:], ps[:], [ps], [z])
                    k.dma("sp", ZR[s][cb * 128:(cb + 1) * 128, SL], z[:], [z], [ZR[s]])
                chk("p1c")
                for tt in range(4):
                    ps = PS[3]
                    for c in range(8):
                        k.mm(ps[:, 0:256], hT[:, c, tt * 128:(tt + 1) * 128], wm[:, c, M_ZA:M_ZA + 256], c == 0, c == 7, [wm, hT], [ps])
                    chk("p1c1")
                    k.cp("dve", zas[:, tt, :], ps[:, 0:256], [ps], [zas])
                    chk("p1c3")
                chk("p1c2")
                k.dma("sp", ZA[s][SL, :].rearrange("(a p) c -> p a c", p=128), zas[:], [zas], [ZA[s]])
                chk("p1d")
                for b in range(2):
                    ps = PS[4]
                    for c in range(8):
                        k.mm(ps[:], wm[:, c, M_ZQ + b * 128:M_ZQ + (b + 1) * 128], hT[:, c, :], c == 0, c == 7, [wm, hT], [ps])
                    k.cp("dve", zq[:, b, :], ps[:], [ps], [zq])
                    k.act(sq[:, b, :], zq[:, b, :], AF.Square, [zq], [sq])
                ps = PS[4]
                for b in range(2):
                    k.mm(ps[:], cst[:, C_OA:C_OA + 128], sq[:, b, :], b == 0, b == 1, [cst, sq], [ps])
                k.ts("dve", rb[:], ps[:], 1.0 / 256, 1e-6, ALU.mult, ALU.add, [ps], [rb])
                k.act(rb[:], rb[:], AF.Sqrt, [rb], [rb])
                k.recip(rb[:], rb[:], [rb], [rb])
                for b in range(2):
                    k.tt("dve", cq[:, b, :], zq[:, b, :], rb[:], ALU.mult, [zq, rb], [cq])
                ps = PS[5]
                for c in range(8):
                    k.mm(ps[:], wm[:, c, M_ZKV:M_ZKV + 128], hT[:, c, :], c == 0, c == 7, [wm, hT], [ps])
                k.cp("dve", zkv[:], ps[:], [ps], [zkv])
                k.act(sq[:, 0, :], zkv[:], AF.Square, [zkv], [sq])
                ps = PS[5]
                k.mm(ps[:], cst[:, C_OA:C_OA + 128], sq[:, 0, :], True, True, [cst, sq], [ps])
                k.ts("dve", rb[:], ps[:], 1.0 / 128, 1e-6, ALU.mult, ALU.add, [ps], [rb])
                k.act(rb[:], rb[:], AF.Sqrt, [rb], [rb])
                k.recip(rb[:], rb[:], [rb], [rb])
                k.tt("dve", ckv[:], zkv[:], rb[:], ALU.mult, [zkv, rb], [ckv])
                chk("p1e")
                R = slice(64, 96)
                pa, pb = PS[3], PS[4]
                for c in range(8):
                    k.mm(pa[0:96, :], wm[:, c, M_KR1:M_KR1 + 96], hT[:, c, :], c == 0, c == 7, [wm, hT], [pa])
                for c in range(8):
                    k.mm(pb[0:96, :], wm[:, c, M_KR2:M_KR2 + 96], hT[:, c, :], c == 0, c == 7, [wm, hT], [pb])
                k.tt("dve", t1[R, :], pa[R, :], csl[R, 0, :], ALU.mult, [pa, csl], [t1])
                k.tt("dve", t2[R, :], pb[R, :], csl[R, 1, :], ALU.mult, [pb, csl], [t2])
                k.tt("dve", t1[R, :], t1[R, :], t2[R, :], ALU.add, [t1, t2], [t1])
                for h in range(6):
                    k.cp("pool", kts[R, h, :], t1[R, :], [t1], [kts])
                chk("p1f")
                for h in range(6):
                    pq, pqs, pk = PS[0], PS[1], PS[2]
                    for b in range(2):
                        k.mm(pq[0:96, :], wuq[:, b, h * 96:(h + 1) * 96], cq[:, b, :], b == 0, b == 1, [wuq, cq], [pq])
                    for b in range(2):
                        k.mm(pqs[0:96, :], wuqs[:, b, h * 96:(h + 1) * 96], cq[:, b, :], b == 0, b == 1, [wuqs, cq], [pqs])
                    k.mm(pk[0:64, :], wkk[:, h * 64:(h + 1) * 64], ckv[:], True, True, [wkk, ckv], [pk])
                    k.ts("dve", q32[0:64, :], pq[0:64, :], scale, None, ALU.mult, None, [pq], [q32])
                    k.tt("dve", t1[R, :], pq[R, :], csl[R, 0, :], ALU.mult, [pq, csl], [t1])
                    k.tt("dve", t2[R, :], pqs[R, :], csl[R, 1, :], ALU.mult, [pqs, csl], [t2])
                    k.stt(q32[R, :], t1[R, :], 1.0, t2[R, :], ALU.mult, ALU.add, [t1, t2], [q32])
                    k.ts("dve", q32[R, :], q32[R, :], scale, None, ALU.mult, None, [q32], [q32])
                    k.cp("act", qts[0:96, h, :], q32[0:96, :], [q32], [qts])
                    k.cp("act", kts[0:64, h, :], pk[0:64, :], [pk], [kts])
                    k.tt("pool", t2[0:96, :], qts[0:96, h, :], qts[0:96, h, :], ALU.mult, [qts], [t2])
                    pn = PS[5]
                    k.mm(pn[0:97, :], e96, t2[0:96, :], True, True, [cst, t2], [pn])
                    k.act(t1[96:97, :], pn[96:97, :], AF.Sqrt, [pn], [t1])
                    k.ts("dve", qts[96:97, h, :], t1[96:97, :], -1.0, None, ALU.mult, None, [t1], [qts])
                    k.tt("pool", t2[0:96, :], kts[0:96, h, :], kts[0:96, h, :], ALU.mult, [kts], [t2])
                    pn = PS[6]
                    k.mm(pn[0:97, :], e96, t2[0:96, :], True, True, [cst, t2], [pn])
                    k.op("dve", lambda pn=pn, h=h, s=s, sl=sl: nc.vector.tensor_reduce(
                        out=kmx[96:97, h, s * NSL + sl:s * NSL + sl + 1], in_=pn[96:97, :], axis=AX.X, op=ALU.max), [pn], [kmx])
                chk("p1g")
                k.dma("sp", QT[s][:, :, SL].rearrange("h p t -> p h t"), qts[0:97, :, :], [qts], [QT[s]])
                k.dma("sp", KT[s][:, 0:96, SL].rearrange("h p t -> p h t"), kts[0:96, :, :], [kts], [KT[s]])
                chk("p1h")
                for tt in range(4):
                    ps = PS[3]
                    k.mm(ps[:, 0:384], ckv[:, tt * 128:(tt + 1) * 128], wkv[:], True, True, [ckv, wkv], [ps])
                    k.cp("act", vas[:, tt, :, 0:64], ps[:, 0:384].rearrange("p (h v) -> p h v", h=6), [ps], [vas])
                k.dma("sp", VA[s][SL, :].rearrange("(a p) c -> p a c", p=128), vas[:].rearrange("p a h v -> p a (h v)"), [vas], [VA[s]])

            def kbound(s):
                for h in range(6):
                    k.op("dve", lambda h=h, s=s: nc.vector.tensor_reduce(
                        out=kmx[96:97, h, NSL * NS:NSL * NS + 1], in_=kmx[96:97, h, s * NSL:(s + 1) * NSL], axis=AX.X, op=ALU.max), [kmx], [kmx])
                    k.act(kmx[96:97, h, NSL * NS:NSL * NS + 1], kmx[96:97, h, NSL * NS:NSL * NS + 1], AF.Sqrt, [kmx], [kmx])
                    k.ts("dve", kts[96:97, h, :], onesr[96:97, :], kmx[96:97, h, NSL * NS:NSL * NS + 1], None, ALU.mult, None, [kmx, onesr], [kts])
                for sl in range(NSL):
                    k.dma("sp", KT[s][:, 96:97, sl * 512:(sl + 1) * 512].rearrange("h p t -> p h t"), kts[96:97, :, :], [kts], [KT[s]])

            front(0)
            for wi in range(len(work)):
                if wi + 1 < len(work):
                    front(wi + 1)
                back(wi)
                if work[wi][1] == NSL - 1:
                    kbound(work[wi][0])
            k.barrier()

        if STOP == "p1":
            return nc, k
        for s in range(NS):
            fourier_phase(nc, k, sb, l, s, T, cst, PS, ZA, YA, dftc_in, dfts_in, fw_in)
            if STOP == "fourier":
                return nc, k
            mla_phase(nc, k, sb, l, s, T, cst, PS, QT, KT, VA, YC)
            if STOP == "mla":
                return nc, k
            rwkv_phase(nc, k, sb, l, s, T, cst, PS, PSB, identb, pp, ppx, ZR, YB, w2_in, a2_in)
            if STOP == "rwkv":
                return nc, k

        out_phase(nc, k, sb, l, NS, T, cst, PS, PSB, identb, pp, HT, YA, YB, YC, X, out_d, wgate_in, wproj_in, wout_in, fg_in, last)

    k.barrier()
    return nc, k


def fourier_phase(nc, k, sb, l, s, T, cst, PS, ZA, YA, dftc_in, dfts_in, fw_in):
    NT = T // 128
    NB = T // 512
    with contextlib.ExitStack() as es:
        za = sb(es, "f_za", [128, NT, 256], BF)
        fw = sb(es, "f_fw", [64, 256], F32)
        wc = sb(es, "f_wc", [128, 2, 256], BF)
        ws = sb(es, "f_ws", [128, 2, 256], BF)
        mats = [sb(es, "f_m%d" % i, [128, NT, 512], BF) for i in range(2)]
        a1 = sb(es, "f_a1", [128, 2, 512], BF)
        a2 = sb(es, "f_a2", [128, 2, 512], BF)
        yo = sb(es, "f_yo", [128, 2, 512], BF)
        k.dma("sp", za[:], ZA[s][:, :].rearrange("(a p) c -> p a c", p=128), [ZA[s]], [za])
        k.dma("sp", fw[:], fw_in[l], [fw_in], [fw])
        k.memset("pool", wc[:], 0.0, [wc])
        k.memset("pool", ws[:], 0.0, [ws])
        nrm = float(1.0 / np.sqrt(T * 64.0))
        pc, psn = PS[0], PS[1]
        k.mm(pc[0:64, 0:256], cst[0:64, C_C64:C_C64 + 64], fw[:], True, True, [cst, fw], [pc])
        k.mm(psn[0:64, 0:256], cst[0:64, C_S64:C_S64 + 64], fw[:], True, True, [cst, fw], [psn])
        for g in range(4):
            rows = slice((g % 2) * 64, (g % 2) * 64 + 64)
            cols = slice(g * 64, (g + 1) * 64)
            k.ts("dve", wc[rows, g // 2, cols], pc[0:64, cols], nrm, None, ALU.mult, None, [pc], [wc])
            k.ts("dve", ws[rows, g // 2, cols], psn[0:64, cols], -nrm, None, ALU.mult, None, [psn], [ws])
        for tb in range(NB):
            TB = slice(tb * 512, (tb + 1) * 512)
            for mi, src in enumerate((dftc_in, dfts_in)):
                m = mats[mi]
                for half in range(2):
                    hs = slice(half * (NT // 2), (half + 1) * (NT // 2)) if NT >= 2 else slice(0, NT)
                    if NT < 2 and half == 1:
                        continue
                    k.dma("sp", m[:, hs, :], src[:, TB].rearrange("(a p) n -> p a n", p=128)[:, hs, :], [src], [m])
                dst = a1 if mi == 0 else a2
                for cb in range(2):
                    ps = PS[2 + cb]
                    for c in range(NT):
                        k.mm(ps[:], za[:, c, cb * 128:(cb + 1) * 128], m[:, c, :], c == 0, c == NT - 1, [za, m], [ps])
                    k.cp("act" if cb else "dve", dst[:, cb, :], ps[:], [ps], [dst])
            for eb in range(2):
                ps = PS[4 + eb]
                i = 0
                for (w_, a_) in ((wc, a1), (ws, a2)):
                    for cb in range(2):
                        k.mm(ps[:], w_[:, cb, eb * 128:(eb + 1) * 128], a_[:, cb, :], i == 0, i == 3, [w_, a_], [ps])
                        i += 1
                k.cp("act", yo[:, eb, :], ps[:], [ps], [yo])
            k.dma("sp", YA[s][:, TB].rearrange("(e p) t -> p e t", p=128), yo[:], [yo], [YA[s]])
        k.barrier()


def mla_phase(nc, k, sb, l, s, T, cst, PS, QT, KT, VA, YC):
    NT = T // 128
    NB = T // 512
    with contextlib.ExitStack() as es:
        qt = [sb(es, "m_qt%d" % i, [128, T], BF) for i in range(2)]
        kt = [sb(es, "m_kt%d" % i, [128, T], BF) for i in range(2)]
        va = [sb(es, "m_va%d" % i, [128, NT, 65], BF) for i in range(2)]
        pt = [sb(es, "m_pt%d" % i, [128, 512], BF) for i in range(4)]
        osb = sb(es, "m_o", [128, 512], F32)
        rc = sb(es, "m_rc", [128, 512], F32)
        yo = [sb(es, "m_yo%d" % i, [64, 512], BF) for i in range(2)]
        pi = 0
        def ld(h):
            k.dma("sp", qt[h % 2][0:97, :], QT[s][h], [QT[s]], [qt[h % 2]])
            k.dma("sp", kt[h % 2][0:97, :], KT[s][h], [KT[s]], [kt[h % 2]])
            k.dma("sp", va[h % 2][:], VA[s][:, h * 65:(h + 1) * 65].rearrange("(a p) c -> p a c", p=128), [VA[s]], [va[h % 2]])

        ld(0)
        pending = []
        for h in range(6):
            q_, k_, v_ = qt[h % 2], kt[h % 2], va[h % 2]
            if h + 1 < 6:
                ld(h + 1)
            for qb in range(NB):
                QB = slice(qb * 512, (qb + 1) * 512)
                po = PS[4 + qb % 2]
                pq_ = {}
                for kc in range(NT + 2):
                    if kc == min(6, NT) and pending:
                        pending.pop(0)()
                    if kc < NT:
                        ps = PS[kc % 4]
                        k.mm(ps[:], k_[0:97, kc * 128:(kc + 1) * 128], q_[0:97, QB], True, True, [k_, q_], [ps])
                        p_ = pt[pi % 4]
                        pi += 1
                        k.act(p_[:], ps[:], AF.Exp, [ps], [p_])
                        pq_[kc] = p_
                    if kc >= 2:
                        j = kc - 2
                        k.mm(po[0:65, :], v_[:, j, :], pq_[j][:], j == 0, j == NT - 1, [v_, pq_[j]], [po])
                def epi(po=po, h=h, qb=qb, QB=QB):
                    k.cp("dve", osb[0:65, :], po[0:65, :], [po], [osb])
                    k.recip(rc[64:65, :], osb[64:65, :], [osb], [rc])
                    pbc = PS[6]
                    k.mm(pbc[0:64, :], cst[64:65, C_OA:C_OA + 64], rc[64:65, :], True, True, [cst, rc], [pbc])
                    y_ = yo[(h * NB + qb) % 2]
                    k.tt("dve", y_[:], osb[0:64, :], pbc[0:64, :], ALU.mult, [osb, pbc], [y_])
                    k.dma("sp", YC[s][h * 64:(h + 1) * 64, QB], y_[:], [y_], [YC[s]])
                pending.append(epi)
        while pending:
            pending.pop(0)()
        k.barrier()


def rwkv_phase(nc, k, sb, l, s, T, cst, PS, PSB, identb, pp, ppx, ZR, YB, w2_in, a2_in):
    NSL = T // 512
    ident = cst[:, C_ID:C_ID + 128]
    with contextlib.ExitStack() as es:
        w2 = sb(es, "r_w2", [128, 384], F32)
        a2w = sb(es, "r_a2", [128, 384], F32)
        k.dma("sp", w2[:], w2_in[l], [w2_in], [w2])
        k.dma("sp", a2w[:], a2_in[l], [a2_in], [a2w])
        zr = sb(es, "r_zr", [128, 11, 514], F32)
        zs = sb(es, "r_zs", [128, 11, 512], F32)
        th = sb(es, "r_th", [128, 512], F32)
        lw = sb(es, "r_lw", [128, 3, 512], F32)
        ic = sb(es, "r_ic", [128, 3, 512], F32)
        kk = sb(es, "r_kk", [128, 3, 512], F32)
        tA = sb(es, "r_tA", [128, 3, 512], F32)
        tB = sb(es, "r_tB", [128, 3, 512], F32)
        cum = sb(es, "r_cum", [128, 3, 512], F32)
        epos = sb(es, "r_ep", [128, 3, 512], F32)
        eneg = sb(es, "r_en", [128, 3, 512], F32)
        AR = sb(es, "r_AR", [128, 3, 2, 512], RDT)
        BK = sb(es, "r_BK", [128, 3, 2, 512], RDT)
        VV = sb(es, "r_VV", [128, 3, 512], RDT)
        Y = sb(es, "r_Y", [128, 3, 512], F32)
        Y0 = sb(es, "r_Y0", [128, 3, 512], F32)
        ST = sb(es, "r_ST", [128, 3, 64], RDT)
        TOKB = [[sb(es, "r_TOK%d_%d" % (i, b), [128, 384], RDT) for b in range(3)] for i in range(3)]
        NU = 6
        NAr = [[sb(es, "r_NAr%d_%d" % (j, i), [128, 256], RDT) for i in range(NU)] for j in range(2)]
        KA = [[sb(es, "r_KA%d_%d" % (j, i), [128, 256], RDT) for i in range(NU)] for j in range(2)]
        NN = [[[sb(es, "r_NN%d_%d_%d" % (q, i, j), [128, 256], RDT) for j in range(2)] for i in range(NU)] for q in range(2)]
        TM = [[[sb(es, "r_TM%d_%d_%d" % (q, i, j), [128, 128], RDT) for j in range(2)] for i in range(NU)] for q in range(2)]
        X0 = [sb(es, "r_X0%d" % i, [128, 64], RDT) for i in range(NU)]
        UT = [sb(es, "r_UT%d" % i, [128, 64], RDT) for i in range(NU)]
        ST32 = sb(es, "r_ST32", [128, 3, 64], F32)
        if RDT == BF:
            ptile, idn, idt = PSB, identb[:], identb
        else:
            ptile, idn, idt = PS[6], ident, cst
        for d in range(2):
            k.memset("pool", ST[:], 0.0, [ST])
            k.memset("pool", ST32[:], 0.0, [ST32])
            slabs = list(range(NSL)) if d == 0 else list(range(NSL - 1, -1, -1))

            def load_zr(sl_):
                S0_ = sl_ * 512
                lo = 1 if sl_ == 0 else 0
                hi = 1 if sl_ == NSL - 1 else 0
                if lo:
                    k.memset("pool", zr[:, :, 0:1], 0.0, [zr])
                if hi:
                    k.memset("pool", zr[:, :, 513:514], 0.0, [zr])
                k.dma("sp", zr[:, :, lo:514 - hi], ZR[s][:, S0_ - 1 + lo:S0_ + 513 - hi].rearrange("(b p) t -> p b t", p=128), [ZR[s]], [zr])
            m2 = cst[:, C_SF:C_SF + 256] if d == 0 else cst[:, C_SB:C_SB + 256]
            mT = cst[:, C_SB:C_SB + 128] if d == 0 else cst[:, C_SF:C_SF + 128]
            for sl in slabs:
                S0 = sl * 512
                if sl == slabs[0]:
                    load_zr(sl)
                for b in range(11):
                    k.act(zs[:, b, :], zr[:, b, 1:513], AF.Identity, [zr, ppx], [zs], scale=ppx[:, b:b + 1])
                    k.stt(zs[:, b, :], zr[:, b, 0:512], pp[:, 8 + b:9 + b], zs[:, b, :], ALU.mult, ALU.add, [zr, pp, zs], [zs])
                    k.stt(zs[:, b, :], zr[:, b, 2:514], pp[:, 19 + b:20 + b], zs[:, b, :], ALU.mult, ALU.add, [zr, pp, zs], [zs])
                si_ = slabs.index(sl)
                if si_ + 1 < len(slabs):
                    load_zr(slabs[si_ + 1])
                DR = slice(d * 64, d * 64 + 64)
                B3 = range(3)
                k.act(th[DR, :], zs[DR, 9, :], AF.Tanh, [zs], [th])
                for b in B3:
                    k.mm(PS[b][:], w2[DR, b * 128:(b + 1) * 128], th[DR, :], True, True, [w2, th], [PS[b]])
                for b in B3:
                    k.mm(PS[3 + b][:], a2w[DR, b * 128:(b + 1) * 128], zs[DR, 10, :], True, True, [a2w, zs], [PS[3 + b]])
                for b in B3:
                    k.ts("dve", kk[:, b, :], zs[:, 3 + b, :], pp[:, 42 + b:43 + b], None, ALU.mult, None, [zs, pp], [kk])
                for b in B3:
                    k.act(lw[:, b, :], PS[b][:], AF.Sigmoid, [PS[b], pp], [lw], bias=pp[:, 30 + d * 3 + b:31 + d * 3 + b])
                for b in B3:
                    k.act(ic[:, b, :], PS[3 + b][:], AF.Sigmoid, [PS[3 + b], pp], [ic], bias=pp[:, 36 + d * 3 + b:37 + d * 3 + b])
                for b in B3:
                    k.tt("pool", tA[:, b, :], kk[:, b, :], kk[:, b, :], ALU.mult, [kk], [tA])
                for b in B3:
                    k.mm(PS[b][:], cst[:, C_OB:C_OB + 128], tA[:, b, :], True, True, [cst, tA], [PS[b]])
                for b in B3:
                    k.ts("dve", lw[:, b, :], lw[:, b, :], -float(np.exp(-0.5)), None, ALU.mult, None, [lw], [lw])
                for b in B3:
                    k.op("dve", lambda b=b: nc.vector.tensor_tensor_scan(
                        out=cum[:, b, :], data0=cst[:, C_SEG:C_SEG + 512], data1=lw[:, b, :], initial=0.0,
                        op0=ALU.mult, op1=ALU.add), [cst, lw], [cum])
                for b in B3:
                    k.act(tB[:, b, :], PS[b][:], AF.Sqrt, [PS[b]], [tB])
                if d == 1:
                    for b in B3:
                        for c in range(4):
                            cs_ = slice(c * 128, (c + 1) * 128)
                            k.stt(tA[:, b, cs_], cum[:, b, cs_], cum[:, b, c * 128 + 127:c * 128 + 128], lw[:, b, cs_],
                                  ALU.subtract, ALU.subtract, [cum, lw], [tA])
                    for b in B3:
                        k.ts("dve", cum[:, b, :], tA[:, b, :], -1.0, None, ALU.mult, None, [tA], [cum])
                for b in B3:
                    k.ts("dve", tB[:, b, :], tB[:, b, :], 1e-12, None, ALU.max, None, [tB], [tB])
                for b in B3:
                    k.act(epos[:, b, :], cum[:, b, :], AF.Exp, [cum], [epos])
                for b in B3:
                    k.act(eneg[:, b, :], cum[:, b, :], AF.Exp, [cum], [eneg], scale=-1.0)
                for b in B3:
                    k.tt("pool", tA[:, b, :], cum[:, b, :], lw[:, b, :], ALU.subtract, [cum, lw], [tA])
                for b in B3:
                    k.recip(tB[:, b, :], tB[:, b, :], [tB], [tB])
                for b in B3:
                    k.act(tA[:, b, :], tA[:, b, :], AF.Exp, [tA], [tA])
                for b in B3:
                    k.tt("dve", kk[:, b, :], kk[:, b, :], tB[:, b, :], ALU.mult, [kk, tB], [kk])
                for b in B3:
                    k.tt("pool", AR[:, b, 1, :], zs[:, b, :], epos[:, b, :], ALU.mult, [zs, epos], [AR])
                for b in B3:
                    k.stt(AR[:, b, 0, :], kk[:, b, :], -1.0, tA[:, b, :], ALU.mult, ALU.mult, [kk, tA], [AR])
                for b in B3:
                    k.tt("dve", tB[:, b, :], kk[:, b, :], ic[:, b, :], ALU.mult, [kk, ic], [tB])
                for b in B3:
                    k.cp("pool", VV[:, b, :], zs[:, 6 + b, :], [zs], [VV])
                for b in B3:
                    k.tt("dve", BK[:, b, 0, :], tB[:, b, :], eneg[:, b, :], ALU.mult, [tB, eneg], [BK])
                for b in B3:
                    k.ts("dve", tB[:, b, :], ic[:, b, :], pp[:, 45 + b:46 + b], ppx[:, 11 + b:12 + b], ALU.mult, ALU.add, [ic, pp, ppx], [tB])
                for b in B3:
                    k.tt("pool", tB[:, b, :], tB[:, b, :], zs[:, 3 + b, :], ALU.mult, [tB, zs], [tB])
                for b in B3:
                    k.tt("dve", BK[:, b, 1, :], tB[:, b, :], eneg[:, b, :], ALU.mult, [tB, eneg], [BK])
                if d == 1:
                    k.dma("sp", Y0[:], YB[s][:, S0:S0 + 512].rearrange("(b p) t -> p b t", p=128), [YB[s]], [Y0])
                chunks = range(4) if d == 0 else range(3, -1, -1)
                TMC = {}

                def stage12(c):
                        cs_ = slice(c * 128, (c + 1) * 128)
                        gcol = c * 128 + 127 if d == 0 else c * 128
                        tkl = TOKB[c % 3]
                        HS = range(6)
                        HR = [slice((h % 2) * 64, (h % 2) * 64 + 64) for h in HS]
                        HB = [h // 2 for h in HS]
                        cs_ = slice(c * 128, (c + 1) * 128)
                        gcol = c * 128 + 127 if d == 0 else c * 128
                        tkl = TOKB[c % 3]
                        yield
                        for b in range(3):
                            k.tr(ptile[:, 0:128], BK[:, b, 0, cs_], idn, [BK, idt], [ptile])
                            k.tr(ptile[:, 128:256], BK[:, b, 1, cs_], idn, [BK, idt], [ptile])
                            k.tr(ptile[:, 256:384], VV[:, b, cs_], idn, [VV, idt], [ptile])
                            k.cp("act", tkl[b][:], ptile[:, 0:384], [ptile], [tkl[b]])
                        HS = range(6)
                        HR = [slice((h % 2) * 64, (h % 2) * 64 + 64) for h in HS]
                        HB = [h // 2 for h in HS]
                        yield
                        for h in HS:
                            b, bank = HB[h], PS[h]
                            k.mm(bank[:, 0:256], BK[HR[h], b, 0, cs_], AR[HR[h], b, :, cs_], True, True, [BK, AR], [bank])
                            k.mm(bank[:, 256:512], BK[HR[h], b, 1, cs_], AR[HR[h], b, :, cs_], True, True, [BK, AR], [bank])
                        yield
                        for h in HS:
                            bank = PS[h]
                            k.tt("dve", NAr[c % 2][h][:], bank[:, 0:256], m2, ALU.mult, [bank, cst], [NAr[c % 2][h]])
                            k.tt("dve", KA[c % 2][h][:], bank[:, 256:512], m2, ALU.mult, [bank, cst], [KA[c % 2][h]])
                        yield
                        for h in HS:
                            b, bank = HB[h], PS[h]
                            k.mm(bank[:, 0:128], AR[HR[h], b, 0, cs_], BK[HR[h], b, 0, cs_], True, True, [BK, AR], [bank])
                        cur = {}
                        tmc = {}
                        TMC[c] = tmc
                        yield
                        for h in HS:
                            bank = PS[h]
                            nn0 = NN[c % 2][h][0]
                            k.tt("dve", nn0[:, 128:256], bank[:, 0:128], mT, ALU.mult, [bank, cst], [nn0])
                            k.cp("pool", nn0[:, 0:128], NAr[c % 2][h][:, 0:128], [NAr[c % 2][h]], [nn0])
                            k.tt("pool", TM[c % 2][h][0][:], NAr[c % 2][h][:, 0:128], ident, ALU.add, [NAr[c % 2][h], cst], [TM[c % 2][h][0]])
                            cur[h] = nn0
                            tmc[h] = TM[c % 2][h][0]
                        yield
                        for lev in range(6):
                            yield
                            for h in HS:
                                bank = PS[h]
                                if lev < 5:
                                    k.mm(bank[:, 0:128], cur[h][:, 128:256], cur[h][:, 0:128], True, True, [cur[h]], [bank])
                                k.mm(bank[:, 128:256], cur[h][:, 0:128], cur[h][:, 128:256], True, True, [cur[h]], [bank])
                            yield
                            for h in HS:
                                nxt = NN[c % 2][h][(lev + 1) % 2]
                                if lev < 5:
                                    k.cp("act", nxt[:], PS[h][:, 0:256], [PS[h]], [nxt])
                                else:
                                    k.cp("act", nxt[:, 128:256], PS[h][:, 128:256], [PS[h]], [nxt])
                                cur[h] = nxt
                            yield
                            for h in HS:
                                k.mm(PS[h][:, 256:384], cur[h][:, 128:256], tmc[h][:], True, True, [cur[h], tmc[h]], [PS[h]])
                            yield
                            for h in HS:
                                tm2 = TM[c % 2][h][(lev + 1) % 2]
                                k.tt("dve", tm2[:], PS[h][:, 256:384], tmc[h][:], ALU.add, [PS[h], tmc[h]], [tm2])
                                tmc[h] = tm2

                        yield

                def stage3(c):
                    cs_ = slice(c * 128, (c + 1) * 128)
                    gcol = c * 128 + 127 if d == 0 else c * 128
                    tkl = TOKB[c % 3]
                    HS = range(6)
                    HR = [slice((h % 2) * 64, (h % 2) * 64 + 64) for h in HS]
                    HB = [h // 2 for h in HS]
                    bank = PS[6]
                    XR = [slice(h * 64, (h + 1) * 64) for h in HS]
                    tv = [tkl[HB[h]][:, 256 + (h % 2) * 64:256 + (h % 2) * 64 + 64] for h in HS]
                    tb_ = [tkl[HB[h]][:, (h % 2) * 64:(h % 2) * 64 + 64] for h in HS]
                    tk_ = [tkl[HB[h]][:, 128 + (h % 2) * 64:128 + (h % 2) * 64 + 64] for h in HS]
                    yield
                    for h in HS:
                        b = HB[h]
                        k.mm(bank[:, XR[h]], AR[HR[h], b, 0, cs_], ST[HR[h], b, :], True, False, [AR, ST], [bank])
                        k.mm(bank[:, XR[h]], KA[c % 2][h][:, 0:128], tv[h], False, True, [KA[c % 2][h], tkl[b]], [bank])
                    yield
                    for h in HS:
                        k.cp("act", X0[h][:], bank[:, XR[h]], [bank], [X0[h]])
                    yield
                    for h in HS:
                        k.mm(bank[:, XR[h]], TMC[c][h][:], X0[h][:], True, True, [TMC[c][h], X0[h]], [bank])
                    yield
                    for h in HS:
                        k.cp("act", UT[h][:], bank[:, XR[h]], [bank], [UT[h]])
                    for pr in range(3):
                        yield
                        for h in (2 * pr, 2 * pr + 1):
                            b = HB[h]
                            o = (h % 2) * 192
                            k.mm(bank[0:64, o:o + 128], ST[HR[h], b, :], AR[HR[h], b, 1, cs_], True, False, [ST, AR], [bank])
                            k.mm(bank[0:64, o:o + 128], UT[h][:], NAr[c % 2][h][:, 128:256], False, False, [UT[h], NAr[c % 2][h]], [bank])
                            k.mm(bank[0:64, o:o + 128], tv[h], KA[c % 2][h][:, 128:256], False, True, [tkl[b], KA[c % 2][h]], [bank])
                            k.mm(bank[0:64, o + 128:o + 192], tb_[h], UT[h][:], True, False, [tkl[b], UT[h]], [bank])
                            k.mm(bank[0:64, o + 128:o + 192], tk_[h], tv[h], False, True, [tkl[b]], [bank])
                        yield
                        for h in (2 * pr, 2 * pr + 1):
                            b = HB[h]
                            o = (h % 2) * 192
                            if d == 0:
                                k.cp("act", Y[HR[h], b, cs_], bank[0:64, o:o + 128], [bank], [Y])
                            else:
                                k.tt("dve", Y[HR[h], b, cs_], bank[0:64, o:o + 128], Y0[HR[h], b, cs_], ALU.add, [bank, Y0], [Y])
                            k.tt("dve", ST32[HR[h], b, :], bank[0:64, o + 128:o + 192], ST32[HR[h], b, :], ALU.add, [bank, ST32], [ST32])
                    yield
                    for b in range(3):
                        k.ts("pool", ST32[:, b, :], ST32[:, b, :], epos[:, b, gcol:gcol + 1], None, ALU.mult, None, [ST32, epos], [ST32])
                    k.cp("act", ST[:], ST32[:], [ST32], [ST])
                    yield

                clist = list(chunks)
                for _ in stage12(clist[0]):
                    pass
                for ci, c in enumerate(clist):
                    g3 = stage3(c)
                    g12 = stage12(clist[ci + 1]) if ci + 1 < len(clist) else iter(())
                    done12 = done3 = False
                    while not (done12 and done3):
                        if not done12:
                            try:
                                next(g12)
                            except StopIteration:
                                done12 = True
                        if not done3:
                            try:
                                next(g3)
                            except StopIteration:
                                done3 = True
                if d == 1:
                    OB = cst[:, C_OB:C_OB + 128]
                    B3 = range(3)
                    for b in B3:
                        k.mm(PS[b][:], a2w[0:64, b * 128:(b + 1) * 128], zs[0:64, 10, :], True, True, [a2w, zs], [PS[b]])
                    for b in B3:
                        k.mm(PS[3 + b][:], OB, Y[:, b, :], True, True, [cst, Y], [PS[3 + b]])
                    for b in B3:
                        k.act(tA[:, b, :], PS[b][:], AF.Sigmoid, [PS[b], pp], [tA], bias=pp[:, 36 + b:37 + b])
                    for b in B3:
                        k.stt(cum[:, b, :], PS[3 + b][:], -1.0 / 64, Y[:, b, :], ALU.mult, ALU.add, [PS[3 + b], Y], [cum])
                    for b in B3:
                        k.tt("pool", eneg[:, b, :], cum[:, b, :], cum[:, b, :], ALU.mult, [cum], [eneg])
                    for b in B3:
                        k.mm(PS[3 + b][:], OB, eneg[:, b, :], True, True, [cst, eneg], [PS[3 + b]])
                    for b in B3:
                        k.tt("dve", tA[:, b, :], tA[:, b, :], ic[:, b, :], ALU.add, [tA, ic], [tA])
                    for b in B3:
                        k.ts("dve", tA[:, b, :], tA[:, b, :], pp[:, 45 + b:46 + b], ppx[:, 14 + b:15 + b], ALU.mult, ALU.add, [tA, pp, ppx], [tA])
                    for b in B3:
                        k.ts("dve", epos[:, b, :], PS[3 + b][:], 1.0 / 64, 64e-5, ALU.mult, ALU.add, [PS[3 + b]], [epos])
                    for b in B3:
                        k.act(epos[:, b, :], epos[:, b, :], AF.Sqrt, [epos], [epos])
                    for b in B3:
                        k.tt("dve", tA[:, b, :], tA[:, b, :], zs[:, 3 + b, :], ALU.mult, [tA, zs], [tA])
                    for b in B3:
                        k.stt(tA[:, b, :], zs[:, b, :], pp[:, 48 + b:49 + b], tA[:, b, :], ALU.mult, ALU.mult, [zs, pp, tA], [tA])
                    for b in B3:
                        k.mm(PS[b][:], OB, tA[:, b, :], True, True, [cst, tA], [PS[b]])
                    for b in B3:
                        k.recip(epos[:, b, :], epos[:, b, :], [epos], [epos])
                    for b in B3:
                        k.tt("dve", tB[:, b, :], PS[b][:], zs[:, 6 + b, :], ALU.mult, [PS[b], zs], [tB])
                    for b in B3:
                        k.tt("dve", cum[:, b, :], cum[:, b, :], epos[:, b, :], ALU.mult, [cum, epos], [cum])
                    for b in B3:
                        k.ts("dve", cum[:, b, :], cum[:, b, :], pp[:, 51 + b:52 + b], pp[:, 54 + b:55 + b], ALU.mult, ALU.add, [cum, pp], [cum])
                    for b in B3:
                        k.tt("pool", Y[:, b, :], cum[:, b, :], tB[:, b, :], ALU.add, [cum, tB], [Y])
                k.dma("sp", YB[s][:, S0:S0 + 512].rearrange("(b p) t -> p b t", p=128), Y[:], [Y], [YB[s]])
        k.barrier()


def out_phase(nc, k, sb, l, NS, T, cst, PS, PSB, identb, pp, HT, YA, YB, YC, X, out_d, wgate_in, wproj_in, wout_in, fg_in, last):
    NSL = T // 512
    with contextlib.ExitStack() as es:
        wg = sb(es, "o_wg", [128, 8, 4096], BF)
        wp = sb(es, "o_wp", [128, 8, D], BF)
        wo = sb(es, "o_wo", [128, 8, D], BF)
        stg = [sb(es, "o_stg%d" % i, [128, 2048], F32) for i in range(2)]
        si = 0
        for c in range(8):
            for hf in range(2):
                st = stg[si % 2]
                si += 1
                k.dma("sp", st[:], wgate_in[l, c * 128:(c + 1) * 128, hf * 2048:(hf + 1) * 2048], [wgate_in], [st])
                k.ts("pool" if si % 2 else "dve", wg[:, c, hf * 2048:(hf + 1) * 2048], st[:], pp[:, c:c + 1], None, ALU.mult, None, [st, pp], [wg])
        for c in range(8):
            st = stg[si % 2]
            si += 1
            k.dma("sp", st[:, 0:D], wproj_in[l, c * 128:(c + 1) * 128, :], [wproj_in], [st])
            k.dma("sp", st[:, D:2 * D], wout_in[l, c * 128:(c + 1) * 128, :], [wout_in], [st])
            k.cp("dve", wp[:, c, :], st[:, 0:D], [st], [wp])
            k.cp("pool", wo[:, c, :], st[:, D:2 * D], [st], [wo])
        hTs = [sb(es, "o_hT%d" % i, [128, 8, 512], BF) for i in range(2)]
        yas = [sb(es, "o_ya%d" % i, [128, 2, 512], BF) for i in range(2)]
        ybs = [sb(es, "o_yb%d" % i, [128, 3, 512], F32) for i in range(2)]
        ycs = [sb(es, "o_yc%d" % i, [128, 3, 512], BF) for i in range(2)]
        gs = [sb(es, "o_gs%d" % i, [128, 512], F32) for i in range(3)]
        yg = sb(es, "o_yg", [128, 8, 512], BF)
        mg = sb(es, "o_mg", [128, 8, 512], BF)
        tm_ = sb(es, "o_tm", [128, 512], F32)
        tm2 = sb(es, "o_tm2", [128, 512], F32)
        xb = [sb(es, "o_xb%d" % i, [128, D], F32) for i in range(2)]
        xn = [sb(es, "o_xn%d" % i, [128, D], F32) for i in range(2)]
        junk = sb(es, "o_junk", [128, D], F32)
        st4 = sb(es, "o_st4", [128, 8], F32)
        fgb = sb(es, "o_fgb", [128, D], F32)
        if last:
            for p in range(128):
                k.dma("sp", fgb[p:p + 1, :], fg_in[:, :], [fg_in], [fgb])
        gi = 0
        xi = 0
        work = [(s, sl) for s in range(NS) for sl in range(NSL)]

        def ld(wi):
            s_, sl_ = work[wi]
            j = wi % 2
            SL_ = slice(sl_ * 512, (sl_ + 1) * 512)
            k.dma("sp", hTs[j][:], HT[s_][:, :, SL_], [HT[s_]], [hTs[j]])
            k.dma("sp", yas[j][:], YA[s_][:, SL_].rearrange("(b p) t -> p b t", p=128), [YA[s_]], [yas[j]])
            k.dma("sp", ybs[j][:], YB[s_][:, SL_].rearrange("(b p) t -> p b t", p=128), [YB[s_]], [ybs[j]])
            k.dma("sp", ycs[j][:], YC[s_][:, SL_].rearrange("(b p) t -> p b t", p=128), [YC[s_]], [ycs[j]])

        ld(0)
        for wi, (s, sl) in enumerate(work):
            if True:
                SL = slice(sl * 512, (sl + 1) * 512)
                hT, ya, yb, yc = hTs[wi % 2], yas[wi % 2], ybs[wi % 2], ycs[wi % 2]
                if wi + 1 < len(work):
                    ld(wi + 1)
                for gb in range(8):
                    ps = PS[gb % 2]
                    for c in range(8):
                        k.mm(ps[:], wg[:, c, gb * 128:(gb + 1) * 128], hT[:, c, :], c == 0, c == 7, [wg, hT], [ps])
                    g_ = gs[gi % 3]
                    gi += 1
                    k.act(g_[:], ps[:], AF.Sigmoid, [ps], [g_])
                    k.tt("dve", g_[:], g_[:], ps[:], ALU.mult, [g_, ps], [g_])
                    if gb < 2:
                        src, srct = ya[:, gb, :], ya
                    elif gb < 5:
                        src, srct = yb[:, gb - 2, :], yb
                    else:
                        src, srct = yc[:, gb - 5, :], yc
                    k.tt("dve", yg[:, gb, :], g_[:], src, ALU.mult, [g_, srct], [yg])
                for ob in range(8):
                    OBS = slice(ob * 128, (ob + 1) * 128)
                    pa, pb, pc = PS[2], PS[3], PS[4]
                    for i, cb in enumerate((0, 1)):
                        k.mm(pa[:], wp[:, cb, OBS], yg[:, cb, :], i == 0, i == 1, [wp, yg], [pa])
                    for i, cb in enumerate((2, 3, 4)):
                        k.mm(pb[:], wp[:, cb, OBS], yg[:, cb, :], i == 0, i == 2, [wp, yg], [pb])
                    for i, cb in enumerate((5, 6, 7)):
                        k.mm(pc[:], wp[:, cb, OBS], yg[:, cb, :], i == 0, i == 2, [wp, yg], [pc])
                    sg = []
                    for j in range(3):
                        ps = PS[j % 2]
                        col = 1024 + j * 1024 + ob * 128
                        for c in range(8):
                            k.mm(ps[:], wg[:, c, col:col + 128], hT[:, c, :], c == 0, c == 7, [wg, hT], [ps])
                        g_ = gs[gi % 3]
                        gi += 1
                        k.act(g_[:], ps[:], AF.Sigmoid, [ps], [g_])
                        sg.append(g_)
                    k.tt("dve", tm_[:], pa[:], sg[0][:], ALU.mult, [pa, sg[0]], [tm_])
                    k.tt("dve", tm2[:], pb[:], sg[1][:], ALU.mult, [pb, sg[1]], [tm2])
                    k.tt("pool", tm_[:], tm_[:], tm2[:], ALU.add, [tm_, tm2], [tm_])
                    k.tt("dve", tm2[:], pc[:], sg[2][:], ALU.mult, [pc, sg[2]], [tm2])
                    k.tt("pool", mg[:, ob, :], tm_[:], tm2[:], ALU.add, [tm_, tm2], [mg])
                for tt in range(4):
                    tok0 = sl * 512 + tt * 128
                    xt = xb[xi % 2]
                    xo = xn[xi % 2]
                    xi += 1
                    k.dma("sp", xt[:], X[s][tok0:tok0 + 128, :], [X[s]], [xt])
                    for hf in range(2):
                        ps = PS[5 + hf]
                        for ob in range(8):
                            k.mm(ps[:], mg[:, ob, tt * 128:(tt + 1) * 128], wo[:, ob, hf * 512:(hf + 1) * 512], ob == 0, ob == 7, [mg, wo], [ps])
                        k.tt("dve", xo[:, hf * 512:(hf + 1) * 512], ps[:], xt[:, hf * 512:(hf + 1) * 512], ALU.add, [ps, xt], [xo])
                    if not last:
                        k.dma("sp", X[s][tok0:tok0 + 128, :], xo[:], [xo], [X[s]])
                    else:
                        k.act(junk[:], xo[:], AF.Square, [xo], [junk, st4], accum_out=st4[:, 0:1])
                        k.ts("dve", st4[:, 1:2], st4[:, 0:1], 1.0 / D, 1e-6, ALU.mult, ALU.add, [st4], [st4])
                        k.act(st4[:, 2:3], st4[:, 1:2], AF.Sqrt, [st4], [st4])
                        k.recip(st4[:, 3:4], st4[:, 2:3], [st4], [st4])
                        k.stt(xo[:], xo[:], st4[:, 3:4], fgb[:], ALU.mult, ALU.mult, [xo, st4, fgb], [xo])
                        k.dma("sp", out_d[s, tok0:tok0 + 128, :], xo[:], [xo], [out_d])
        k.barrier()


def _consts():
    c = np.zeros((128, NCST), np.float32)
    c[:, C_ID:C_ID + 128] = np.eye(128)
    j = np.arange(128)[:, None]
    t = np.arange(128)[None, :]
    c[:, C_SF:C_SF + 128] = j < t
    c[:, C_IF:C_IF + 128] = j <= t
    c[:, C_SB:C_SB + 128] = j > t
    c[:, C_IB:C_IB + 128] = j >= t
    c[:, C_OB:C_OB + 128] = (j // 64) == (t // 64)
    c[:, C_OA:C_OA + 128] = 1.0
    seg = np.ones(512, np.float32)
    seg[::128] = 0.0
    c[:, C_SEG:C_SEG + 512] = seg[None, :]
    a = np.arange(64)
    ang = 2 * np.pi * np.outer(a, a) / 64.0
    c[0:64, C_C64:C_C64 + 64] = np.cos(ang)
    c[0:64, C_S64:C_S64 + 64] = np.sin(ang)
    inv = (10000.0 ** (-np.arange(0, 32, 2, dtype=np.float32) / np.float32(32))).astype(np.float32)
    for p in range(64, 96):
        c[p, C_INVF] = inv[(p - 64) % 16]
        c[p, C_SGN] = -1.0 if p < 80 else 1.0
    c[0:96, C_E96 + 96] = 1.0
    return c


def _dft(T):
    n = np.arange(T, dtype=np.int64)
    m = (np.outer(n, n) % T).astype(np.float64) * (2 * np.pi / T)
    return np.cos(m).astype(ml_dtypes.bfloat16), np.sin(m).astype(ml_dtypes.bfloat16)


def host_inputs(inp, T, NL, NS, ncores):
    f = lambda a: np.ascontiguousarray(np.asarray(a), dtype=np.float32)
    w_in = f(inp["w_in"])
    wmix = np.concatenate([
        w_in[:, :, O_ZR:O_ZR + 1408], w_in[:, :, O_ZQ:O_ZQ + 256], w_in[:, :, O_ZKV:O_ZKV + 128],
        w_in[:, :, O_ZKV:O_ZKV + 64], w_in[:, :, O_ZKR:O_ZKR + 32],
        w_in[:, :, O_ZKV:O_ZKV + 64], w_in[:, :, O_ZKR + 16:O_ZKR + 32], w_in[:, :, O_ZKR:O_ZKR + 16],
        w_in[:, :, O_ZA:O_ZA + 256]], axis=2)
    assert wmix.shape[2] == CM
    wgate = np.concatenate([w_in[:, :, O_ZAG:O_ZAG + 256], w_in[:, :, O_ZBG:O_ZBG + 384],
                            w_in[:, :, O_ZCG:O_ZCG + 384], w_in[:, :, O_ZM:O_ZM + 3072]], axis=2)
    wproj = np.concatenate([f(inp["proj_a"]), f(inp["proj_b"]), f(inp["proj_c"])], axis=1)
    pp = np.zeros((NL, 128, NPP), np.float32)

    def blk(v, n):
        return np.transpose(v.reshape(NL, n, 128), (0, 2, 1))

    pp[:, :, 0:8] = blk(f(inp["norm_g"]), 8)
    pp[:, :, 8:19] = blk(f(inp["shift_mu_prev"]), 11)
    pp[:, :, 19:30] = blk(f(inp["shift_mu_next"]), 11)
    pp[:, :, 30:36] = blk(f(inp["decay_w0"]).reshape(NL, 768), 6)
    pp[:, :, 36:42] = blk(f(inp["iclr_a0"]).reshape(NL, 768), 6)
    pp[:, :, 42:45] = blk(f(inp["key_k"]), 3)
    pp[:, :, 45:48] = blk(f(inp["key_a"]), 3)
    pp[:, :, 48:51] = blk(f(inp["bonus_r_k"]).reshape(NL, 384), 3)
    pp[:, :, 51:54] = blk(f(inp["lnx_g"]), 3)
    pp[:, :, 54:57] = blk(f(inp["lnx_b"]), 3)
    pp[:, :, 57:59] = blk(f(inp["q_norm_g"]), 2)
    pp[:, :, 59:60] = blk(f(inp["kv_norm_g"]), 1)
    fw = np.transpose(f(inp["fourier_w"]), (0, 2, 1, 3)).reshape(NL, 64, 256)
    w2 = f(inp["decay_w2"]).reshape(NL, 128, 384)
    a2 = f(inp["iclr_a2"]).reshape(NL, 128, 384)
    wuq = f(inp["w_uq"])
    wq4 = wuq.reshape(NL, 256, 6, 96)
    wuqs = np.concatenate([wq4[..., 0:64], wq4[..., 80:96], wq4[..., 64:80]], axis=-1).reshape(NL, 256, 576)
    wkv4 = f(inp["w_ukv"]).reshape(NL, 128, 6, 128)
    wukvk = np.ascontiguousarray(wkv4[..., 0:64]).reshape(NL, 128, 384)
    wukvv = np.ascontiguousarray(wkv4[..., 64:128]).reshape(NL, 128, 384)
    dc, ds = _dft(T)
    shared = dict(cst=_consts(), dftc=dc, dfts=ds, wmix=np.ascontiguousarray(wmix), wgate=np.ascontiguousarray(wgate),
                  wproj=np.ascontiguousarray(wproj), wout=f(inp["w_out"]), pp=pp, fw=np.ascontiguousarray(fw), w2=w2, a2=a2,
                  wuq=np.ascontiguousarray(wuq), wuqs=np.ascontiguousarray(wuqs), wukvk=wukvk, wukvv=wukvv,
                  fg=f(inp["final_g"]).reshape(1, D))
    x = f(inp["x"])
    pos = np.ascontiguousarray(np.asarray(inp["positions"]), dtype=np.int32)
    maps = []
    for c in range(ncores):
        m = dict(shared)
        m["x"] = np.ascontiguousarray(x[c * NS:(c + 1) * NS])
        m["pos"] = np.ascontiguousarray(pos[c * NS:(c + 1) * NS]).reshape(NS, 1, T)
        maps.append(m)
    return maps


def kernel(**inputs):
    x = np.asarray(inputs["x"])
    B, T, _ = x.shape
    NL = np.asarray(inputs["w_in"]).shape[0]
    NS = B // NCORES
    nc, _ = build(T, NL, NS)
    maps = host_inputs(inputs, T, NL, NS, NCORES)
    res = run_bass_kernel_spmd(nc, maps, core_ids=list(range(NCORES)))
    return np.concatenate([np.asarray(r["out"], dtype=np.float32) for r in res.results], axis=0)
```

```python
import contextlib
import numpy as np
import ml_dtypes
import concourse.bass as bass
import concourse.mybir as mybir
from concourse.bass_utils import run_bass_kernel_spmd

F32 = mybir.dt.float32
BF = mybir.dt.bfloat16
I32 = mybir.dt.int32
AF = mybir.ActivationFunctionType
ALU = mybir.AluOpType
AX = mybir.AxisListType

D = 1024
NCORES = 8
CH = 128
RDT = BF
STOP = None


class StopBuild(Exception):
    pass


def chk(tag):
    if STOP == tag:
        raise StopBuild()


class Dep:
    __slots__ = ("w", "r")

    def __init__(self):
        self.w = {}
        self.r = {}


class Tl:
    def __init__(self, t, d=None, ps=False):
        self.t = t
        self.d = d or Dep()
        self.ps = ps

    def __getitem__(self, idx):
        return self.t[idx]


class K:
    NDMA = 24

    def __init__(self, nc):
        self.nc = nc
        self.es = contextlib.ExitStack()
        self.eng = dict(pe=nc.tensor, act=nc.scalar, dve=nc.vector, pool=nc.gpsimd, sp=nc.sync)
        self.sem = {}
        self.cnt = {}
        for e in ["pe", "act", "dve", "pool"]:
            self.sem[e] = self.es.enter_context(nc.semaphore("s_" + e))
            self.cnt[e] = 0
        for i in range(self.NDMA):
            key = ("d", i)
            self.sem[key] = self.es.enter_context(nc.semaphore("d%d" % i))
            self.cnt[key] = 0
        self.rr = 0
        self.waited = {}
        self.ninstr = 0

    def _wait(self, e, reads, writes):
        need = {}
        for t in reads:
            for key, v in t.d.w.items():
                need[key] = max(need.get(key, 0), v)
            if t.ps:
                for key, v in t.d.r.items():
                    need[key] = max(need.get(key, 0), v)
        for t in writes:
            for key, v in t.d.w.items():
                need[key] = max(need.get(key, 0), v)
            for key, v in t.d.r.items():
                need[key] = max(need.get(key, 0), v)
        for key, v in need.items():
            if key == e and e == "pe":
                continue
            if self.waited.get((e, key), 0) < v:
                self.eng[e].wait_ge(self.sem[key], v)
                self.waited[(e, key)] = v
                self.ninstr += 1

    def _done(self, key, v, reads, writes):
        for t in writes:
            t.d.w = {key: v}
            t.d.r = {}
        for t in reads:
            if t.d.r.get(key, 0) < v:
                t.d.r[key] = v

    def op(self, e, fn, r, w):
        self._wait(e, r, w)
        ins = fn()
        self.cnt[e] += 1
        ins.then_inc(self.sem[e], 1)
        self._done(e, self.cnt[e], r, w)
        self.ninstr += 1
        return ins

    def dma(self, q, out, in_, r, w):
        self._wait(q, r, w)
        key = ("d", self.rr)
        self.rr = (self.rr + 1) % self.NDMA
        if self.waited.get((q, key), 0) < self.cnt[key]:
            self.eng[q].wait_ge(self.sem[key], self.cnt[key])
            self.waited[(q, key)] = self.cnt[key]
        ins = self.eng[q].dma_start(out=out, in_=in_)
        self.cnt[key] += 16
        ins.then_inc(self.sem[key], 16)
        self._done(key, self.cnt[key], r, w)
        self.ninstr += 1

    def barrier(self, engines=("pe", "act", "dve", "pool", "sp")):
        for e in engines:
            for key, v in self.cnt.items():
                if key == e or v == 0:
                    continue
                if self.waited.get((e, key), 0) < v:
                    self.eng[e].wait_ge(self.sem[key], v)
                    self.waited[(e, key)] = v

    def mm(self, out, lhsT, rhs, start, stop, r, w):
        return self.op("pe", lambda: self.nc.tensor.matmul(out, lhsT, rhs, start=start, stop=stop), r, w)

    def tr(self, out, in_, ident, r, w):
        return self.op("pe", lambda: self.nc.tensor.transpose(out, in_, ident), r, w)

    def act(self, out, in_, func, r, w, **kw):
        return self.op("act", lambda: self.nc.scalar.activation(out=out, in_=in_, func=func, **kw), r, w)

    def ts(self, e, out, in0, s1, s2, op0, op1, r, w):
        eng = self.eng[e]
        if op1 is None:
            return self.op(e, lambda: eng.tensor_scalar(out=out, in0=in0, scalar1=s1, scalar2=None, op0=op0), r, w)
        return self.op(e, lambda: eng.tensor_scalar(out=out, in0=in0, scalar1=s1, scalar2=s2, op0=op0, op1=op1), r, w)

    def tt(self, e, out, in0, in1, op, r, w):
        eng = self.eng[e]
        return self.op(e, lambda: eng.tensor_tensor(out=out, in0=in0, in1=in1, op=op), r, w)

    def stt(self, out, in0, scalar, in1, op0, op1, r, w):
        return self.op("dve", lambda: self.nc.vector.scalar_tensor_tensor(out=out, in0=in0, scalar=scalar, in1=in1, op0=op0, op1=op1), r, w)

    def cp(self, e, out, in_, r, w):
        if e == "act":
            return self.op(e, lambda: self.nc.scalar.copy(out=out, in_=in_), r, w)
        eng = self.eng[e]
        return self.op(e, lambda: eng.tensor_copy(out=out, in_=in_), r, w)

    def recip(self, out, in_, r, w):
        return self.op("dve", lambda: self.nc.vector.reciprocal(out=out, in_=in_), r, w)

    def memset(self, e, ap, val, w):
        eng = self.eng[e]
        return self.op(e, lambda: eng.memset(ap, val), [], w)


O_ZA, O_ZAG, O_ZR, O_ZBG, O_ZQ, O_ZKV, O_ZKR, O_ZCG, O_ZM = 0, 256, 512, 1920, 2304, 2560, 2688, 2720, 3104
M_ZR, M_ZQ, M_ZKV, M_KR1, M_KR2, M_ZA, CM = 0, 1408, 1664, 1792, 1888, 1984, 2240
NPP = 60
C_ID, C_SF, C_IF, C_SB, C_IB, C_OB, C_OA, C_SEG, C_C64, C_S64, C_INVF, C_SGN, C_E96, NCST = (
    0, 128, 256, 384, 512, 640, 768, 896, 1408, 1472, 1536, 1537, 1538, 1538 + 97)


def build(T, NL, NS, debug=False):
    nc = bass.Bass("TRN2", target_bir_lowering=False)
    k = K(nc)
    try:
        return _build(nc, k, T, NL, NS, debug)
    except StopBuild:
        k.barrier()
        return nc, k


def _build(nc, k, T, NL, NS, debug):
    NSL = T // 512
    NCK = T // CH
    NT = T // 128
    okind = "ExternalOutput" if debug else "Internal"

    def din(name, shape, dt):
        return Tl(nc.dram_tensor(name, shape, dt, kind="ExternalInput").ap())

    x_in = din("x", [NS, T, D], F32)
    pos_in = din("pos", [NS, 1, T], I32)
    cst_in = din("cst", [128, NCST], F32)
    dftc_in = din("dftc", [T, T], BF)
    dfts_in = din("dfts", [T, T], BF)
    wmix_in = din("wmix", [NL, D, CM], F32)
    wgate_in = din("wgate", [NL, D, 4096], F32)
    wproj_in = din("wproj", [NL, D, D], F32)
    wout_in = din("wout", [NL, D, D], F32)
    pp_in = din("pp", [NL, 128, NPP], F32)
    fw_in = din("fw", [NL, 64, 256], F32)
    w2_in = din("w2", [NL, 128, 384], F32)
    a2_in = din("a2", [NL, 128, 384], F32)
    wuq_in = din("wuq", [NL, 256, 576], F32)
    wuqs_in = din("wuqs", [NL, 256, 576], F32)
    wukvk_in = din("wukvk", [NL, 128, 384], F32)
    wukvv_in = din("wukvv", [NL, 128, 384], F32)
    fg_in = din("fg", [1, D], F32)
    out_d = Tl(nc.dram_tensor("out", [NS, T, D], F32, kind="ExternalOutput").ap())

    def dscr(name, shape, dt):
        return Tl(nc.dram_tensor(name, shape, dt, kind=okind).ap())

    X = [dscr("X%d" % s, [T, D], F32) for s in range(NS)]
    HT = [dscr("HT%d" % s, [128, 8, T], BF) for s in range(NS)]
    ZR = [dscr("ZR%d" % s, [1408, T], F32) for s in range(NS)]
    ZA = [dscr("ZA%d" % s, [T, 256], BF) for s in range(NS)]
    QT = [dscr("QT%d" % s, [6, 97, T], BF) for s in range(NS)]
    KT = [dscr("KT%d" % s, [6, 97, T], BF) for s in range(NS)]
    VA = [dscr("VA%d" % s, [T, 6 * 65], BF) for s in range(NS)]
    YA = [dscr("YA%d" % s, [256, T], BF) for s in range(NS)]
    YB = [dscr("YB%d" % s, [384, T], F32) for s in range(NS)]
    YC = [dscr("YC%d" % s, [384, T], BF) for s in range(NS)]

    es0 = k.es

    uid = [0]

    def sb(es, name, shape, dt):
        uid[0] += 1
        return Tl(es.enter_context(nc.sbuf_tensor("sb%d_%s" % (uid[0], name), shape, dt)))

    cst = sb(es0, "cst", [128, NCST], F32)
    identb = sb(es0, "identb", [128, 128], BF)
    onesb = sb(es0, "onesb", [128, 128], BF)
    pp = sb(es0, "pp", [128, NPP], F32)
    ppx = sb(es0, "ppx", [128, 40], F32)
    CSD = [dscr("CSD%d" % s, [2, 32, T], F32) for s in range(NS)]
    PS = [Tl(es0.enter_context(nc.psum_tensor("ps%d" % i, [128, 512], F32)), ps=True) for i in range(7)]
    PSB = Tl(es0.enter_context(nc.psum_tensor("psb", [128, 1024], BF)), ps=True)

    k.dma("sp", cst[:], cst_in[:, :], [cst_in], [cst])
    k.cp("dve", identb[:], cst[:, C_ID:C_ID + 128], [cst], [identb])
    k.cp("dve", onesb[:], cst[:, C_OA:C_OA + 128], [cst], [onesb])
    ident = cst[:, C_ID:C_ID + 128]

    with contextlib.ExitStack() as es:
        posi = sb(es, "posi", [128, T], I32)
        ang = sb(es, "ang", [128, T], F32)
        kq = sb(es, "kq", [128, T], F32)
        rr_ = sb(es, "rr", [128, T], F32)
        cs1t = sb(es, "cs1t", [128, T], F32)
        cs2t = sb(es, "cs2t", [128, T], F32)
        CS1 = [cs1t] * NS
        CS2 = [cs2t] * NS
        P = slice(64, 96)
        for s in range(NS):
            for p in range(64, 96):
                k.dma("sp", posi[p:p + 1, :], pos_in[s, :, :], [pos_in], [posi])
            k.cp("dve", ang[P, :], posi[P, :], [posi], [ang])
            k.ts("dve", ang[P, :], ang[P, :], cst[P, C_INVF:C_INVF + 1], None, ALU.mult, None, [ang, cst], [ang])
            k.ts("dve", kq[P, :], ang[P, :], float(1.0 / (2 * np.pi)), None, ALU.mult, None, [ang], [kq])
            k.ts("dve", kq[P, :], kq[P, :], 12582912.0, None, ALU.add, None, [kq], [kq])
            k.ts("dve", kq[P, :], kq[P, :], -12582912.0, None, ALU.add, None, [kq], [kq])
            c1 = 6.28125
            c2 = float(np.float32(2 * np.pi - c1))
            c3 = float(2 * np.pi - c1 - np.float64(np.float32(2 * np.pi - c1)))
            k.stt(rr_[P, :], kq[P, :], -c1, ang[P, :], ALU.mult, ALU.add, [kq, ang], [rr_])
            k.stt(rr_[P, :], kq[P, :], -c2, rr_[P, :], ALU.mult, ALU.add, [kq, rr_], [rr_])
            k.stt(rr_[P, :], kq[P, :], -c3, rr_[P, :], ALU.mult, ALU.add, [kq, rr_], [rr_])
            k.ts("dve", rr_[P, :], rr_[P, :], 3.1415925, -3.1415925, ALU.min, ALU.max, [rr_], [rr_])
            k.act(CS2[s][P, :], rr_[P, :], AF.Sin, [rr_], [CS2[s]])
            k.ts("dve", CS2[s][P, :], CS2[s][P, :], cst[P, C_SGN:C_SGN + 1], None, ALU.mult, None, [CS2[s], cst], [CS2[s]])
            k.ts("dve", kq[P, :], rr_[P, :], -1.0, None, ALU.mult, None, [rr_], [kq])
            k.tt("dve", kq[P, :], kq[P, :], rr_[P, :], ALU.max, [kq, rr_], [kq])
            k.ts("dve", kq[P, :], kq[P, :], -1.0, float(np.pi / 2), ALU.mult, ALU.add, [kq], [kq])
            k.act(CS1[s][P, :], kq[P, :], AF.Sin, [kq], [CS1[s]])
            k.dma("sp", CSD[s][0], CS1[s][P, :], [CS1[s]], [CSD[s]])
            k.dma("sp", CSD[s][1], CS2[s][P, :], [CS2[s]], [CSD[s]])
        k.barrier()

    scale = float(96 ** -0.5)
    if STOP == "rope":
        return nc, k

    for l in range(NL):
        Xsrc = [Tl(x_in.t[s], x_in.d) for s in range(NS)] if l == 0 else X
        last = l == NL - 1
        k.dma("sp", pp[:], pp_in[l], [pp_in], [pp])
        k.tt("dve", ppx[:, 0:11], pp[:, 8:19], pp[:, 19:30], ALU.add, [pp], [ppx])
        k.ts("dve", ppx[:, 0:11], ppx[:, 0:11], -1.0, 1.0, ALU.mult, ALU.add, [ppx], [ppx])
        k.ts("dve", ppx[:, 11:14], pp[:, 45:48], -1.0, 1.0, ALU.mult, ALU.add, [pp], [ppx])
        k.ts("dve", ppx[:, 14:17], pp[:, 45:48], -2.0, 2.0, ALU.mult, ALU.add, [pp], [ppx])

        with contextlib.ExitStack() as es:
            wm = sb(es, "wm", [128, 8, CM], BF)
            stg = [sb(es, "stg%d" % i, [128, CM], F32) for i in range(2)]
            for c in range(8):
                st = stg[c % 2]
                k.dma("sp", st[:], wmix_in[l, c * 128:(c + 1) * 128, :], [wmix_in], [st])
                if c % 2:
                    k.act(wm[:, c, :], st[:], AF.Identity, [st, pp], [wm], scale=pp[:, c:c + 1])
                else:
                    k.ts("dve", wm[:, c, :], st[:], pp[:, c:c + 1], None, ALU.mult, None, [st, pp], [wm])
            wuq = sb(es, "wuq", [128, 2, 576], BF)
            wuqs = sb(es, "wuqs", [128, 2, 576], BF)
            wkk = sb(es, "wkk", [128, 384], BF)
            wkv = sb(es, "wkv", [128, 384], BF)
            for c in range(2):
                k.dma("sp", stg[0][:, 0:576], wuq_in[l, c * 128:(c + 1) * 128, :], [wuq_in], [stg[0]])
                k.ts("dve", wuq[:, c, :], stg[0][:, 0:576], pp[:, 57 + c:58 + c], None, ALU.mult, None, [stg[0], pp], [wuq])
                k.dma("sp", stg[1][:, 0:576], wuqs_in[l, c * 128:(c + 1) * 128, :], [wuqs_in], [stg[1]])
                k.ts("dve", wuqs[:, c, :], stg[1][:, 0:576], pp[:, 57 + c:58 + c], None, ALU.mult, None, [stg[1], pp], [wuqs])
            k.dma("sp", stg[0][:, 0:384], wukvk_in[l], [wukvk_in], [stg[0]])
            k.ts("dve", wkk[:], stg[0][:, 0:384], pp[:, 59:60], None, ALU.mult, None, [stg[0], pp], [wkk])
            k.dma("sp", stg[1][:, 0:384], wukvv_in[l], [wukvv_in], [stg[1]])
            k.ts("dve", wkv[:], stg[1][:, 0:384], pp[:, 59:60], None, ALU.mult, None, [stg[1], pp], [wkv])

            chk("p1a")
            xb = [sb(es, "xb%d" % i, [128, D], F32) for i in range(2)]
            junk = sb(es, "junk", [128, D], F32)
            st4 = sb(es, "st4", [128, 8], F32)
            hb = sb(es, "hb", [128, D], BF)
            hT = sb(es, "hT", [128, 8, 512], BF)
            zo = [sb(es, "zo%d" % i, [128, 512], F32) for i in range(3)]
            zq = sb(es, "zq", [128, 2, 512], F32)
            zkv = sb(es, "zkv", [128, 512], F32)
            sq = sb(es, "sq", [128, 2, 512], F32)
            rb = sb(es, "rb", [128, 512], F32)
            cq = sb(es, "cq", [128, 2, 512], BF)
            ckv = sb(es, "ckv", [128, 512], BF)
            t1 = sb(es, "t1", [128, 512], F32)
            t2 = sb(es, "t2", [128, 512], F32)
            qts = sb(es, "qts", [128, 6, 512], BF)
            kts = sb(es, "kts", [128, 6, 512], BF)
            q32 = sb(es, "q32", [128, 512], F32)
            vas = sb(es, "vas", [128, 4, 6, 65], BF)
            zas = sb(es, "zas", [128, 4, 256], BF)
            kmx = sb(es, "kmx", [128, 6, NSL * NS + 1], F32)
            csl = sb(es, "csl", [128, 2, 512], F32)
            e96 = cst[0:96, C_E96:C_E96 + 97]
            onesr = sb(es, "onesr", [128, 512], F32)
            k.memset("pool", onesr[:], 1.0, [onesr])
            k.memset("pool", vas[:], 1.0, [vas])
            k.memset("pool", kts[:], 1.0, [kts])
            k.memset("pool", qts[:], 0.0, [qts])
            zi = 0
            hTs = [hT, sb(es, "hT2", [128, 8, 512], BF)]
            csls = [csl, sb(es, "csl2", [128, 2, 512], F32)]
            work = [(s, sl) for s in range(NS) for sl in range(NSL)]

            def front(wi):
                s, sl = work[wi]
                hT, csl = hTs[wi % 2], csls[wi % 2]
                S0 = sl * 512
                SL = slice(S0, S0 + 512)
                k.dma("sp", csl[64:96, 0, :], CSD[s][0, :, SL], [CSD[s]], [csl])
                k.dma("sp", csl[64:96, 1, :], CSD[s][1, :, SL], [CSD[s]], [csl])
                for tt in range(4):
                    tok0 = S0 + tt * 128
                    xt = xb[tt % 2]
                    k.dma("sp", xt[:], Xsrc[s][tok0:tok0 + 128, :], [Xsrc[s]], [xt])
                    if l == 0:
                        k.dma("sp", X[s][tok0:tok0 + 128, :], xt[:], [xt], [X[s]])
                    k.act(junk[:], xt[:], AF.Square, [xt], [junk, st4], accum_out=st4[:, 0:1])
                    k.ts("dve", st4[:, 1:2], st4[:, 0:1], 1.0 / D, 1e-6, ALU.mult, ALU.add, [st4], [st4])
                    k.act(st4[:, 2:3], st4[:, 1:2], AF.Sqrt, [st4], [st4])
                    k.recip(st4[:, 3:4], st4[:, 2:3], [st4], [st4])
                    k.ts("dve", hb[:], xt[:], st4[:, 3:4], None, ALU.mult, None, [xt, st4], [hb])
                    for c in range(8):
                        k.tr(PSB[:, c * 128:(c + 1) * 128], hb[:, c * 128:(c + 1) * 128], identb[:], [hb, identb], [PSB])
                    k.cp("act", hT[:, :, tt * 128:(tt + 1) * 128], PSB[:, :].rearrange("p (c t) -> p c t", c=8), [PSB], [hT])
                k.dma("sp", HT[s][:, :, SL], hT[:], [hT], [HT[s]])
                chk("p1b")

            def back(wi):
                nonlocal zi
                s, sl = work[wi]
                hT, csl = hTs[wi % 2], csls[wi % 2]
                S0 = sl * 512
                SL = slice(S0, S0 + 512)
                R = slice(64, 96)
                for cb in range(11):
                    ps = PS[cb % 3]
                    for c in range(8):
                        k.mm(ps[:], wm[:, c, M_ZR + cb * 128:M_ZR + (cb + 1) * 128], hT[:, c, :], c == 0, c == 7, [wm, hT], [ps])
                    z = zo[zi % 3]
                    zi += 1
                    k.cp("act" if cb % 2 else "dve", z[:], ps[:], [ps], [z])
                    k.dma("sp", ZR[s][cb * 128:(cb + 1) * 128, SL], z[:], [z], [ZR[s]])
                chk("p1c")
                for tt in range(4):
                    ps = PS[3]
                    for c in range(8):
                        k.mm(ps[:, 0:256], hT[:, c, tt * 128:(tt + 1) * 128], wm[:, c, M_ZA:M_ZA + 256], c == 0, c == 7, [wm, hT], [ps])
                    chk("p1c1")
                    k.cp("dve", zas[:, tt, :], ps[:, 0:256], [ps], [zas])
                    chk("p1c3")
                chk("p1c2")
                k.dma("sp", ZA[s][SL, :].rearrange("(a p) c -> p a c", p=128), zas[:], [zas], [ZA[s]])
                chk("p1d")
                for b in range(2):
                    ps = PS[4]
                    for c in range(8):
                        k.mm(ps[:], wm[:, c, M_ZQ + b * 128:M_ZQ + (b + 1) * 128], hT[:, c, :], c == 0, c == 7, [wm, hT], [ps])
                    k.cp("dve", zq[:, b, :], ps[:], [ps], [zq])
                    k.act(sq[:, b, :], zq[:, b, :], AF.Square, [zq], [sq])
                ps = PS[4]
                for b in range(2):
                    k.mm(ps[:], cst[:, C_OA:C_OA + 128], sq[:, b, :], b == 0, b == 1, [cst, sq], [ps])
                k.ts("dve", rb[:], ps[:], 1.0 / 256, 1e-6, ALU.mult, ALU.add, [ps], [rb])
                k.act(rb[:], rb[:], AF.Sqrt, [rb], [rb])
                k.recip(rb[:], rb[:], [rb], [rb])
                for b in range(2):
                    k.tt("dve", cq[:, b, :], zq[:, b, :], rb[:], ALU.mult, [zq, rb], [cq])
                ps = PS[5]
                for c in range(8):
                    k.mm(ps[:], wm[:, c, M_ZKV:M_ZKV + 128], hT[:, c, :], c == 0, c == 7, [wm, hT], [ps])
                k.cp("dve", zkv[:], ps[:], [ps], [zkv])
                k.act(sq[:, 0, :], zkv[:], AF.Square, [zkv], [sq])
                ps = PS[5]
                k.mm(ps[:], cst[:, C_OA:C_OA + 128], sq[:, 0, :], True, True, [cst, sq], [ps])
                k.ts("dve", rb[:], ps[:], 1.0 / 128, 1e-6, ALU.mult, ALU.add, [ps], [rb])
                k.act(rb[:], rb[:], AF.Sqrt, [rb], [rb])
                k.recip(rb[:], rb[:], [rb], [rb])
                k.tt("dve", ckv[:], zkv[:], rb[:], ALU.mult, [zkv, rb], [ckv])
                chk("p1e")
                R = slice(64, 96)
                pa, pb = PS[3], PS[4]
                for c in range(8):
                    k.mm(pa[0:96, :], wm[:, c, M_KR1:M_KR1 + 96], hT[:, c, :], c == 0, c == 7, [wm, hT], [pa])
                for c in range(8):
                    k.mm(pb[0:96, :], wm[:, c, M_KR2:M_KR2 + 96], hT[:, c, :], c == 0, c == 7, [wm, hT], [pb])
                k.tt("dve", t1[R, :], pa[R, :], csl[R, 0, :], ALU.mult, [pa, csl], [t1])
                k.tt("dve", t2[R, :], pb[R, :], csl[R, 1, :], ALU.mult, [pb, csl], [t2])
                k.tt("dve", t1[R, :], t1[R, :], t2[R, :], ALU.add, [t1, t2], [t1])
                for h in range(6):
                    k.cp("act" if h % 2 else "dve", kts[R, h, :], t1[R, :], [t1], [kts])
                chk("p1f")
                for h in range(6):
                    pq, pqs, pk = PS[0], PS[1], PS[2]
                    for b in range(2):
                        k.mm(pq[0:96, :], wuq[:, b, h * 96:(h + 1) * 96], cq[:, b, :], b == 0, b == 1, [wuq, cq], [pq])
                    for b in range(2):
                        k.mm(pqs[0:96, :], wuqs[:, b, h * 96:(h + 1) * 96], cq[:, b, :], b == 0, b == 1, [wuqs, cq], [pqs])
                    k.mm(pk[0:64, :], wkk[:, h * 64:(h + 1) * 64], ckv[:], True, True, [wkk, ckv], [pk])
                    k.ts("dve", q32[0:64, :], pq[0:64, :], scale, None, ALU.mult, None, [pq], [q32])
                    k.tt("dve", t1[R, :], pq[R, :], csl[R, 0, :], ALU.mult, [pq, csl], [t1])
                    k.tt("dve", t2[R, :], pqs[R, :], csl[R, 1, :], ALU.mult, [pqs, csl], [t2])
                    k.stt(q32[R, :], t1[R, :], 1.0, t2[R, :], ALU.mult, ALU.add, [t1, t2], [q32])
                    k.ts("dve", q32[R, :], q32[R, :], scale, None, ALU.mult, None, [q32], [q32])
                    k.cp("act", qts[0:96, h, :], q32[0:96, :], [q32], [qts])
                    k.cp("act", kts[0:64, h, :], pk[0:64, :], [pk], [kts])
                    k.act(t2[0:96, :], qts[0:96, h, :], AF.Square, [qts], [t2])
                    pn = PS[5]
                    k.mm(pn[0:97, :], e96, t2[0:96, :], True, True, [cst, t2], [pn])
                    k.act(t1[96:97, :], pn[96:97, :], AF.Sqrt, [pn], [t1])
                    k.ts("dve", qts[96:97, h, :], t1[96:97, :], -1.0, None, ALU.mult, None, [t1], [qts])
                    k.act(t2[0:96, :], kts[0:96, h, :], AF.Square, [kts], [t2])
                    pn = PS[6]
                    k.mm(pn[0:97, :], e96, t2[0:96, :], True, True, [cst, t2], [pn])
                    k.op("dve", lambda pn=pn, h=h, s=s, sl=sl: nc.vector.tensor_reduce(
                        out=kmx[96:97, h, s * NSL + sl:s * NSL + sl + 1], in_=pn[96:97, :], axis=AX.X, op=ALU.max), [pn], [kmx])
                chk("p1g")
                k.dma("sp", QT[s][:, :, SL].rearrange("h p t -> p h t"), qts[0:97, :, :], [qts], [QT[s]])
                k.dma("sp", KT[s][:, 0:96, SL].rearrange("h p t -> p h t"), kts[0:96, :, :], [kts], [KT[s]])
                chk("p1h")
                for tt in range(4):
                    ps = PS[3]
                    k.mm(ps[:, 0:384], ckv[:, tt * 128:(tt + 1) * 128], wkv[:], True, True, [ckv, wkv], [ps])
                    k.cp("act", vas[:, tt, :, 0:64], ps[:, 0:384].rearrange("p (h v) -> p h v", h=6), [ps], [vas])
                k.dma("sp", VA[s][SL, :].rearrange("(a p) c -> p a c", p=128), vas[:].rearrange("p a h v -> p a (h v)"), [vas], [VA[s]])

            def kbound(s):
                for h in range(6):
                    k.op("dve", lambda h=h, s=s: nc.vector.tensor_reduce(
                        out=kmx[96:97, h, NSL * NS:NSL * NS + 1], in_=kmx[96:97, h, s * NSL:(s + 1) * NSL], axis=AX.X, op=ALU.max), [kmx], [kmx])
                    k.act(kmx[96:97, h, NSL * NS:NSL * NS + 1], kmx[96:97, h, NSL * NS:NSL * NS + 1], AF.Sqrt, [kmx], [kmx])
                    k.ts("dve", kts[96:97, h, :], onesr[96:97, :], kmx[96:97, h, NSL * NS:NSL * NS + 1], None, ALU.mult, None, [kmx, onesr], [kts])
                for sl in range(NSL):
                    k.dma("sp", KT[s][:, 96:97, sl * 512:(sl + 1) * 512].rearrange("h p t -> p h t"), kts[96:97, :, :], [kts], [KT[s]])

            front(0)
            for wi in range(len(work)):
                if wi + 1 < len(work):
                    front(wi + 1)
                back(wi)
                if work[wi][1] == NSL - 1:
                    kbound(work[wi][0])
            k.barrier()

        if STOP == "p1":
            return nc, k
        for s in range(NS):
            fourier_phase(nc, k, sb, l, s, T, cst, PS, ZA, YA, dftc_in, dfts_in, fw_in)
            if STOP == "fourier":
                return nc, k
            mla_phase(nc, k, sb, l, s, T, cst, PS, QT, KT, VA, YC)
            if STOP == "mla":
                return nc, k
            rwkv_phase(nc, k, sb, l, s, T, cst, PS, PSB, identb, pp, ppx, ZR, YB, w2_in, a2_in)
            if STOP == "rwkv":
                return nc, k

        out_phase(nc, k, sb, l, NS, T, cst, PS, PSB, identb, pp, HT, YA, YB, YC, X, out_d, wgate_in, wproj_in, wout_in, fg_in, last)

    k.barrier()
    return nc, k


def fourier_phase(nc, k, sb, l, s, T, cst, PS, ZA, YA, dftc_in, dfts_in, fw_in):
    NT = T // 128
    NB = T // 512
    with contextlib.ExitStack() as es:
        za = sb(es, "f_za", [128, NT, 256], BF)
        fw = sb(es, "f_fw", [64, 256], F32)
        wc = sb(es, "f_wc", [128, 2, 256], BF)
        ws = sb(es, "f_ws", [128, 2, 256], BF)
        mats = [sb(es, "f_m%d" % i, [128, NT, 512], BF) for i in range(2)]
        a1 = sb(es, "f_a1", [128, 2, 512], BF)
        a2 = sb(es, "f_a2", [128, 2, 512], BF)
        yo = sb(es, "f_yo", [128, 2, 512], BF)
        k.dma("sp", za[:], ZA[s][:, :].rearrange("(a p) c -> p a c", p=128), [ZA[s]], [za])
        k.dma("sp", fw[:], fw_in[l], [fw_in], [fw])
        k.memset("pool", wc[:], 0.0, [wc])
        k.memset("pool", ws[:], 0.0, [ws])
        nrm = float(1.0 / np.sqrt(T * 64.0))
        pc, psn = PS[0], PS[1]
        k.mm(pc[0:64, 0:256], cst[0:64, C_C64:C_C64 + 64], fw[:], True, True, [cst, fw], [pc])
        k.mm(psn[0:64, 0:256], cst[0:64, C_S64:C_S64 + 64], fw[:], True, True, [cst, fw], [psn])
        for g in range(4):
            rows = slice((g % 2) * 64, (g % 2) * 64 + 64)
            cols = slice(g * 64, (g + 1) * 64)
            k.ts("dve", wc[rows, g // 2, cols], pc[0:64, cols], nrm, None, ALU.mult, None, [pc], [wc])
            k.ts("dve", ws[rows, g // 2, cols], psn[0:64, cols], -nrm, None, ALU.mult, None, [psn], [ws])
        for tb in range(NB):
            TB = slice(tb * 512, (tb + 1) * 512)
            for mi, src in enumerate((dftc_in, dfts_in)):
                m = mats[mi]
                for half in range(2):
                    hs = slice(half * (NT // 2), (half + 1) * (NT // 2)) if NT >= 2 else slice(0, NT)
                    if NT < 2 and half == 1:
                        continue
                    k.dma("sp", m[:, hs, :], src[:, TB].rearrange("(a p) n -> p a n", p=128)[:, hs, :], [src], [m])
                dst = a1 if mi == 0 else a2
                for cb in range(2):
                    ps = PS[2 + cb]
                    for c in range(NT):
                        k.mm(ps[:], za[:, c, cb * 128:(cb + 1) * 128], m[:, c, :], c == 0, c == NT - 1, [za, m], [ps])
                    k.cp("act" if cb else "dve", dst[:, cb, :], ps[:], [ps], [dst])
            for eb in range(2):
                ps = PS[4 + eb]
                i = 0
                for (w_, a_) in ((wc, a1), (ws, a2)):
                    for cb in range(2):
                        k.mm(ps[:], w_[:, cb, eb * 128:(eb + 1) * 128], a_[:, cb, :], i == 0, i == 3, [w_, a_], [ps])
                        i += 1
                k.cp("act", yo[:, eb, :], ps[:], [ps], [yo])
            k.dma("sp", YA[s][:, TB].rearrange("(e p) t -> p e t", p=128), yo[:], [yo], [YA[s]])
        k.barrier()


def mla_phase(nc, k, sb, l, s, T, cst, PS, QT, KT, VA, YC):
    NT = T // 128
    NB = T // 512
    with contextlib.ExitStack() as es:
        qt = [sb(es, "m_qt%d" % i, [128, T], BF) for i in range(2)]
        kt = [sb(es, "m_kt%d" % i, [128, T], BF) for i in range(2)]
        va = [sb(es, "m_va%d" % i, [128, NT, 65], BF) for i in range(2)]
        pt = [sb(es, "m_pt%d" % i, [128, 512], BF) for i in range(4)]
        osb = sb(es, "m_o", [128, 512], F32)
        rc = sb(es, "m_rc", [128, 512], F32)
        yo = [sb(es, "m_yo%d" % i, [64, 512], BF) for i in range(2)]
        pi = 0
        def ld(h):
            k.dma("sp", qt[h % 2][0:97, :], QT[s][h], [QT[s]], [qt[h % 2]])
            k.dma("sp", kt[h % 2][0:97, :], KT[s][h], [KT[s]], [kt[h % 2]])
            k.dma("sp", va[h % 2][:], VA[s][:, h * 65:(h + 1) * 65].rearrange("(a p) c -> p a c", p=128), [VA[s]], [va[h % 2]])

        ld(0)
        pending = []
        for h in range(6):
            q_, k_, v_ = qt[h % 2], kt[h % 2], va[h % 2]
            if h + 1 < 6:
                ld(h + 1)
            for qb in range(NB):
                QB = slice(qb * 512, (qb + 1) * 512)
                po = PS[4 + qb % 2]
                pq_ = {}
                for kc in range(NT + 2):
                    if kc == min(6, NT) and pending:
                        pending.pop(0)()
                    if kc < NT:
                        ps = PS[kc % 4]
                        k.mm(ps[:], k_[0:97, kc * 128:(kc + 1) * 128], q_[0:97, QB], True, True, [k_, q_], [ps])
                        p_ = pt[pi % 4]
                        pi += 1
                        k.act(p_[:], ps[:], AF.Exp, [ps], [p_])
                        pq_[kc] = p_
                    if kc >= 2:
                        j = kc - 2
                        k.mm(po[0:65, :], v_[:, j, :], pq_[j][:], j == 0, j == NT - 1, [v_, pq_[j]], [po])
                def epi(po=po, h=h, qb=qb, QB=QB):
                    k.cp("dve", osb[0:65, :], po[0:65, :], [po], [osb])
                    k.recip(rc[64:65, :], osb[64:65, :], [osb], [rc])
                    pbc = PS[6]
                    k.mm(pbc[0:64, :], cst[64:65, C_OA:C_OA + 64], rc[64:65, :], True, True, [cst, rc], [pbc])
                    y_ = yo[(h * NB + qb) % 2]
                    k.tt("dve", y_[:], osb[0:64, :], pbc[0:64, :], ALU.mult, [osb, pbc], [y_])
                    k.dma("sp", YC[s][h * 64:(h + 1) * 64, QB], y_[:], [y_], [YC[s]])
                pending.append(epi)
        while pending:
            pending.pop(0)()
        k.barrier()


def rwkv_phase(nc, k, sb, l, s, T, cst, PS, PSB, identb, pp, ppx, ZR, YB, w2_in, a2_in):
    NSL = T // 512
    ident = cst[:, C_ID:C_ID + 128]
    with contextlib.ExitStack() as es:
        w2 = sb(es, "r_w2", [128, 384], F32)
        a2w = sb(es, "r_a2", [128, 384], F32)
        k.dma("sp", w2[:], w2_in[l], [w2_in], [w2])
        k.dma("sp", a2w[:], a2_in[l], [a2_in], [a2w])
        zr = sb(es, "r_zr", [128, 11, 514], F32)
        zs = sb(es, "r_zs", [128, 11, 512], F32)
        th = sb(es, "r_th", [128, 512], F32)
        lw = sb(es, "r_lw", [128, 3, 512], F32)
        ic = sb(es, "r_ic", [128, 3, 512], F32)
        kk = sb(es, "r_kk", [128, 3, 512], F32)
        tA = sb(es, "r_tA", [128, 3, 512], F32)
        tB = sb(es, "r_tB", [128, 3, 512], F32)
        cum = sb(es, "r_cum", [128, 3, 512], F32)
        epos = sb(es, "r_ep", [128, 3, 512], F32)
        eneg = sb(es, "r_en", [128, 3, 512], F32)
        AR = sb(es, "r_AR", [128, 3, 2, 512], RDT)
        BK = sb(es, "r_BK", [128, 3, 2, 512], RDT)
        VV = sb(es, "r_VV", [128, 3, 512], RDT)
        Y = sb(es, "r_Y", [128, 3, 512], F32)
        Y0 = sb(es, "r_Y0", [128, 3, 512], F32)
        ST = sb(es, "r_ST", [128, 3, 64], RDT)
        TOKB = [[sb(es, "r_TOK%d_%d" % (i, b), [128, 384], RDT) for b in range(3)] for i in range(3)]
        NU = 6
        NAr = [[sb(es, "r_NAr%d_%d" % (j, i), [128, 256], RDT) for i in range(NU)] for j in range(2)]
        KA = [[sb(es, "r_KA%d_%d" % (j, i), [128, 256], RDT) for i in range(NU)] for j in range(2)]
        NN = [[[sb(es, "r_NN%d_%d_%d" % (q, i, j), [128, 256], RDT) for j in range(2)] for i in range(NU)] for q in range(2)]
        TM = [[[sb(es, "r_TM%d_%d_%d" % (q, i, j), [128, 128], RDT) for j in range(2)] for i in range(NU)] for q in range(2)]
        X0 = [sb(es, "r_X0%d" % i, [128, 64], RDT) for i in range(NU)]
        UT = [sb(es, "r_UT%d" % i, [128, 64], RDT) for i in range(NU)]
        ST32 = sb(es, "r_ST32", [128, 3, 64], F32)
        if RDT == BF:
            ptile, idn, idt = PSB, identb[:], identb
        else:
            ptile, idn, idt = PS[6], ident, cst
        for d in range(2):
            k.memset("pool", ST[:], 0.0, [ST])
            k.memset("pool", ST32[:], 0.0, [ST32])
            slabs = list(range(NSL)) if d == 0 else list(range(NSL - 1, -1, -1))

            def load_zr(sl_):
                S0_ = sl_ * 512
                lo = 1 if sl_ == 0 else 0
                hi = 1 if sl_ == NSL - 1 else 0
                if lo:
                    k.memset("pool", zr[:, :, 0:1], 0.0, [zr])
                if hi:
                    k.memset("pool", zr[:, :, 513:514], 0.0, [zr])
                k.dma("sp", zr[:, :, lo:514 - hi], ZR[s][:, S0_ - 1 + lo:S0_ + 513 - hi].rearrange("(b p) t -> p b t", p=128), [ZR[s]], [zr])
            m2 = cst[:, C_SF:C_SF + 256] if d == 0 else cst[:, C_SB:C_SB + 256]
            mT = cst[:, C_SB:C_SB + 128] if d == 0 else cst[:, C_SF:C_SF + 128]
            for sl in slabs:
                S0 = sl * 512
                if sl == slabs[0]:
                    load_zr(sl)
                for b in range(11):
                    k.act(zs[:, b, :], zr[:, b, 1:513], AF.Identity, [zr, ppx], [zs], scale=ppx[:, b:b + 1])
                    k.stt(zs[:, b, :], zr[:, b, 0:512], pp[:, 8 + b:9 + b], zs[:, b, :], ALU.mult, ALU.add, [zr, pp, zs], [zs])
                    k.stt(zs[:, b, :], zr[:, b, 2:514], pp[:, 19 + b:20 + b], zs[:, b, :], ALU.mult, ALU.add, [zr, pp, zs], [zs])
                si_ = slabs.index(sl)
                if si_ + 1 < len(slabs):
                    load_zr(slabs[si_ + 1])
                DR = slice(d * 64, d * 64 + 64)
                B3 = range(3)
                k.act(th[DR, :], zs[DR, 9, :], AF.Tanh, [zs], [th])
                for b in B3:
                    k.mm(PS[b][:], w2[DR, b * 128:(b + 1) * 128], th[DR, :], True, True, [w2, th], [PS[b]])
                for b in B3:
                    k.mm(PS[3 + b][:], a2w[DR, b * 128:(b + 1) * 128], zs[DR, 10, :], True, True, [a2w, zs], [PS[3 + b]])
                for b in B3:
                    k.ts("dve", kk[:, b, :], zs[:, 3 + b, :], pp[:, 42 + b:43 + b], None, ALU.mult, None, [zs, pp], [kk])
                for b in B3:
                    k.act(lw[:, b, :], PS[b][:], AF.Sigmoid, [PS[b], pp], [lw], bias=pp[:, 30 + d * 3 + b:31 + d * 3 + b])
                for b in B3:
                    k.act(ic[:, b, :], PS[3 + b][:], AF.Sigmoid, [PS[3 + b], pp], [ic], bias=pp[:, 36 + d * 3 + b:37 + d * 3 + b])
                for b in B3:
                    k.tt("pool", tA[:, b, :], kk[:, b, :], kk[:, b, :], ALU.mult, [kk], [tA])
                for b in B3:
                    k.mm(PS[b][:], cst[:, C_OB:C_OB + 128], tA[:, b, :], True, True, [cst, tA], [PS[b]])
                for b in B3:
                    k.ts("dve", lw[:, b, :], lw[:, b, :], -float(np.exp(-0.5)), None, ALU.mult, None, [lw], [lw])
                for b in B3:
                    k.op("dve", lambda b=b: nc.vector.tensor_tensor_scan(
                        out=cum[:, b, :], data0=cst[:, C_SEG:C_SEG + 512], data1=lw[:, b, :], initial=0.0,
                        op0=ALU.mult, op1=ALU.add), [cst, lw], [cum])
                for b in B3:
                    k.act(tB[:, b, :], PS[b][:], AF.Sqrt, [PS[b]], [tB])
                if d == 1:
                    for b in B3:
                        for c in range(4):
                            cs_ = slice(c * 128, (c + 1) * 128)
                            k.stt(tA[:, b, cs_], cum[:, b, cs_], cum[:, b, c * 128 + 127:c * 128 + 128], lw[:, b, cs_],
                                  ALU.subtract, ALU.subtract, [cum, lw], [tA])
                    for b in B3:
                        k.ts("dve", cum[:, b, :], tA[:, b, :], -1.0, None, ALU.mult, None, [tA], [cum])
                for b in B3:
                    k.ts("dve", tB[:, b, :], tB[:, b, :], 1e-12, None, ALU.max, None, [tB], [tB])
                for b in B3:
                    k.act(epos[:, b, :], cum[:, b, :], AF.Exp, [cum], [epos])
                for b in B3:
                    k.act(eneg[:, b, :], cum[:, b, :], AF.Exp, [cum], [eneg], scale=-1.0)
                for b in B3:
                    k.tt("pool", tA[:, b, :], cum[:, b, :], lw[:, b, :], ALU.subtract, [cum, lw], [tA])
                for b in B3:
                    k.recip(tB[:, b, :], tB[:, b, :], [tB], [tB])
                for b in B3:
                    k.act(tA[:, b, :], tA[:, b, :], AF.Exp, [tA], [tA])
                for b in B3:
                    k.tt("dve", kk[:, b, :], kk[:, b, :], tB[:, b, :], ALU.mult, [kk, tB], [kk])
                for b in B3:
                    k.tt("pool", AR[:, b, 1, :], zs[:, b, :], epos[:, b, :], ALU.mult, [zs, epos], [AR])
                for b in B3:
                    k.stt(AR[:, b, 0, :], kk[:, b, :], -1.0, tA[:, b, :], ALU.mult, ALU.mult, [kk, tA], [AR])
                for b in B3:
                    k.tt("dve", tB[:, b, :], kk[:, b, :], ic[:, b, :], ALU.mult, [kk, ic], [tB])
                for b in B3:
                    k.cp("pool", VV[:, b, :], zs[:, 6 + b, :], [zs], [VV])
                for b in B3:
                    k.tt("dve", BK[:, b, 0, :], tB[:, b, :], eneg[:, b, :], ALU.mult, [tB, eneg], [BK])
                for b in B3:
                    k.ts("dve", tB[:, b, :], ic[:, b, :], pp[:, 45 + b:46 + b], ppx[:, 11 + b:12 + b], ALU.mult, ALU.add, [ic, pp, ppx], [tB])
                for b in B3:
                    k.tt("pool", tB[:, b, :], tB[:, b, :], zs[:, 3 + b, :], ALU.mult, [tB, zs], [tB])
                for b in B3:
                    k.tt("dve", BK[:, b, 1, :], tB[:, b, :], eneg[:, b, :], ALU.mult, [tB, eneg], [BK])
                if d == 1:
                    k.dma("sp", Y0[:], YB[s][:, S0:S0 + 512].rearrange("(b p) t -> p b t", p=128), [YB[s]], [Y0])
                chunks = range(4) if d == 0 else range(3, -1, -1)
                TMC = {}

                def stage12(c):
                        cs_ = slice(c * 128, (c + 1) * 128)
                        gcol = c * 128 + 127 if d == 0 else c * 128
                        tkl = TOKB[c % 3]
                        HS = range(6)
                        HR = [slice((h % 2) * 64, (h % 2) * 64 + 64) for h in HS]
                        HB = [h // 2 for h in HS]
                        cs_ = slice(c * 128, (c + 1) * 128)
                        gcol = c * 128 + 127 if d == 0 else c * 128
                        tkl = TOKB[c % 3]
                        yield
                        for b in range(3):
                            k.tr(ptile[:, 0:128], BK[:, b, 0, cs_], idn, [BK, idt], [ptile])
                            k.tr(ptile[:, 128:256], BK[:, b, 1, cs_], idn, [BK, idt], [ptile])
                            k.tr(ptile[:, 256:384], VV[:, b, cs_], idn, [VV, idt], [ptile])
                            k.cp("act", tkl[b][:], ptile[:, 0:384], [ptile], [tkl[b]])
                        HS = range(6)
                        HR = [slice((h % 2) * 64, (h % 2) * 64 + 64) for h in HS]
                        HB = [h // 2 for h in HS]
                        yield
                        for h in HS:
                            b, bank = HB[h], PS[h]
                            k.mm(bank[:, 0:256], BK[HR[h], b, 0, cs_], AR[HR[h], b, :, cs_], True, True, [BK, AR], [bank])
                            k.mm(bank[:, 256:512], BK[HR[h], b, 1, cs_], AR[HR[h], b, :, cs_], True, True, [BK, AR], [bank])
                        yield
                        for h in HS:
                            bank = PS[h]
                            k.tt("dve", NAr[c % 2][h][:], bank[:, 0:256], m2, ALU.mult, [bank, cst], [NAr[c % 2][h]])
                            k.tt("dve", KA[c % 2][h][:], bank[:, 256:512], m2, ALU.mult, [bank, cst], [KA[c % 2][h]])
                        yield
                        for h in HS:
                            b, bank = HB[h], PS[h]
                            k.mm(bank[:, 0:128], AR[HR[h], b, 0, cs_], BK[HR[h], b, 0, cs_], True, True, [BK, AR], [bank])
                        cur = {}
                        tmc = {}
                        TMC[c] = tmc
                        yield
                        for h in HS:
                            bank = PS[h]
                            nn0 = NN[c % 2][h][0]
                            k.tt("dve", nn0[:, 128:256], bank[:, 0:128], mT, ALU.mult, [bank, cst], [nn0])
                            k.cp("pool", nn0[:, 0:128], NAr[c % 2][h][:, 0:128], [NAr[c % 2][h]], [nn0])
                            k.tt("pool", TM[c % 2][h][0][:], NAr[c % 2][h][:, 0:128], ident, ALU.add, [NAr[c % 2][h], cst], [TM[c % 2][h][0]])
                            cur[h] = nn0
                            tmc[h] = TM[c % 2][h][0]
                        yield
                        for lev in range(6):
                            yield
                            for h in HS:
                                bank = PS[h]
                                if lev < 5:
                                    k.mm(bank[:, 0:128], cur[h][:, 128:256], cur[h][:, 0:128], True, True, [cur[h]], [bank])
                                k.mm(bank[:, 128:256], cur[h][:, 0:128], cur[h][:, 128:256], True, True, [cur[h]], [bank])
                            yield
                            for h in HS:
                                nxt = NN[c % 2][h][(lev + 1) % 2]
                                if lev < 5:
                                    k.cp("act", nxt[:], PS[h][:, 0:256], [PS[h]], [nxt])
                                else:
                                    k.cp("act", nxt[:, 128:256], PS[h][:, 128:256], [PS[h]], [nxt])
                                cur[h] = nxt
                            yield
                            for h in HS:
                                k.mm(PS[h][:, 256:384], cur[h][:, 128:256], tmc[h][:], True, True, [cur[h], tmc[h]], [PS[h]])
                            yield
                            for h in HS:
                                tm2 = TM[c % 2][h][(lev + 1) % 2]
                                k.tt("dve", tm2[:], PS[h][:, 256:384], tmc[h][:], ALU.add, [PS[h], tmc[h]], [tm2])
                                tmc[h] = tm2

                        yield

                def stage3(c):
                    cs_ = slice(c * 128, (c + 1) * 128)
                    gcol = c * 128 + 127 if d == 0 else c * 128
                    tkl = TOKB[c % 3]
                    HS = range(6)
                    HR = [slice((h % 2) * 64, (h % 2) * 64 + 64) for h in HS]
                    HB = [h // 2 for h in HS]
                    bank = PS[6]
                    XR = [slice(h * 64, (h + 1) * 64) for h in HS]
                    tv = [tkl[HB[h]][:, 256 + (h % 2) * 64:256 + (h % 2) * 64 + 64] for h in HS]
                    tb_ = [tkl[HB[h]][:, (h % 2) * 64:(h % 2) * 64 + 64] for h in HS]
                    tk_ = [tkl[HB[h]][:, 128 + (h % 2) * 64:128 + (h % 2) * 64 + 64] for h in HS]
                    yield
                    for h in HS:
                        b = HB[h]
                        k.mm(bank[:, XR[h]], AR[HR[h], b, 0, cs_], ST[HR[h], b, :], True, False, [AR, ST], [bank])
                        k.mm(bank[:, XR[h]], KA[c % 2][h][:, 0:128], tv[h], False, True, [KA[c % 2][h], tkl[b]], [bank])
                    yield
                    for h in HS:
                        k.cp("act", X0[h][:], bank[:, XR[h]], [bank], [X0[h]])
                    yield
                    for h in HS:
                        k.mm(bank[:, XR[h]], TMC[c][h][:], X0[h][:], True, True, [TMC[c][h], X0[h]], [bank])
                    yield
                    for h in HS:
                        k.cp("act", UT[h][:], bank[:, XR[h]], [bank], [UT[h]])
                    for pr in range(3):
                        yield
                        for h in (2 * pr, 2 * pr + 1):
                            b = HB[h]
                            o = (h % 2) * 192
                            k.mm(bank[0:64, o:o + 128], ST[HR[h], b, :], AR[HR[h], b, 1, cs_], True, False, [ST, AR], [bank])
                            k.mm(bank[0:64, o:o + 128], UT[h][:], NAr[c % 2][h][:, 128:256], False, False, [UT[h], NAr[c % 2][h]], [bank])
                            k.mm(bank[0:64, o:o + 128], tv[h], KA[c % 2][h][:, 128:256], False, True, [tkl[b], KA[c % 2][h]], [bank])
                            k.mm(bank[0:64, o + 128:o + 192], tb_[h], UT[h][:], True, False, [tkl[b], UT[h]], [bank])
                            k.mm(bank[0:64, o + 128:o + 192], tk_[h], tv[h], False, True, [tkl[b]], [bank])
                        yield
                        for h in (2 * pr, 2 * pr + 1):
                            b = HB[h]
                            o = (h % 2) * 192
                            if d == 0:
                                k.cp("act", Y[HR[h], b, cs_], bank[0:64, o:o + 128], [bank], [Y])
                            else:
                                k.tt("dve", Y[HR[h], b, cs_], bank[0:64, o:o + 128], Y0[HR[h], b, cs_], ALU.add, [bank, Y0], [Y])
                            k.tt("dve", ST32[HR[h], b, :], bank[0:64, o + 128:o + 192], ST32[HR[h], b, :], ALU.add, [bank, ST32], [ST32])
                    yield
                    for b in range(3):
                        k.ts("pool", ST32[:, b, :], ST32[:, b, :], epos[:, b, gcol:gcol + 1], None, ALU.mult, None, [ST32, epos], [ST32])
                    k.cp("act", ST[:], ST32[:], [ST32], [ST])
                    yield

                clist = list(chunks)
                for _ in stage12(clist[0]):
                    pass
                for ci, c in enumerate(clist):
                    g3 = stage3(c)
                    g12 = stage12(clist[ci + 1]) if ci + 1 < len(clist) else iter(())
                    done12 = done3 = False
                    while not (done12 and done3):
                        if not done12:
                            try:
                                next(g12)
                            except StopIteration:
                                done12 = True
                        if not done3:
                            try:
                                next(g3)
                            except StopIteration:
                                done3 = True
                if d == 1:
                    OB = cst[:, C_OB:C_OB + 128]
                    B3 = range(3)
                    for b in B3:
                        k.mm(PS[b][:], a2w[0:64, b * 128:(b + 1) * 128], zs[0:64, 10, :], True, True, [a2w, zs], [PS[b]])
                    for b in B3:
                        k.mm(PS[3 + b][:], OB, Y[:, b, :], True, True, [cst, Y], [PS[3 + b]])
                    for b in B3:
                        k.act(tA[:, b, :], PS[b][:], AF.Sigmoid, [PS[b], pp], [tA], bias=pp[:, 36 + b:37 + b])
                    for b in B3:
                        k.stt(cum[:, b, :], PS[3 + b][:], -1.0 / 64, Y[:, b, :], ALU.mult, ALU.add, [PS[3 + b], Y], [cum])
                    for b in B3:
                        k.tt("pool", eneg[:, b, :], cum[:, b, :], cum[:, b, :], ALU.mult, [cum], [eneg])
                    for b in B3:
                        k.mm(PS[3 + b][:], OB, eneg[:, b, :], True, True, [cst, eneg], [PS[3 + b]])
                    for b in B3:
                        k.tt("dve", tA[:, b, :], tA[:, b, :], ic[:, b, :], ALU.add, [tA, ic], [tA])
                    for b in B3:
                        k.ts("dve", tA[:, b, :], tA[:, b, :], pp[:, 45 + b:46 + b], ppx[:, 14 + b:15 + b], ALU.mult, ALU.add, [tA, pp, ppx], [tA])
                    for b in B3:
                        k.ts("dve", epos[:, b, :], PS[3 + b][:], 1.0 / 64, 64e-5, ALU.mult, ALU.add, [PS[3 + b]], [epos])
                    for b in B3:
                        k.act(epos[:, b, :], epos[:, b, :], AF.Sqrt, [epos], [epos])
                    for b in B3:
                        k.tt("dve", tA[:, b, :], tA[:, b, :], zs[:, 3 + b, :], ALU.mult, [tA, zs], [tA])
                    for b in B3:
                        k.stt(tA[:, b, :], zs[:, b, :], pp[:, 48 + b:49 + b], tA[:, b, :], ALU.mult, ALU.mult, [zs, pp, tA], [tA])
                    for b in B3:
                        k.mm(PS[b][:], OB, tA[:, b, :], True, True, [cst, tA], [PS[b]])
                    for b in B3:
                        k.recip(epos[:, b, :], epos[:, b, :], [epos], [epos])
                    for b in B3:
                        k.tt("dve", tB[:, b, :], PS[b][:], zs[:, 6 + b, :], ALU.mult, [PS[b], zs], [tB])
                    for b in B3:
                        k.tt("dve", cum[:, b, :], cum[:, b, :], epos[:, b, :], ALU.mult, [cum, epos], [cum])
                    for b in B3:
                        k.ts("dve", cum[:, b, :], cum[:, b, :], pp[:, 51 + b:52 + b], pp[:, 54 + b:55 + b], ALU.mult, ALU.add, [cum, pp], [cum])
                    for b in B3:
                        k.tt("pool", Y[:, b, :], cum[:, b, :], tB[:, b, :], ALU.add, [cum, tB], [Y])
                k.dma("sp", YB[s][:, S0:S0 + 512].rearrange("(b p) t -> p b t", p=128), Y[:], [Y], [YB[s]])
        k.barrier()


def out_phase(nc, k, sb, l, NS, T, cst, PS, PSB, identb, pp, HT, YA, YB, YC, X, out_d, wgate_in, wproj_in, wout_in, fg_in, last):
    NSL = T // 512
    with contextlib.ExitStack() as es:
        wg = sb(es, "o_wg", [128, 8, 4096], BF)
        wp = sb(es, "o_wp", [128, 8, D], BF)
        wo = sb(es, "o_wo", [128, 8, D], BF)
        stg = [sb(es, "o_stg%d" % i, [128, 2048], F32) for i in range(2)]
        si = 0
        for c in range(8):
            for hf in range(2):
                st = stg[si % 2]
                si += 1
                k.dma("sp", st[:], wgate_in[l, c * 128:(c + 1) * 128, hf * 2048:(hf + 1) * 2048], [wgate_in], [st])
                if si % 2:
                    k.act(wg[:, c, hf * 2048:(hf + 1) * 2048], st[:], AF.Identity, [st, pp], [wg], scale=pp[:, c:c + 1])
                else:
                    k.ts("dve", wg[:, c, hf * 2048:(hf + 1) * 2048], st[:], pp[:, c:c + 1], None, ALU.mult, None, [st, pp], [wg])
        for c in range(8):
            st = stg[si % 2]
            si += 1
            k.dma("sp", st[:, 0:D], wproj_in[l, c * 128:(c + 1) * 128, :], [wproj_in], [st])
            k.dma("sp", st[:, D:2 * D], wout_in[l, c * 128:(c + 1) * 128, :], [wout_in], [st])
            k.cp("dve", wp[:, c, :], st[:, 0:D], [st], [wp])
            k.cp("act", wo[:, c, :], st[:, D:2 * D], [st], [wo])
        hTs = [sb(es, "o_hT%d" % i, [128, 8, 512], BF) for i in range(2)]
        yas = [sb(es, "o_ya%d" % i, [128, 2, 512], BF) for i in range(2)]
        ybs = [sb(es, "o_yb%d" % i, [128, 3, 512], F32) for i in range(2)]
        ycs = [sb(es, "o_yc%d" % i, [128, 3, 512], BF) for i in range(2)]
        gs = [sb(es, "o_gs%d" % i, [128, 512], F32) for i in range(3)]
        yg = sb(es, "o_yg", [128, 8, 512], BF)
        mg = sb(es, "o_mg", [128, 8, 512], BF)
        tm_ = sb(es, "o_tm", [128, 512], F32)
        tm2 = sb(es, "o_tm2", [128, 512], F32)
        xb = [sb(es, "o_xb%d" % i, [128, D], F32) for i in range(2)]
        xn = [sb(es, "o_xn%d" % i, [128, D], F32) for i in range(2)]
        junk = sb(es, "o_junk", [128, D], F32)
        st4 = sb(es, "o_st4", [128, 8], F32)
        fgb = sb(es, "o_fgb", [128, D], F32)
        if last:
            for p in range(128):
                k.dma("sp", fgb[p:p + 1, :], fg_in[:, :], [fg_in], [fgb])
        gi = 0
        xi = 0
        work = [(s, sl) for s in range(NS) for sl in range(NSL)]

        def ld(wi):
            s_, sl_ = work[wi]
            j = wi % 2
            SL_ = slice(sl_ * 512, (sl_ + 1) * 512)
            k.dma("sp", hTs[j][:], HT[s_][:, :, SL_], [HT[s_]], [hTs[j]])
            k.dma("sp", yas[j][:], YA[s_][:, SL_].rearrange("(b p) t -> p b t", p=128), [YA[s_]], [yas[j]])
            k.dma("sp", ybs[j][:], YB[s_][:, SL_].rearrange("(b p) t -> p b t", p=128), [YB[s_]], [ybs[j]])
            k.dma("sp", ycs[j][:], YC[s_][:, SL_].rearrange("(b p) t -> p b t", p=128), [YC[s_]], [ycs[j]])

        ld(0)
        for wi, (s, sl) in enumerate(work):
            if True:
                SL = slice(sl * 512, (sl + 1) * 512)
                hT, ya, yb, yc = hTs[wi % 2], yas[wi % 2], ybs[wi % 2], ycs[wi % 2]
                if wi + 1 < len(work):
                    ld(wi + 1)
                for gb in range(8):
                    ps = PS[gb % 2]
                    for c in range(8):
                        k.mm(ps[:], wg[:, c, gb * 128:(gb + 1) * 128], hT[:, c, :], c == 0, c == 7, [wg, hT], [ps])
                    g_ = gs[gi % 3]
                    gi += 1
                    k.act(g_[:], ps[:], AF.Sigmoid, [ps], [g_])
                    k.tt("dve", g_[:], g_[:], ps[:], ALU.mult, [g_, ps], [g_])
                    if gb < 2:
                        src, srct = ya[:, gb, :], ya
                    elif gb < 5:
                        src, srct = yb[:, gb - 2, :], yb
                    else:
                        src, srct = yc[:, gb - 5, :], yc
                    k.tt("dve", yg[:, gb, :], g_[:], src, ALU.mult, [g_, srct], [yg])
                for ob in range(8):
                    OBS = slice(ob * 128, (ob + 1) * 128)
                    pa, pb, pc = PS[2], PS[3], PS[4]
                    for i, cb in enumerate((0, 1)):
                        k.mm(pa[:], wp[:, cb, OBS], yg[:, cb, :], i == 0, i == 1, [wp, yg], [pa])
                    for i, cb in enumerate((2, 3, 4)):
                        k.mm(pb[:], wp[:, cb, OBS], yg[:, cb, :], i == 0, i == 2, [wp, yg], [pb])
                    for i, cb in enumerate((5, 6, 7)):
                        k.mm(pc[:], wp[:, cb, OBS], yg[:, cb, :], i == 0, i == 2, [wp, yg], [pc])
                    sg = []
                    for j in range(3):
                        ps = PS[j % 2]
                        col = 1024 + j * 1024 + ob * 128
                        for c in range(8):
                            k.mm(ps[:], wg[:, c, col:col + 128], hT[:, c, :], c == 0, c == 7, [wg, hT], [ps])
                        g_ = gs[gi % 3]
                        gi += 1
                        k.act(g_[:], ps[:], AF.Sigmoid, [ps], [g_])
                        sg.append(g_)
                    k.tt("dve", tm_[:], pa[:], sg[0][:], ALU.mult, [pa, sg[0]], [tm_])
                    k.tt("dve", tm2[:], pb[:], sg[1][:], ALU.mult, [pb, sg[1]], [tm2])
                    k.tt("pool", tm_[:], tm_[:], tm2[:], ALU.add, [tm_, tm2], [tm_])
                    k.tt("dve", tm2[:], pc[:], sg[2][:], ALU.mult, [pc, sg[2]], [tm2])
                    k.tt("pool", mg[:, ob, :], tm_[:], tm2[:], ALU.add, [tm_, tm2], [mg])
                for tt in range(4):
                    tok0 = sl * 512 + tt * 128
                    xt = xb[xi % 2]
                    xo = xn[xi % 2]
                    xi += 1
                    k.dma("sp", xt[:], X[s][tok0:tok0 + 128, :], [X[s]], [xt])
                    for hf in range(2):
                        ps = PS[5 + hf]
                        for ob in range(8):
                            k.mm(ps[:], mg[:, ob, tt * 128:(tt + 1) * 128], wo[:, ob, hf * 512:(hf + 1) * 512], ob == 0, ob == 7, [mg, wo], [ps])
                        k.tt("dve", xo[:, hf * 512:(hf + 1) * 512], ps[:], xt[:, hf * 512:(hf + 1) * 512], ALU.add, [ps, xt], [xo])
                    if not last:
                        k.dma("sp", X[s][tok0:tok0 + 128, :], xo[:], [xo], [X[s]])
                    else:
                        k.act(junk[:], xo[:], AF.Square, [xo], [junk, st4], accum_out=st4[:, 0:1])
                        k.ts("dve", st4[:, 1:2], st4[:, 0:1], 1.0 / D, 1e-6, ALU.mult, ALU.add, [st4], [st4])
                        k.act(st4[:, 2:3], st4[:, 1:2], AF.Sqrt, [st4], [st4])
                        k.recip(st4[:, 3:4], st4[:, 2:3], [st4], [st4])
                        k.stt(xo[:], xo[:], st4[:, 3:4], fgb[:], ALU.mult, ALU.mult, [xo, st4, fgb], [xo])
                        k.dma("sp", out_d[s, tok0:tok0 + 128, :], xo[:], [xo], [out_d])
        k.barrier()


def _consts():
    c = np.zeros((128, NCST), np.float32)
    c[:, C_ID:C_ID + 128] = np.eye(128)
    j = np.arange(128)[:, None]
    t = np.arange(128)[None, :]
    c[:, C_SF:C_SF + 128] = j < t
    c[:, C_IF:C_IF + 128] = j <= t
    c[:, C_SB:C_SB + 128] = j > t
    c[:, C_IB:C_IB + 128] = j >= t
    c[:, C_OB:C_OB + 128] = (j // 64) == (t // 64)
    c[:, C_OA:C_OA + 128] = 1.0
    seg = np.ones(512, np.float32)
    seg[::128] = 0.0
    c[:, C_SEG:C_SEG + 512] = seg[None, :]
    a = np.arange(64)
    ang = 2 * np.pi * np.outer(a, a) / 64.0
    c[0:64, C_C64:C_C64 + 64] = np.cos(ang)
    c[0:64, C_S64:C_S64 + 64] = np.sin(ang)
    inv = (10000.0 ** (-np.arange(0, 32, 2, dtype=np.float32) / np.float32(32))).astype(np.float32)
    for p in range(64, 96):
        c[p, C_INVF] = inv[(p - 64) % 16]
        c[p, C_SGN] = -1.0 if p < 80 else 1.0
    c[0:96, C_E96 + 96] = 1.0
    return c


def _dft(T):
    n = np.arange(T, dtype=np.int64)
    m = (np.outer(n, n) % T).astype(np.float64) * (2 * np.pi / T)
    return np.cos(m).astype(ml_dtypes.bfloat16), np.sin(m).astype(ml_dtypes.bfloat16)


def host_inputs(inp, T, NL, NS, ncores):
    f = lambda a: np.ascontiguousarray(np.asarray(a), dtype=np.float32)
    w_in = f(inp["w_in"])
    wmix = np.concatenate([
        w_in[:, :, O_ZR:O_ZR + 1408], w_in[:, :, O_ZQ:O_ZQ + 256], w_in[:, :, O_ZKV:O_ZKV + 128],
        w_in[:, :, O_ZKV:O_ZKV + 64], w_in[:, :, O_ZKR:O_ZKR + 32],
        w_in[:, :, O_ZKV:O_ZKV + 64], w_in[:, :, O_ZKR + 16:O_ZKR + 32], w_in[:, :, O_ZKR:O_ZKR + 16],
        w_in[:, :, O_ZA:O_ZA + 256]], axis=2)
    assert wmix.shape[2] == CM
    wgate = np.concatenate([w_in[:, :, O_ZAG:O_ZAG + 256], w_in[:, :, O_ZBG:O_ZBG + 384],
                            w_in[:, :, O_ZCG:O_ZCG + 384], w_in[:, :, O_ZM:O_ZM + 3072]], axis=2)
    wproj = np.concatenate([f(inp["proj_a"]), f(inp["proj_b"]), f(inp["proj_c"])], axis=1)
    pp = np.zeros((NL, 128, NPP), np.float32)

    def blk(v, n):
        return np.transpose(v.reshape(NL, n, 128), (0, 2, 1))

    pp[:, :, 0:8] = blk(f(inp["norm_g"]), 8)
    pp[:, :, 8:19] = blk(f(inp["shift_mu_prev"]), 11)
    pp[:, :, 19:30] = blk(f(inp["shift_mu_next"]), 11)
    pp[:, :, 30:36] = blk(f(inp["decay_w0"]).reshape(NL, 768), 6)
    pp[:, :, 36:42] = blk(f(inp["iclr_a0"]).reshape(NL, 768), 6)
    pp[:, :, 42:45] = blk(f(inp["key_k"]), 3)
    pp[:, :, 45:48] = blk(f(inp["key_a"]), 3)
    pp[:, :, 48:51] = blk(f(inp["bonus_r_k"]).reshape(NL, 384), 3)
    pp[:, :, 51:54] = blk(f(inp["lnx_g"]), 3)
    pp[:, :, 54:57] = blk(f(inp["lnx_b"]), 3)
    pp[:, :, 57:59] = blk(f(inp["q_norm_g"]), 2)
    pp[:, :, 59:60] = blk(f(inp["kv_norm_g"]), 1)
    fw = np.transpose(f(inp["fourier_w"]), (0, 2, 1, 3)).reshape(NL, 64, 256)
    w2 = f(inp["decay_w2"]).reshape(NL, 128, 384)
    a2 = f(inp["iclr_a2"]).reshape(NL, 128, 384)
    wuq = f(inp["w_uq"])
    wq4 = wuq.reshape(NL, 256, 6, 96)
    wuqs = np.concatenate([wq4[..., 0:64], wq4[..., 80:96], wq4[..., 64:80]], axis=-1).reshape(NL, 256, 576)
    wkv4 = f(inp["w_ukv"]).reshape(NL, 128, 6, 128)
    wukvk = np.ascontiguousarray(wkv4[..., 0:64]).reshape(NL, 128, 384)
    wukvv = np.ascontiguousarray(wkv4[..., 64:128]).reshape(NL, 128, 384)
    dc, ds = _dft(T)
    shared = dict(cst=_consts(), dftc=dc, dfts=ds, wmix=np.ascontiguousarray(wmix), wgate=np.ascontiguousarray(wgate),
                  wproj=np.ascontiguousarray(wproj), wout=f(inp["w_out"]), pp=pp, fw=np.ascontiguousarray(fw), w2=w2, a2=a2,
                  wuq=np.ascontiguousarray(wuq), wuqs=np.ascontiguousarray(wuqs), wukvk=wukvk, wukvv=wukvv,
                  fg=f(inp["final_g"]).reshape(1, D))
    x = f(inp["x"])
    pos = np.ascontiguousarray(np.asarray(inp["positions"]), dtype=np.int32)
    maps = []
    for c in range(ncores):
        m = dict(shared)
        m["x"] = np.ascontiguousarray(x[c * NS:(c + 1) * NS])
        m["pos"] = np.ascontiguousarray(pos[c * NS:(c + 1) * NS]).reshape(NS, 1, T)
        maps.append(m)
    return maps


def kernel(**inputs):
    x = np.asarray(inputs["x"])
    B, T, _ = x.shape
    NL = np.asarray(inputs["w_in"]).shape[0]
    NS = B // NCORES
    nc, _ = build(T, NL, NS)
    maps = host_inputs(inputs, T, NL, NS, NCORES)
    res = run_bass_kernel_spmd(nc, maps, core_ids=list(range(NCORES)))
    return np.concatenate([np.asarray(r["out"], dtype=np.float32) for r in res.results], axis=0)
```

```python
import contextlib
import numpy as np
import ml_dtypes
import concourse.bass as bass
import concourse.mybir as mybir
from concourse.bass_utils import run_bass_kernel_spmd

F32 = mybir.dt.float32
BF = mybir.dt.bfloat16
I32 = mybir.dt.int32
AF = mybir.ActivationFunctionType
ALU = mybir.AluOpType
AX = mybir.AxisListType

D = 1024
NCORES = 8
CH = 128
RDT = BF
STOP = None


class StopBuild(Exception):
    pass


def chk(tag):
    if STOP == tag:
        raise StopBuild()


class Dep:
    __slots__ = ("w", "r")

    def __init__(self):
        self.w = {}
        self.r = {}


class Tl:
    def __init__(self, t, d=None, ps=False):
        self.t = t
        self.d = d or Dep()
        self.ps = ps

    def __getitem__(self, idx):
        return self.t[idx]


class K:
    NDMA = 24

    def __init__(self, nc):
        self.nc = nc
        self.es = contextlib.ExitStack()
        self.eng = dict(pe=nc.tensor, act=nc.scalar, dve=nc.vector, pool=nc.gpsimd, sp=nc.sync)
        self.sem = {}
        self.cnt = {}
        for e in ["pe", "act", "dve", "pool"]:
            self.sem[e] = self.es.enter_context(nc.semaphore("s_" + e))
            self.cnt[e] = 0
        for i in range(self.NDMA):
            key = ("d", i)
            self.sem[key] = self.es.enter_context(nc.semaphore("d%d" % i))
            self.cnt[key] = 0
        self.rr = 0
        self.waited = {}
        self.ninstr = 0

    def _wait(self, e, reads, writes):
        need = {}
        for t in reads:
            for key, v in t.d.w.items():
                need[key] = max(need.get(key, 0), v)
            if t.ps:
                for key, v in t.d.r.items():
                    need[key] = max(need.get(key, 0), v)
        for t in writes:
            for key, v in t.d.w.items():
                need[key] = max(need.get(key, 0), v)
            for key, v in t.d.r.items():
                need[key] = max(need.get(key, 0), v)
        for key, v in need.items():
            if key == e and e == "pe":
                continue
            if self.waited.get((e, key), 0) < v:
                self.eng[e].wait_ge(self.sem[key], v)
                self.waited[(e, key)] = v
                self.ninstr += 1

    def _done(self, key, v, reads, writes):
        for t in writes:
            t.d.w = {key: v}
            t.d.r = {}
        for t in reads:
            if t.d.r.get(key, 0) < v:
                t.d.r[key] = v

    def op(self, e, fn, r, w):
        self._wait(e, r, w)
        ins = fn()
        self.cnt[e] += 1
        ins.then_inc(self.sem[e], 1)
        self._done(e, self.cnt[e], r, w)
        self.ninstr += 1
        return ins

    def dma(self, q, out, in_, r, w):
        self._wait(q, r, w)
        key = ("d", self.rr)
        self.rr = (self.rr + 1) % self.NDMA
        if self.waited.get((q, key), 0) < self.cnt[key]:
            self.eng[q].wait_ge(self.sem[key], self.cnt[key])
            self.waited[(q, key)] = self.cnt[key]
        ins = self.eng[q].dma_start(out=out, in_=in_)
        self.cnt[key] += 16
        ins.then_inc(self.sem[key], 16)
        self._done(key, self.cnt[key], r, w)
        self.ninstr += 1

    def barrier(self, engines=("pe", "act", "dve", "pool", "sp")):
        for e in engines:
            for key, v in self.cnt.items():
                if key == e or v == 0:
                    continue
                if self.waited.get((e, key), 0) < v:
                    self.eng[e].wait_ge(self.sem[key], v)
                    self.waited[(e, key)] = v

    def mm(self, out, lhsT, rhs, start, stop, r, w):
        return self.op("pe", lambda: self.nc.tensor.matmul(out, lhsT, rhs, start=start, stop=stop), r, w)

    def tr(self, out, in_, ident, r, w):
        return self.op("pe", lambda: self.nc.tensor.transpose(out, in_, ident), r, w)

    def act(self, out, in_, func, r, w, **kw):
        return self.op("act", lambda: self.nc.scalar.activation(out=out, in_=in_, func=func, **kw), r, w)

    def ts(self, e, out, in0, s1, s2, op0, op1, r, w):
        eng = self.eng[e]
        if op1 is None:
            return self.op(e, lambda: eng.tensor_scalar(out=out, in0=in0, scalar1=s1, scalar2=None, op0=op0), r, w)
        return self.op(e, lambda: eng.tensor_scalar(out=out, in0=in0, scalar1=s1, scalar2=s2, op0=op0, op1=op1), r, w)

    def tt(self, e, out, in0, in1, op, r, w):
        eng = self.eng[e]
        return self.op(e, lambda: eng.tensor_tensor(out=out, in0=in0, in1=in1, op=op), r, w)

    def stt(self, out, in0, scalar, in1, op0, op1, r, w):
        return self.op("dve", lambda: self.nc.vector.scalar_tensor_tensor(out=out, in0=in0, scalar=scalar, in1=in1, op0=op0, op1=op1), r, w)

    def cp(self, e, out, in_, r, w):
        if e == "act":
            return self.op(e, lambda: self.nc.scalar.copy(out=out, in_=in_), r, w)
        eng = self.eng[e]
        return self.op(e, lambda: eng.tensor_copy(out=out, in_=in_), r, w)

    def recip(self, out, in_, r, w):
        return self.op("dve", lambda: self.nc.vector.reciprocal(out=out, in_=in_), r, w)

    def memset(self, e, ap, val, w):
        eng = self.eng[e]
        return self.op(e, lambda: eng.memset(ap, val), [], w)


O_ZA, O_ZAG, O_ZR, O_ZBG, O_ZQ, O_ZKV, O_ZKR, O_ZCG, O_ZM = 0, 256, 512, 1920, 2304, 2560, 2688, 2720, 3104
M_ZR, M_ZQ, M_ZKV, M_KR1, M_KR2, M_ZA, CM = 0, 1408, 1664, 1792, 1888, 1984, 2240
NPP = 60
C_ID, C_SF, C_IF, C_SB, C_IB, C_OB, C_OA, C_SEG, C_C64, C_S64, C_INVF, C_SGN, C_E96, NCST = (
    0, 128, 256, 384, 512, 640, 768, 896, 1408, 1472, 1536, 1537, 1538, 1538 + 97)


def build(T, NL, NS, debug=False):
    nc = bass.Bass("TRN2", target_bir_lowering=False)
    k = K(nc)
    try:
        return _build(nc, k, T, NL, NS, debug)
    except StopBuild:
        k.barrier()
        return nc, k


def _build(nc, k, T, NL, NS, debug):
    NSL = T // 512
    NCK = T // CH
    NT = T // 128
    okind = "ExternalOutput" if debug else "Internal"

    def din(name, shape, dt):
        return Tl(nc.dram_tensor(name, shape, dt, kind="ExternalInput").ap())

    x_in = din("x", [NS, T, D], F32)
    pos_in = din("pos", [NS, 1, T], I32)
    cst_in = din("cst", [128, NCST], F32)
    dftc_in = din("dftc", [T, T], BF)
    dfts_in = din("dfts", [T, T], BF)
    wmix_in = din("wmix", [NL, D, CM], F32)
    wgate_in = din("wgate", [NL, D, 4096], F32)
    wproj_in = din("wproj", [NL, D, D], F32)
    wout_in = din("wout", [NL, D, D], F32)
    pp_in = din("pp", [NL, 128, NPP], F32)
    fw_in = din("fw", [NL, 64, 256], F32)
    w2_in = din("w2", [NL, 128, 384], F32)
    a2_in = din("a2", [NL, 128, 384], F32)
    wuq_in = din("wuq", [NL, 256, 576], F32)
    wuqs_in = din("wuqs", [NL, 256, 576], F32)
    wukvk_in = din("wukvk", [NL, 128, 384], F32)
    wukvv_in = din("wukvv", [NL, 128, 384], F32)
    fg_in = din("fg", [1, D], F32)
    out_d = Tl(nc.dram_tensor("out", [NS, T, D], F32, kind="ExternalOutput").ap())

    def dscr(name, shape, dt):
        return Tl(nc.dram_tensor(name, shape, dt, kind=okind).ap())

    X = [dscr("X%d" % s, [T, D], F32) for s in range(NS)]
    HT = [dscr("HT%d" % s, [128, 8, T], BF) for s in range(NS)]
    ZR = [dscr("ZR%d" % s, [1408, T], F32) for s in range(NS)]
    ZA = [dscr("ZA%d" % s, [T, 256], BF) for s in range(NS)]
    QT = [dscr("QT%d" % s, [6, 97, T], BF) for s in range(NS)]
    KT = [dscr("KT%d" % s, [6, 97, T], BF) for s in range(NS)]
    VA = [dscr("VA%d" % s, [T, 6 * 65], BF) for s in range(NS)]
    YA = [dscr("YA%d" % s, [256, T], BF) for s in range(NS)]
    YB = [dscr("YB%d" % s, [384, T], F32) for s in range(NS)]
    YC = [dscr("YC%d" % s, [384, T], BF) for s in range(NS)]

    es0 = k.es

    uid = [0]

    def sb(es, name, shape, dt):
        uid[0] += 1
        return Tl(es.enter_context(nc.sbuf_tensor("sb%d_%s" % (uid[0], name), shape, dt)))

    cst = sb(es0, "cst", [128, NCST], F32)
    identb = sb(es0, "identb", [128, 128], BF)
    onesb = sb(es0, "onesb", [128, 128], BF)
    pp = sb(es0, "pp", [128, NPP], F32)
    ppx = sb(es0, "ppx", [128, 40], F32)
    CSD = [dscr("CSD%d" % s, [2, 32, T], F32) for s in range(NS)]
    PS = [Tl(es0.enter_context(nc.psum_tensor("ps%d" % i, [128, 512], F32)), ps=True) for i in range(7)]
    PSB = Tl(es0.enter_context(nc.psum_tensor("psb", [128, 1024], BF)), ps=True)

    k.dma("sp", cst[:], cst_in[:, :], [cst_in], [cst])
    k.cp("dve", identb[:], cst[:, C_ID:C_ID + 128], [cst], [identb])
    k.cp("dve", onesb[:], cst[:, C_OA:C_OA + 128], [cst], [onesb])
    ident = cst[:, C_ID:C_ID + 128]

    with contextlib.ExitStack() as es:
        posi = sb(es, "posi", [128, T], I32)
        ang = sb(es, "ang", [128, T], F32)
        kq = sb(es, "kq", [128, T], F32)
        rr_ = sb(es, "rr", [128, T], F32)
        cs1t = sb(es, "cs1t", [128, T], F32)
        cs2t = sb(es, "cs2t", [128, T], F32)
        CS1 = [cs1t] * NS
        CS2 = [cs2t] * NS
        P = slice(64, 96)
        for s in range(NS):
            for p in range(64, 96):
                k.dma("sp", posi[p:p + 1, :], pos_in[s, :, :], [pos_in], [posi])
            k.cp("dve", ang[P, :], posi[P, :], [posi], [ang])
            k.ts("dve", ang[P, :], ang[P, :], cst[P, C_INVF:C_INVF + 1], None, ALU.mult, None, [ang, cst], [ang])
            k.ts("dve", kq[P, :], ang[P, :], float(1.0 / (2 * np.pi)), None, ALU.mult, None, [ang], [kq])
            k.ts("dve", kq[P, :], kq[P, :], 12582912.0, None, ALU.add, None, [kq], [kq])
            k.ts("dve", kq[P, :], kq[P, :], -12582912.0, None, ALU.add, None, [kq], [kq])
            c1 = 6.28125
            c2 = float(np.float32(2 * np.pi - c1))
            c3 = float(2 * np.pi - c1 - np.float64(np.float32(2 * np.pi - c1)))
            k.stt(rr_[P, :], kq[P, :], -c1, ang[P, :], ALU.mult, ALU.add, [kq, ang], [rr_])
            k.stt(rr_[P, :], kq[P, :], -c2, rr_[P, :], ALU.mult, ALU.add, [kq, rr_], [rr_])
            k.stt(rr_[P, :], kq[P, :], -c3, rr_[P, :], ALU.mult, ALU.add, [kq, rr_], [rr_])
            k.ts("dve", rr_[P, :], rr_[P, :], 3.1415925, -3.1415925, ALU.min, ALU.max, [rr_], [rr_])
            k.act(CS2[s][P, :], rr_[P, :], AF.Sin, [rr_], [CS2[s]])
            k.ts("dve", CS2[s][P, :], CS2[s][P, :], cst[P, C_SGN:C_SGN + 1], None, ALU.mult, None, [CS2[s], cst], [CS2[s]])
            k.ts("dve", kq[P, :], rr_[P, :], -1.0, None, ALU.mult, None, [rr_], [kq])
            k.tt("dve", kq[P, :], kq[P, :], rr_[P, :], ALU.max, [kq, rr_], [kq])
            k.ts("dve", kq[P, :], kq[P, :], -1.0, float(np.pi / 2), ALU.mult, ALU.add, [kq], [kq])
            k.act(CS1[s][P, :], kq[P, :], AF.Sin, [kq], [CS1[s]])
            k.dma("sp", CSD[s][0], CS1[s][P, :], [CS1[s]], [CSD[s]])
            k.dma("sp", CSD[s][1], CS2[s][P, :], [CS2[s]], [CSD[s]])
        k.barrier()

    scale = float(96 ** -0.5)
    if STOP == "rope":
        return nc, k

    for l in range(NL):
        Xsrc = [Tl(x_in.t[s], x_in.d) for s in range(NS)] if l == 0 else X
        last = l == NL - 1
        k.dma("sp", pp[:], pp_in[l], [pp_in], [pp])
        k.tt("dve", ppx[:, 0:11], pp[:, 8:19], pp[:, 19:30], ALU.add, [pp], [ppx])
        k.ts("dve", ppx[:, 0:11], ppx[:, 0:11], -1.0, 1.0, ALU.mult, ALU.add, [ppx], [ppx])
        k.ts("dve", ppx[:, 11:14], pp[:, 45:48], -1.0, 1.0, ALU.mult, ALU.add, [pp], [ppx])
        k.ts("dve", ppx[:, 14:17], pp[:, 45:48], -2.0, 2.0, ALU.mult, ALU.add, [pp], [ppx])

        with contextlib.ExitStack() as es:
            wm = sb(es, "wm", [128, 8, CM], BF)
            stg = [sb(es, "stg%d" % i, [128, CM], F32) for i in range(2)]
            for c in range(8):
                st = stg[c % 2]
                k.dma("sp", st[:], wmix_in[l, c * 128:(c + 1) * 128, :], [wmix_in], [st])
                if c % 2:
                    k.act(wm[:, c, :], st[:], AF.Identity, [st, pp], [wm], scale=pp[:, c:c + 1])
                else:
                    k.ts("dve", wm[:, c, :], st[:], pp[:, c:c + 1], None, ALU.mult, None, [st, pp], [wm])
            wuq = sb(es, "wuq", [128, 2, 576], BF)
            wuqs = sb(es, "wuqs", [128, 2, 576], BF)
            wkk = sb(es, "wkk", [128, 384], BF)
            wkv = sb(es, "wkv", [128, 384], BF)
            for c in range(2):
                k.dma("sp", stg[0][:, 0:576], wuq_in[l, c * 128:(c + 1) * 128, :], [wuq_in], [stg[0]])
                k.ts("dve", wuq[:, c, :], stg[0][:, 0:576], pp[:, 57 + c:58 + c], None, ALU.mult, None, [stg[0], pp], [wuq])
                k.dma("sp", stg[1][:, 0:576], wuqs_in[l, c * 128:(c + 1) * 128, :], [wuqs_in], [stg[1]])
                k.ts("dve", wuqs[:, c, :], stg[1][:, 0:576], pp[:, 57 + c:58 + c], None, ALU.mult, None, [stg[1], pp], [wuqs])
            k.dma("sp", stg[0][:, 0:384], wukvk_in[l], [wukvk_in], [stg[0]])
            k.ts("dve", wkk[:], stg[0][:, 0:384], pp[:, 59:60], None, ALU.mult, None, [stg[0], pp], [wkk])
            k.dma("sp", stg[1][:, 0:384], wukvv_in[l], [wukvv_in], [stg[1]])
            k.ts("dve", wkv[:], stg[1][:, 0:384], pp[:, 59:60], None, ALU.mult, None, [stg[1], pp], [wkv])

            chk("p1a")
            xb = [sb(es, "xb%d" % i, [128, D], F32) for i in range(2)]
            junk = sb(es, "junk", [128, D], F32)
            st4 = sb(es, "st4", [128, 8], F32)
            hb = sb(es, "hb", [128, D], BF)
            hT = sb(es, "hT", [128, 8, 512], BF)
            zo = [sb(es, "zo%d" % i, [128, 512], F32) for i in range(3)]
            zq = sb(es, "zq", [128, 2, 512], F32)
            zkv = sb(es, "zkv", [128, 512], F32)
            sq = sb(es, "sq", [128, 2, 512], F32)
            rb = sb(es, "rb", [128, 512], F32)
            cq = sb(es, "cq", [128, 2, 512], BF)
            ckv = sb(es, "ckv", [128, 512], BF)
            t1 = sb(es, "t1", [128, 512], F32)
            t2 = sb(es, "t2", [128, 512], F32)
            qts = sb(es, "qts", [128, 6, 512], BF)
            kts = sb(es, "kts", [128, 6, 512], BF)
            q32 = sb(es, "q32", [128, 512], F32)
            vas = sb(es, "vas", [128, 4, 6, 65], BF)
            zas = sb(es, "zas", [128, 4, 256], BF)
            kmx = sb(es, "kmx", [128, 6, NSL * NS + 1], F32)
            csl = sb(es, "csl", [128, 2, 512], F32)
            e96 = cst[0:96, C_E96:C_E96 + 97]
            onesr = sb(es, "onesr", [128, 512], F32)
            k.memset("pool", onesr[:], 1.0, [onesr])
            k.memset("pool", vas[:], 1.0, [vas])
            k.memset("pool", kts[:], 1.0, [kts])
            k.memset("pool", qts[:], 0.0, [qts])
            zi = 0
            hTs = [hT, sb(es, "hT2", [128, 8, 512], BF)]
            csls = [csl, sb(es, "csl2", [128, 2, 512], F32)]
            work = [(s, sl) for s in range(NS) for sl in range(NSL)]

            def front(wi):
                s, sl = work[wi]
                hT, csl = hTs[wi % 2], csls[wi % 2]
                S0 = sl * 512
                SL = slice(S0, S0 + 512)
                k.dma("sp", csl[64:96, 0, :], CSD[s][0, :, SL], [CSD[s]], [csl])
                k.dma("sp", csl[64:96, 1, :], CSD[s][1, :, SL], [CSD[s]], [csl])
                for tt in range(4):
                    tok0 = S0 + tt * 128
                    xt = xb[tt % 2]
                    k.dma("sp", xt[:], Xsrc[s][tok0:tok0 + 128, :], [Xsrc[s]], [xt])
                    if l == 0:
                        k.dma("sp", X[s][tok0:tok0 + 128, :], xt[:], [xt], [X[s]])
                    k.act(junk[:], xt[:], AF.Square, [xt], [junk, st4], accum_out=st4[:, 0:1])
                    k.ts("dve", st4[:, 1:2], st4[:, 0:1], 1.0 / D, 1e-6, ALU.mult, ALU.add, [st4], [st4])
                    k.act(st4[:, 2:3], st4[:, 1:2], AF.Sqrt, [st4], [st4])
                    k.recip(st4[:, 3:4], st4[:, 2:3], [st4], [st4])
                    k.ts("dve", hb[:], xt[:], st4[:, 3:4], None, ALU.mult, None, [xt, st4], [hb])
                    for c in range(8):
                        k.tr(PSB[:, c * 128:(c + 1) * 128], hb[:, c * 128:(c + 1) * 128], identb[:], [hb, identb], [PSB])
                    k.cp("act", hT[:, :, tt * 128:(tt + 1) * 128], PSB[:, :].rearrange("p (c t) -> p c t", c=8), [PSB], [hT])
                k.dma("sp", HT[s][:, :, SL], hT[:], [hT], [HT[s]])
                chk("p1b")

            def back(wi):
                nonlocal zi
                s, sl = work[wi]
                hT, csl = hTs[wi % 2], csls[wi % 2]
                S0 = sl * 512
                SL = slice(S0, S0 + 512)
                R = slice(64, 96)
                for cb in range(11):
                    ps = PS[cb % 3]
                    for c in range(8):
                        k.mm(ps[:], wm[:, c, M_ZR + cb * 128:M_ZR + (cb + 1) * 128], hT[:, c, :], c == 0, c == 7, [wm, hT], [ps])
                    z = zo[zi % 3]
                    zi += 1
                    k.cp("act" if cb % 2 else "dve", z[:], ps[:], [ps], [z])
                    k.dma("sp", ZR[s][cb * 128:(cb + 1) * 128, SL], z[:], [z], [ZR[s]])
                chk("p1c")
                for tt in range(4):
                    ps = PS[3]
                    for c in range(8):
                        k.mm(ps[:, 0:256], hT[:, c, tt * 128:(tt + 1) * 128], wm[:, c, M_ZA:M_ZA + 256], c == 0, c == 7, [wm, hT], [ps])
                    chk("p1c1")
                    k.cp("dve", zas[:, tt, :], ps[:, 0:256], [ps], [zas])
                    chk("p1c3")
                chk("p1c2")
                k.dma("sp", ZA[s][SL, :].rearrange("(a p) c -> p a c", p=128), zas[:], [zas], [ZA[s]])
                chk("p1d")
                for b in range(2):
                    ps = PS[4]
                    for c in range(8):
                        k.mm(ps[:], wm[:, c, M_ZQ + b * 128:M_ZQ + (b + 1) * 128], hT[:, c, :], c == 0, c == 7, [wm, hT], [ps])
                    k.cp("dve", zq[:, b, :], ps[:], [ps], [zq])
                    k.act(sq[:, b, :], zq[:, b, :], AF.Square, [zq], [sq])
                ps = PS[4]
                for b in range(2):
                    k.mm(ps[:], cst[:, C_OA:C_OA + 128], sq[:, b, :], b == 0, b == 1, [cst, sq], [ps])
                k.ts("dve", rb[:], ps[:], 1.0 / 256, 1e-6, ALU.mult, ALU.add, [ps], [rb])
                k.act(rb[:], rb[:], AF.Sqrt, [rb], [rb])
                k.recip(rb[:], rb[:], [rb], [rb])
                for b in range(2):
                    k.tt("dve", cq[:, b, :], zq[:, b, :], rb[:], ALU.mult, [zq, rb], [cq])
                ps = PS[5]
                for c in range(8):
                    k.mm(ps[:], wm[:, c, M_ZKV:M_ZKV + 128], hT[:, c, :], c == 0, c == 7, [wm, hT], [ps])
                k.cp("dve", zkv[:], ps[:], [ps], [zkv])
                k.act(sq[:, 0, :], zkv[:], AF.Square, [zkv], [sq])
                ps = PS[5]
                k.mm(ps[:], cst[:, C_OA:C_OA + 128], sq[:, 0, :], True, True, [cst, sq], [ps])
                k.ts("dve", rb[:], ps[:], 1.0 / 128, 1e-6, ALU.mult, ALU.add, [ps], [rb])
                k.act(rb[:], rb[:], AF.Sqrt, [rb], [rb])
                k.recip(rb[:], rb[:], [rb], [rb])
                k.tt("dve", ckv[:], zkv[:], rb[:], ALU.mult, [zkv, rb], [ckv])
                chk("p1e")
                R = slice(64, 96)
                pa, pb = PS[3], PS[4]
                for c in range(8):
                    k.mm(pa[0:96, :], wm[:, c, M_KR1:M_KR1 + 96], hT[:, c, :], c == 0, c == 7, [wm, hT], [pa])
                for c in range(8):
                    k.mm(pb[0:96, :], wm[:, c, M_KR2:M_KR2 + 96], hT[:, c, :], c == 0, c == 7, [wm, hT], [pb])
                k.tt("dve", t1[R, :], pa[R, :], csl[R, 0, :], ALU.mult, [pa, csl], [t1])
                k.tt("dve", t2[R, :], pb[R, :], csl[R, 1, :], ALU.mult, [pb, csl], [t2])
                k.tt("dve", t1[R, :], t1[R, :], t2[R, :], ALU.add, [t1, t2], [t1])
                for h in range(6):
                    k.cp("act" if h % 2 else "dve", kts[R, h, :], t1[R, :], [t1], [kts])
                chk("p1f")
                for h in range(6):
                    pq, pqs, pk = PS[0], PS[1], PS[2]
                    for b in range(2):
                        k.mm(pq[0:96, :], wuq[:, b, h * 96:(h + 1) * 96], cq[:, b, :], b == 0, b == 1, [wuq, cq], [pq])
                    for b in range(2):
                        k.mm(pqs[0:96, :], wuqs[:, b, h * 96:(h + 1) * 96], cq[:, b, :], b == 0, b == 1, [wuqs, cq], [pqs])
                    k.mm(pk[0:64, :], wkk[:, h * 64:(h + 1) * 64], ckv[:], True, True, [wkk, ckv], [pk])
                    k.ts("dve", q32[0:64, :], pq[0:64, :], scale, None, ALU.mult, None, [pq], [q32])
                    k.tt("dve", t1[R, :], pq[R, :], csl[R, 0, :], ALU.mult, [pq, csl], [t1])
                    k.tt("dve", t2[R, :], pqs[R, :], csl[R, 1, :], ALU.mult, [pqs, csl], [t2])
                    k.stt(q32[R, :], t1[R, :], 1.0, t2[R, :], ALU.mult, ALU.add, [t1, t2], [q32])
                    k.ts("dve", q32[R, :], q32[R, :], scale, None, ALU.mult, None, [q32], [q32])
                    k.cp("act", qts[0:96, h, :], q32[0:96, :], [q32], [qts])
                    k.cp("act", kts[0:64, h, :], pk[0:64, :], [pk], [kts])
                    k.act(t2[0:96, :], qts[0:96, h, :], AF.Square, [qts], [t2])
                    pn = PS[5]
                    k.mm(pn[0:97, :], e96, t2[0:96, :], True, True, [cst, t2], [pn])
                    k.act(t1[96:97, :], pn[96:97, :], AF.Sqrt, [pn], [t1])
                    k.ts("dve", qts[96:97, h, :], t1[96:97, :], -1.0, None, ALU.mult, None, [t1], [qts])
                    k.act(t2[0:96, :], kts[0:96, h, :], AF.Square, [kts], [t2])
                    pn = PS[6]
                    k.mm(pn[0:97, :], e96, t2[0:96, :], True, True, [cst, t2], [pn])
                    k.op("dve", lambda pn=pn, h=h, s=s, sl=sl: nc.vector.tensor_reduce(
                        out=kmx[96:97, h, s * NSL + sl:s * NSL + sl + 1], in_=pn[96:97, :], axis=AX.X, op=ALU.max), [pn], [kmx])
                chk("p1g")
                k.dma("sp", QT[s][:, :, SL].rearrange("h p t -> p h t"), qts[0:97, :, :], [qts], [QT[s]])
                k.dma("sp", KT[s][:, 0:96, SL].rearrange("h p t -> p h t"), kts[0:96, :, :], [kts], [KT[s]])
                chk("p1h")
                for tt in range(4):
                    ps = PS[3]
                    k.mm(ps[:, 0:384], ckv[:, tt * 128:(tt + 1) * 128], wkv[:], True, True, [ckv, wkv], [ps])
                    k.cp("act", vas[:, tt, :, 0:64], ps[:, 0:384].rearrange("p (h v) -> p h v", h=6), [ps], [vas])
                k.dma("sp", VA[s][SL, :].rearrange("(a p) c -> p a c", p=128), vas[:].rearrange("p a h v -> p a (h v)"), [vas], [VA[s]])

            def kbound(s):
                for h in range(6):
                    k.op("dve", lambda h=h, s=s: nc.vector.tensor_reduce(
                        out=kmx[96:97, h, NSL * NS:NSL * NS + 1], in_=kmx[96:97, h, s * NSL:(s + 1) * NSL], axis=AX.X, op=ALU.max), [kmx], [kmx])
                    k.act(kmx[96:97, h, NSL * NS:NSL * NS + 1], kmx[96:97, h, NSL * NS:NSL * NS + 1], AF.Sqrt, [kmx], [kmx])
                    k.ts("dve", kts[96:97, h, :], onesr[96:97, :], kmx[96:97, h, NSL * NS:NSL * NS + 1], None, ALU.mult, None, [kmx, onesr], [kts])
                for sl in range(NSL):
                    k.dma("sp", KT[s][:, 96:97, sl * 512:(sl + 1) * 512].rearrange("h p t -> p h t"), kts[96:97, :, :], [kts], [KT[s]])

            front(0)
            for wi in range(len(work)):
                if wi + 1 < len(work):
                    front(wi + 1)
                back(wi)
                if work[wi][1] == NSL - 1:
                    kbound(work[wi][0])
            k.barrier()

        if STOP == "p1":
            return nc, k
        for s in range(NS):
            fourier_phase(nc, k, sb, l, s, T, cst, PS, ZA, YA, dftc_in, dfts_in, fw_in)
            if STOP == "fourier":
                return nc, k
            mla_phase(nc, k, sb, l, s, T, cst, PS, QT, KT, VA, YC)
            if STOP == "mla":
                return nc, k
            rwkv_phase(nc, k, sb, l, s, T, cst, PS, PSB, identb, pp, ppx, ZR, YB, w2_in, a2_in)
            if STOP == "rwkv":
                return nc, k

        out_phase(nc, k, sb, l, NS, T, cst, PS, PSB, identb, pp, HT, YA, YB, YC, X, out_d, wgate_in, wproj_in, wout_in, fg_in, last)

    k.barrier()
    return nc, k


def fourier_phase(nc, k, sb, l, s, T, cst, PS, ZA, YA, dftc_in, dfts_in, fw_in):
    NT = T // 128
    NB = T // 512
    with contextlib.ExitStack() as es:
        za = sb(es, "f_za", [128, NT, 256], BF)
        fw = sb(es, "f_fw", [64, 256], F32)
        wc = sb(es, "f_wc", [128, 2, 256], BF)
        ws = sb(es, "f_ws", [128, 2, 256], BF)
        mats = [sb(es, "f_m%d" % i, [128, NT, 512], BF) for i in range(2)]
        a1 = sb(es, "f_a1", [128, 2, 512], BF)
        a2 = sb(es, "f_a2", [128, 2, 512], BF)
        yo = sb(es, "f_yo", [128, 2, 512], BF)
        k.dma("sp", za[:], ZA[s][:, :].rearrange("(a p) c -> p a c", p=128), [ZA[s]], [za])
        k.dma("sp", fw[:], fw_in[l], [fw_in], [fw])
        k.memset("pool", wc[:], 0.0, [wc])
        k.memset("pool", ws[:], 0.0, [ws])
        nrm = float(1.0 / np.sqrt(T * 64.0))
        pc, psn = PS[0], PS[1]
        k.mm(pc[0:64, 0:256], cst[0:64, C_C64:C_C64 + 64], fw[:], True, True, [cst, fw], [pc])
        k.mm(psn[0:64, 0:256], cst[0:64, C_S64:C_S64 + 64], fw[:], True, True, [cst, fw], [psn])
        for g in range(4):
            rows = slice((g % 2) * 64, (g % 2) * 64 + 64)
            cols = slice(g * 64, (g + 1) * 64)
            k.ts("dve", wc[rows, g // 2, cols], pc[0:64, cols], nrm, None, ALU.mult, None, [pc], [wc])
            k.ts("dve", ws[rows, g // 2, cols], psn[0:64, cols], -nrm, None, ALU.mult, None, [psn], [ws])
        for tb in range(NB):
            TB = slice(tb * 512, (tb + 1) * 512)
            for mi, src in enumerate((dftc_in, dfts_in)):
                m = mats[mi]
                for half in range(2):
                    hs = slice(half * (NT // 2), (half + 1) * (NT // 2)) if NT >= 2 else slice(0, NT)
                    if NT < 2 and half == 1:
                        continue
                    k.dma("sp", m[:, hs, :], src[:, TB].rearrange("(a p) n -> p a n", p=128)[:, hs, :], [src], [m])
                dst = a1 if mi == 0 else a2
                for cb in range(2):
                    ps = PS[2 + cb]
                    for c in range(NT):
                        k.mm(ps[:], za[:, c, cb * 128:(cb + 1) * 128], m[:, c, :], c == 0, c == NT - 1, [za, m], [ps])
                    k.cp("act" if cb else "dve", dst[:, cb, :], ps[:], [ps], [dst])
            for eb in range(2):
                ps = PS[4 + eb]
                i = 0
                for (w_, a_) in ((wc, a1), (ws, a2)):
                    for cb in range(2):
                        k.mm(ps[:], w_[:, cb, eb * 128:(eb + 1) * 128], a_[:, cb, :], i == 0, i == 3, [w_, a_], [ps])
                        i += 1
                k.cp("act", yo[:, eb, :], ps[:], [ps], [yo])
            k.dma("sp", YA[s][:, TB].rearrange("(e p) t -> p e t", p=128), yo[:], [yo], [YA[s]])
        k.barrier()


def mla_phase(nc, k, sb, l, s, T, cst, PS, QT, KT, VA, YC):
    NT = T // 128
    NB = T // 512
    with contextlib.ExitStack() as es:
        qt = [sb(es, "m_qt%d" % i, [128, T], BF) for i in range(2)]
        kt = [sb(es, "m_kt%d" % i, [128, T], BF) for i in range(2)]
        va = [sb(es, "m_va%d" % i, [128, NT, 65], BF) for i in range(2)]
        pt = [sb(es, "m_pt%d" % i, [128, 512], BF) for i in range(4)]
        osb = sb(es, "m_o", [128, 512], F32)
        rc = sb(es, "m_rc", [128, 512], F32)
        yo = [sb(es, "m_yo%d" % i, [64, 512], BF) for i in range(2)]
        pi = 0
        def ld(h):
            k.dma("sp", qt[h % 2][0:97, :], QT[s][h], [QT[s]], [qt[h % 2]])
            k.dma("sp", kt[h % 2][0:97, :], KT[s][h], [KT[s]], [kt[h % 2]])
            k.dma("sp", va[h % 2][:], VA[s][:, h * 65:(h + 1) * 65].rearrange("(a p) c -> p a c", p=128), [VA[s]], [va[h % 2]])

        ld(0)
        pending = []
        for h in range(6):
            q_, k_, v_ = qt[h % 2], kt[h % 2], va[h % 2]
            if h + 1 < 6:
                ld(h + 1)
            for qb in range(NB):
                QB = slice(qb * 512, (qb + 1) * 512)
                po = PS[4 + qb % 2]
                pq_ = {}
                for kc in range(NT + 2):
                    if kc == min(6, NT) and pending:
                        pending.pop(0)()
                    if kc < NT:
                        ps = PS[kc % 4]
                        k.mm(ps[:], k_[0:97, kc * 128:(kc + 1) * 128], q_[0:97, QB], True, True, [k_, q_], [ps])
                        p_ = pt[pi % 4]
                        pi += 1
                        k.act(p_[:], ps[:], AF.Exp, [ps], [p_])
                        pq_[kc] = p_
                    if kc >= 2:
                        j = kc - 2
                        k.mm(po[0:65, :], v_[:, j, :], pq_[j][:], j == 0, j == NT - 1, [v_, pq_[j]], [po])
                def epi(po=po, h=h, qb=qb, QB=QB):
                    k.cp("dve", osb[0:65, :], po[0:65, :], [po], [osb])
                    k.recip(rc[64:65, :], osb[64:65, :], [osb], [rc])
                    pbc = PS[6]
                    k.mm(pbc[0:64, :], cst[64:65, C_OA:C_OA + 64], rc[64:65, :], True, True, [cst, rc], [pbc])
                    y_ = yo[(h * NB + qb) % 2]
                    k.tt("dve", y_[:], osb[0:64, :], pbc[0:64, :], ALU.mult, [osb, pbc], [y_])
                    k.dma("sp", YC[s][h * 64:(h + 1) * 64, QB], y_[:], [y_], [YC[s]])
                pending.append(epi)
        while pending:
            pending.pop(0)()
        k.barrier()


def rwkv_phase(nc, k, sb, l, s, T, cst, PS, PSB, identb, pp, ppx, ZR, YB, w2_in, a2_in):
    NSL = T // 512
    ident = cst[:, C_ID:C_ID + 128]
    with contextlib.ExitStack() as es:
        w2 = sb(es, "r_w2", [128, 384], F32)
        a2w = sb(es, "r_a2", [128, 384], F32)
        k.dma("sp", w2[:], w2_in[l], [w2_in], [w2])
        k.dma("sp", a2w[:], a2_in[l], [a2_in], [a2w])
        zr = sb(es, "r_zr", [128, 11, 514], F32)
        zs = sb(es, "r_zs", [128, 11, 512], F32)
        th = sb(es, "r_th", [128, 512], F32)
        lw = sb(es, "r_lw", [128, 3, 512], F32)
        ic = sb(es, "r_ic", [128, 3, 512], F32)
        kk = sb(es, "r_kk", [128, 3, 512], F32)
        tA = sb(es, "r_tA", [128, 3, 512], F32)
        tB = sb(es, "r_tB", [128, 3, 512], F32)
        cum = sb(es, "r_cum", [128, 3, 512], F32)
        epos = sb(es, "r_ep", [128, 3, 512], F32)
        eneg = sb(es, "r_en", [128, 3, 512], F32)
        AR = sb(es, "r_AR", [128, 3, 2, 512], RDT)
        BK = sb(es, "r_BK", [128, 3, 2, 512], RDT)
        VV = sb(es, "r_VV", [128, 3, 512], RDT)
        Y = sb(es, "r_Y", [128, 3, 512], F32)
        Y0 = sb(es, "r_Y0", [128, 3, 512], F32)
        ST = sb(es, "r_ST", [128, 3, 64], RDT)
        TOKB = [[sb(es, "r_TOK%d_%d" % (i, b), [128, 384], RDT) for b in range(3)] for i in range(3)]
        NU = 6
        NAr = [[sb(es, "r_NAr%d_%d" % (j, i), [128, 256], RDT) for i in range(NU)] for j in range(2)]
        KA = [[sb(es, "r_KA%d_%d" % (j, i), [128, 256], RDT) for i in range(NU)] for j in range(2)]
        NN = [[[sb(es, "r_NN%d_%d_%d" % (q, i, j), [128, 256], RDT) for j in range(2)] for i in range(NU)] for q in range(2)]
        TM = [[[sb(es, "r_TM%d_%d_%d" % (q, i, j), [128, 128], RDT) for j in range(2)] for i in range(NU)] for q in range(2)]
        X0 = [sb(es, "r_X0%d" % i, [128, 64], RDT) for i in range(NU)]
        UT = [sb(es, "r_UT%d" % i, [128, 64], RDT) for i in range(NU)]
        ST32 = sb(es, "r_ST32", [128, 3, 64], F32)
        if RDT == BF:
            ptile, idn, idt = PSB, identb[:], identb
        else:
            ptile, idn, idt = PS[6], ident, cst
        for d in range(2):
            k.memset("pool", ST[:], 0.0, [ST])
            k.memset("pool", ST32[:], 0.0, [ST32])
            slabs = list(range(NSL)) if d == 0 else list(range(NSL - 1, -1, -1))

            def load_zr(sl_):
                S0_ = sl_ * 512
                lo = 1 if sl_ == 0 else 0
                hi = 1 if sl_ == NSL - 1 else 0
                if lo:
                    k.memset("pool", zr[:, :, 0:1], 0.0, [zr])
                if hi:
                    k.memset("pool", zr[:, :, 513:514], 0.0, [zr])
                k.dma("sp", zr[:, :, lo:514 - hi], ZR[s][:, S0_ - 1 + lo:S0_ + 513 - hi].rearrange("(b p) t -> p b t", p=128), [ZR[s]], [zr])
            m2 = cst[:, C_SF:C_SF + 256] if d == 0 else cst[:, C_SB:C_SB + 256]
            mT = cst[:, C_SB:C_SB + 128] if d == 0 else cst[:, C_SF:C_SF + 128]
            for sl in slabs:
                S0 = sl * 512
                if sl == slabs[0]:
                    load_zr(sl)
                for b in range(11):
                    k.act(zs[:, b, :], zr[:, b, 1:513], AF.Identity, [zr, ppx], [zs], scale=ppx[:, b:b + 1])
                    k.stt(zs[:, b, :], zr[:, b, 0:512], pp[:, 8 + b:9 + b], zs[:, b, :], ALU.mult, ALU.add, [zr, pp, zs], [zs])
                    k.stt(zs[:, b, :], zr[:, b, 2:514], pp[:, 19 + b:20 + b], zs[:, b, :], ALU.mult, ALU.add, [zr, pp, zs], [zs])
                si_ = slabs.index(sl)
                if si_ + 1 < len(slabs):
                    load_zr(slabs[si_ + 1])
                DR = slice(d * 64, d * 64 + 64)
                B3 = range(3)
                k.act(th[DR, :], zs[DR, 9, :], AF.Tanh, [zs], [th])
                for b in B3:
                    k.mm(PS[b][:], w2[DR, b * 128:(b + 1) * 128], th[DR, :], True, True, [w2, th], [PS[b]])
                for b in B3:
                    k.mm(PS[3 + b][:], a2w[DR, b * 128:(b + 1) * 128], zs[DR, 10, :], True, True, [a2w, zs], [PS[3 + b]])
                for b in B3:
                    k.ts("dve", kk[:, b, :], zs[:, 3 + b, :], pp[:, 42 + b:43 + b], None, ALU.mult, None, [zs, pp], [kk])
                for b in B3:
                    k.act(lw[:, b, :], PS[b][:], AF.Sigmoid, [PS[b], pp], [lw], bias=pp[:, 30 + d * 3 + b:31 + d * 3 + b])
                for b in B3:
                    k.act(ic[:, b, :], PS[3 + b][:], AF.Sigmoid, [PS[3 + b], pp], [ic], bias=pp[:, 36 + d * 3 + b:37 + d * 3 + b])
                for b in B3:
                    k.act(tA[:, b, :], kk[:, b, :], AF.Square, [kk], [tA])
                for b in B3:
                    k.mm(PS[b][:], cst[:, C_OB:C_OB + 128], tA[:, b, :], True, True, [cst, tA], [PS[b]])
                for b in B3:
                    k.ts("dve", lw[:, b, :], lw[:, b, :], -float(np.exp(-0.5)), None, ALU.mult, None, [lw], [lw])
                for b in B3:
                    k.op("dve", lambda b=b: nc.vector.tensor_tensor_scan(
                        out=cum[:, b, :], data0=cst[:, C_SEG:C_SEG + 512], data1=lw[:, b, :], initial=0.0,
                        op0=ALU.mult, op1=ALU.add), [cst, lw], [cum])
                for b in B3:
                    k.act(tB[:, b, :], PS[b][:], AF.Sqrt, [PS[b]], [tB])
                if d == 1:
                    for b in B3:
                        for c in range(4):
                            cs_ = slice(c * 128, (c + 1) * 128)
                            k.stt(tA[:, b, cs_], cum[:, b, cs_], cum[:, b, c * 128 + 127:c * 128 + 128], lw[:, b, cs_],
                                  ALU.subtract, ALU.subtract, [cum, lw], [tA])
                    for b in B3:
                        k.ts("dve", cum[:, b, :], tA[:, b, :], -1.0, None, ALU.mult, None, [tA], [cum])
                for b in B3:
                    k.ts("dve", tB[:, b, :], tB[:, b, :], 1e-12, None, ALU.max, None, [tB], [tB])
                for b in B3:
                    k.act(epos[:, b, :], cum[:, b, :], AF.Exp, [cum], [epos])
                for b in B3:
                    k.act(eneg[:, b, :], cum[:, b, :], AF.Exp, [cum], [eneg], scale=-1.0)
                for b in B3:
                    k.tt("pool", tA[:, b, :], cum[:, b, :], lw[:, b, :], ALU.subtract, [cum, lw], [tA])
                for b in B3:
                    k.recip(tB[:, b, :], tB[:, b, :], [tB], [tB])
                for b in B3:
                    k.act(tA[:, b, :], tA[:, b, :], AF.Exp, [tA], [tA])
                for b in B3:
                    k.tt("dve", kk[:, b, :], kk[:, b, :], tB[:, b, :], ALU.mult, [kk, tB], [kk])
                for b in B3:
                    k.tt("pool", AR[:, b, 1, :], zs[:, b, :], epos[:, b, :], ALU.mult, [zs, epos], [AR])
                for b in B3:
                    k.stt(AR[:, b, 0, :], kk[:, b, :], -1.0, tA[:, b, :], ALU.mult, ALU.mult, [kk, tA], [AR])
                for b in B3:
                    k.tt("dve", tB[:, b, :], kk[:, b, :], ic[:, b, :], ALU.mult, [kk, ic], [tB])
                for b in B3:
                    k.cp("act", VV[:, b, :], zs[:, 6 + b, :], [zs], [VV])
                for b in B3:
                    k.tt("dve", BK[:, b, 0, :], tB[:, b, :], eneg[:, b, :], ALU.mult, [tB, eneg], [BK])
                for b in B3:
                    k.ts("dve", tB[:, b, :], ic[:, b, :], pp[:, 45 + b:46 + b], ppx[:, 11 + b:12 + b], ALU.mult, ALU.add, [ic, pp, ppx], [tB])
                for b in B3:
                    k.tt("pool", tB[:, b, :], tB[:, b, :], zs[:, 3 + b, :], ALU.mult, [tB, zs], [tB])
                for b in B3:
                    k.tt("dve", BK[:, b, 1, :], tB[:, b, :], eneg[:, b, :], ALU.mult, [tB, eneg], [BK])
                if d == 1:
                    k.dma("sp", Y0[:], YB[s][:, S0:S0 + 512].rearrange("(b p) t -> p b t", p=128), [YB[s]], [Y0])
                chunks = range(4) if d == 0 else range(3, -1, -1)
                TMC = {}

                def stage12(c):
                        cs_ = slice(c * 128, (c + 1) * 128)
                        gcol = c * 128 + 127 if d == 0 else c * 128
                        tkl = TOKB[c % 3]
                        HS = range(6)
                        HR = [slice((h % 2) * 64, (h % 2) * 64 + 64) for h in HS]
                        HB = [h // 2 for h in HS]
                        cs_ = slice(c * 128, (c + 1) * 128)
                        gcol = c * 128 + 127 if d == 0 else c * 128
                        tkl = TOKB[c % 3]
                        yield
                        for b in range(3):
                            k.tr(ptile[:, 0:128], BK[:, b, 0, cs_], idn, [BK, idt], [ptile])
                            k.tr(ptile[:, 128:256], BK[:, b, 1, cs_], idn, [BK, idt], [ptile])
                            k.tr(ptile[:, 256:384], VV[:, b, cs_], idn, [VV, idt], [ptile])
                            k.cp("act", tkl[b][:], ptile[:, 0:384], [ptile], [tkl[b]])
                        HS = range(6)
                        HR = [slice((h % 2) * 64, (h % 2) * 64 + 64) for h in HS]
                        HB = [h // 2 for h in HS]
                        yield
                        for h in HS:
                            b, bank = HB[h], PS[h]
                            k.mm(bank[:, 0:256], BK[HR[h], b, 0, cs_], AR[HR[h], b, :, cs_], True, True, [BK, AR], [bank])
                            k.mm(bank[:, 256:512], BK[HR[h], b, 1, cs_], AR[HR[h], b, :, cs_], True, True, [BK, AR], [bank])
                        yield
                        for h in HS:
                            bank = PS[h]
                            k.tt("dve", NAr[c % 2][h][:], bank[:, 0:256], m2, ALU.mult, [bank, cst], [NAr[c % 2][h]])
                            k.tt("dve", KA[c % 2][h][:], bank[:, 256:512], m2, ALU.mult, [bank, cst], [KA[c % 2][h]])
                        yield
                        for h in HS:
                            b, bank = HB[h], PS[h]
                            k.mm(bank[:, 0:128], AR[HR[h], b, 0, cs_], BK[HR[h], b, 0, cs_], True, True, [BK, AR], [bank])
                        cur = {}
                        tmc = {}
                        TMC[c] = tmc
                        yield
                        for h in HS:
                            bank = PS[h]
                            nn0 = NN[c % 2][h][0]
                            k.tt("dve", nn0[:, 128:256], bank[:, 0:128], mT, ALU.mult, [bank, cst], [nn0])
                            k.cp("act", nn0[:, 0:128], NAr[c % 2][h][:, 0:128], [NAr[c % 2][h]], [nn0])
                            k.tt("pool", TM[c % 2][h][0][:], NAr[c % 2][h][:, 0:128], ident, ALU.add, [NAr[c % 2][h], cst], [TM[c % 2][h][0]])
                            cur[h] = nn0
                            tmc[h] = TM[c % 2][h][0]
                        yield
                        for lev in range(6):
                            yield
                            for h in HS:
                                bank = PS[h]
                                if lev < 5:
                                    k.mm(bank[:, 0:128], cur[h][:, 128:256], cur[h][:, 0:128], True, True, [cur[h]], [bank])
                                k.mm(bank[:, 128:256], cur[h][:, 0:128], cur[h][:, 128:256], True, True, [cur[h]], [bank])
                            yield
                            for h in HS:
                                nxt = NN[c % 2][h][(lev + 1) % 2]
                                if lev < 5:
                                    k.cp("act", nxt[:], PS[h][:, 0:256], [PS[h]], [nxt])
                                else:
                                    k.cp("act", nxt[:, 128:256], PS[h][:, 128:256], [PS[h]], [nxt])
                                cur[h] = nxt
                            yield
                            for h in HS:
                                k.mm(PS[h][:, 256:384], cur[h][:, 128:256], tmc[h][:], True, True, [cur[h], tmc[h]], [PS[h]])
                            yield
                            for h in HS:
                                tm2 = TM[c % 2][h][(lev + 1) % 2]
                                k.tt("dve", tm2[:], PS[h][:, 256:384], tmc[h][:], ALU.add, [PS[h], tmc[h]], [tm2])
                                tmc[h] = tm2

                        yield

                def stage3(c):
                    cs_ = slice(c * 128, (c + 1) * 128)
                    gcol = c * 128 + 127 if d == 0 else c * 128
                    tkl = TOKB[c % 3]
                    HS = range(6)
                    HR = [slice((h % 2) * 64, (h % 2) * 64 + 64) for h in HS]
                    HB = [h // 2 for h in HS]
                    bank = PS[6]
                    XR = [slice(h * 64, (h + 1) * 64) for h in HS]
                    tv = [tkl[HB[h]][:, 256 + (h % 2) * 64:256 + (h % 2) * 64 + 64] for h in HS]
                    tb_ = [tkl[HB[h]][:, (h % 2) * 64:(h % 2) * 64 + 64] for h in HS]
                    tk_ = [tkl[HB[h]][:, 128 + (h % 2) * 64:128 + (h % 2) * 64 + 64] for h in HS]
                    yield
                    for h in HS:
                        b = HB[h]
                        k.mm(bank[:, XR[h]], AR[HR[h], b, 0, cs_], ST[HR[h], b, :], True, False, [AR, ST], [bank])
                        k.mm(bank[:, XR[h]], KA[c % 2][h][:, 0:128], tv[h], False, True, [KA[c % 2][h], tkl[b]], [bank])
                    yield
                    for h in HS:
                        k.cp("act", X0[h][:], bank[:, XR[h]], [bank], [X0[h]])
                    yield
                    for h in HS:
                        k.mm(bank[:, XR[h]], TMC[c][h][:], X0[h][:], True, True, [TMC[c][h], X0[h]], [bank])
                    yield
                    for h in HS:
                        k.cp("act", UT[h][:], bank[:, XR[h]], [bank], [UT[h]])
                    for pr in range(3):
                        yield
                        for h in (2 * pr, 2 * pr + 1):
                            b = HB[h]
                            o = (h % 2) * 192
                            k.mm(bank[0:64, o:o + 128], ST[HR[h], b, :], AR[HR[h], b, 1, cs_], True, False, [ST, AR], [bank])
                            k.mm(bank[0:64, o:o + 128], UT[h][:], NAr[c % 2][h][:, 128:256], False, False, [UT[h], NAr[c % 2][h]], [bank])
                            k.mm(bank[0:64, o:o + 128], tv[h], KA[c % 2][h][:, 128:256], False, True, [tkl[b], KA[c % 2][h]], [bank])
                            k.mm(bank[0:64, o + 128:o + 192], tb_[h], UT[h][:], True, False, [tkl[b], UT[h]], [bank])
                            k.mm(bank[0:64, o + 128:o + 192], tk_[h], tv[h], False, True, [tkl[b]], [bank])
                        yield
                        for h in (2 * pr, 2 * pr + 1):
                            b = HB[h]
                            o = (h % 2) * 192
                            if d == 0:
                                k.cp("act", Y[HR[h], b, cs_], bank[0:64, o:o + 128], [bank], [Y])
                            else:
                                k.tt("dve", Y[HR[h], b, cs_], bank[0:64, o:o + 128], Y0[HR[h], b, cs_], ALU.add, [bank, Y0], [Y])
                            k.tt("dve", ST32[HR[h], b, :], bank[0:64, o + 128:o + 192], ST32[HR[h], b, :], ALU.add, [bank, ST32], [ST32])
                    yield
                    for b in range(3):
                        k.ts("pool", ST32[:, b, :], ST32[:, b, :], epos[:, b, gcol:gcol + 1], None, ALU.mult, None, [ST32, epos], [ST32])
                    k.cp("act", ST[:], ST32[:], [ST32], [ST])
                    yield

                clist = list(chunks)
                for _ in stage12(clist[0]):
                    pass
                for ci, c in enumerate(clist):
                    g3 = stage3(c)
                    g12 = stage12(clist[ci + 1]) if ci + 1 < len(clist) else iter(())
                    done12 = done3 = False
                    while not (done12 and done3):
                        if not done12:
                            try:
                                next(g12)
                            except StopIteration:
                                done12 = True
                        if not done3:
                            try:
                                next(g3)
                            except StopIteration:
                                done3 = True
                if d == 1:
                    OB = cst[:, C_OB:C_OB + 128]
                    B3 = range(3)
                    for b in B3:
                        k.mm(PS[b][:], a2w[0:64, b * 128:(b + 1) * 128], zs[0:64, 10, :], True, True, [a2w, zs], [PS[b]])
                    for b in B3:
                        k.mm(PS[3 + b][:], OB, Y[:, b, :], True, True, [cst, Y], [PS[3 + b]])
                    for b in B3:
                        k.act(tA[:, b, :], PS[b][:], AF.Sigmoid, [PS[b], pp], [tA], bias=pp[:, 36 + b:37 + b])
                    for b in B3:
                        k.stt(cum[:, b, :], PS[3 + b][:], -1.0 / 64, Y[:, b, :], ALU.mult, ALU.add, [PS[3 + b], Y], [cum])
                    for b in B3:
                        k.act(eneg[:, b, :], cum[:, b, :], AF.Square, [cum], [eneg])
                    for b in B3:
                        k.mm(PS[3 + b][:], OB, eneg[:, b, :], True, True, [cst, eneg], [PS[3 + b]])
                    for b in B3:
                        k.tt("dve", tA[:, b, :], tA[:, b, :], ic[:, b, :], ALU.add, [tA, ic], [tA])
                    for b in B3:
                        k.ts("dve", tA[:, b, :], tA[:, b, :], pp[:, 45 + b:46 + b], ppx[:, 14 + b:15 + b], ALU.mult, ALU.add, [tA, pp, ppx], [tA])
                    for b in B3:
                        k.ts("dve", epos[:, b, :], PS[3 + b][:], 1.0 / 64, 64e-5, ALU.mult, ALU.add, [PS[3 + b]], [epos])
                    for b in B3:
                        k.act(epos[:, b, :], epos[:, b, :], AF.Sqrt, [epos], [epos])
                    for b in B3:
                        k.tt("dve", tA[:, b, :], tA[:, b, :], zs[:, 3 + b, :], ALU.mult, [tA, zs], [tA])
                    for b in B3:
                        k.stt(tA[:, b, :], zs[:, b, :], pp[:, 48 + b:49 + b], tA[:, b, :], ALU.mult, ALU.mult, [zs, pp, tA], [tA])
                    for b in B3:
                        k.mm(PS[b][:], OB, tA[:, b, :], True, True, [cst, tA], [PS[b]])
                    for b in B3:
                        k.recip(epos[:, b, :], epos[:, b, :], [epos], [epos])
                    for b in B3:
                        k.tt("dve", tB[:, b, :], PS[b][:], zs[:, 6 + b, :], ALU.mult, [PS[b], zs], [tB])
                    for b in B3:
                        k.tt("dve", cum[:, b, :], cum[:, b, :], epos[:, b, :], ALU.mult, [cum, epos], [cum])
                    for b in B3:
                        k.ts("dve", cum[:, b, :], cum[:, b, :], pp[:, 51 + b:52 + b], pp[:, 54 + b:55 + b], ALU.mult, ALU.add, [cum, pp], [cum])
                    for b in B3:
                        k.tt("pool", Y[:, b, :], cum[:, b, :], tB[:, b, :], ALU.add, [cum, tB], [Y])
                k.dma("sp", YB[s][:, S0:S0 + 512].rearrange("(b p) t -> p b t", p=128), Y[:], [Y], [YB[s]])
        k.barrier()


def out_phase(nc, k, sb, l, NS, T, cst, PS, PSB, identb, pp, HT, YA, YB, YC, X, out_d, wgate_in, wproj_in, wout_in, fg_in, last):
    NSL = T // 512
    with contextlib.ExitStack() as es:
        wg = sb(es, "o_wg", [128, 8, 4096], BF)
        wp = sb(es, "o_wp", [128, 8, D], BF)
        wo = sb(es, "o_wo", [128, 8, D], BF)
        stg = [sb(es, "o_stg%d" % i, [128, 2048], F32) for i in range(2)]
        si = 0
        for c in range(8):
            for hf in range(2):
                st = stg[si % 2]
                si += 1
                k.dma("sp", st[:], wgate_in[l, c * 128:(c + 1) * 128, hf * 2048:(hf + 1) * 2048], [wgate_in], [st])
                if si % 2:
                    k.act(wg[:, c, hf * 2048:(hf + 1) * 2048], st[:], AF.Identity, [st, pp], [wg], scale=pp[:, c:c + 1])
                else:
                    k.ts("dve", wg[:, c, hf * 2048:(hf + 1) * 2048], st[:], pp[:, c:c + 1], None, ALU.mult, None, [st, pp], [wg])
        for c in range(8):
            st = stg[si % 2]
            si += 1
            k.dma("sp", st[:, 0:D], wproj_in[l, c * 128:(c + 1) * 128, :], [wproj_in], [st])
            k.dma("sp", st[:, D:2 * D], wout_in[l, c * 128:(c + 1) * 128, :], [wout_in], [st])
            k.cp("dve", wp[:, c, :], st[:, 0:D], [st], [wp])
            k.cp("act", wo[:, c, :], st[:, D:2 * D], [st], [wo])
        hTs = [sb(es, "o_hT%d" % i, [128, 8, 512], BF) for i in range(2)]
        yas = [sb(es, "o_ya%d" % i, [128, 2, 512], BF) for i in range(2)]
        ybs = [sb(es, "o_yb%d" % i, [128, 3, 512], F32) for i in range(2)]
        ycs = [sb(es, "o_yc%d" % i, [128, 3, 512], BF) for i in range(2)]
        gs = [sb(es, "o_gs%d" % i, [128, 512], F32) for i in range(3)]
        yg = sb(es, "o_yg", [128, 8, 512], BF)
        mg = sb(es, "o_mg", [128, 8, 512], BF)
        tm_ = sb(es, "o_tm", [128, 512], F32)
        tm2 = sb(es, "o_tm2", [128, 512], F32)
        xb = [sb(es, "o_xb%d" % i, [128, D], F32) for i in range(2)]
        xn = [sb(es, "o_xn%d" % i, [128, D], F32) for i in range(2)]
        junk = sb(es, "o_junk", [128, D], F32)
        st4 = sb(es, "o_st4", [128, 8], F32)
        fgb = sb(es, "o_fgb", [128, D], F32)
        if last:
            for p in range(128):
                k.dma("sp", fgb[p:p + 1, :], fg_in[:, :], [fg_in], [fgb])
        gi = 0
        xi = 0
        work = [(s, sl) for s in range(NS) for sl in range(NSL)]

        def ld(wi):
            s_, sl_ = work[wi]
            j = wi % 2
            SL_ = slice(sl_ * 512, (sl_ + 1) * 512)
            k.dma("sp", hTs[j][:], HT[s_][:, :, SL_], [HT[s_]], [hTs[j]])
            k.dma("sp", yas[j][:], YA[s_][:, SL_].rearrange("(b p) t -> p b t", p=128), [YA[s_]], [yas[j]])
            k.dma("sp", ybs[j][:], YB[s_][:, SL_].rearrange("(b p) t -> p b t", p=128), [YB[s_]], [ybs[j]])
            k.dma("sp", ycs[j][:], YC[s_][:, SL_].rearrange("(b p) t -> p b t", p=128), [YC[s_]], [ycs[j]])

        ld(0)
        for wi, (s, sl) in enumerate(work):
            if True:
                SL = slice(sl * 512, (sl + 1) * 512)
                hT, ya, yb, yc = hTs[wi % 2], yas[wi % 2], ybs[wi % 2], ycs[wi % 2]
                if wi + 1 < len(work):
                    ld(wi + 1)
                for gb in range(8):
                    ps = PS[gb % 2]
                    for c in range(8):
                        k.mm(ps[:], wg[:, c, gb * 128:(gb + 1) * 128], hT[:, c, :], c == 0, c == 7, [wg, hT], [ps])
                    g_ = gs[gi % 3]
                    gi += 1
                    k.act(g_[:], ps[:], AF.Sigmoid, [ps], [g_])
                    k.tt("dve", g_[:], g_[:], ps[:], ALU.mult, [g_, ps], [g_])
                    if gb < 2:
                        src, srct = ya[:, gb, :], ya
                    elif gb < 5:
                        src, srct = yb[:, gb - 2, :], yb
                    else:
                        src, srct = yc[:, gb - 5, :], yc
                    k.tt("dve", yg[:, gb, :], g_[:], src, ALU.mult, [g_, srct], [yg])
                for ob in range(8):
                    OBS = slice(ob * 128, (ob + 1) * 128)
                    pa, pb, pc = PS[2], PS[3], PS[4]
                    for i, cb in enumerate((0, 1)):
                        k.mm(pa[:], wp[:, cb, OBS], yg[:, cb, :], i == 0, i == 1, [wp, yg], [pa])
                    for i, cb in enumerate((2, 3, 4)):
                        k.mm(pb[:], wp[:, cb, OBS], yg[:, cb, :], i == 0, i == 2, [wp, yg], [pb])
                    for i, cb in enumerate((5, 6, 7)):
                        k.mm(pc[:], wp[:, cb, OBS], yg[:, cb, :], i == 0, i == 2, [wp, yg], [pc])
                    sg = []
                    for j in range(3):
                        ps = PS[j % 2]
                        col = 1024 + j * 1024 + ob * 128
                        for c in range(8):
                            k.mm(ps[:], wg[:, c, col:col + 128], hT[:, c, :], c == 0, c == 7, [wg, hT], [ps])
                        g_ = gs[gi % 3]
                        gi += 1
                        k.act(g_[:], ps[:], AF.Sigmoid, [ps], [g_])
                        sg.append(g_)
                    k.tt("dve", tm_[:], pa[:], sg[0][:], ALU.mult, [pa, sg[0]], [tm_])
                    k.tt("dve", tm2[:], pb[:], sg[1][:], ALU.mult, [pb, sg[1]], [tm2])
                    k.tt("pool", tm_[:], tm_[:], tm2[:], ALU.add, [tm_, tm2], [tm_])
                    k.tt("dve", tm2[:], pc[:], sg[2][:], ALU.mult, [pc, sg[2]], [tm2])
                    k.tt("pool", mg[:, ob, :], tm_[:], tm2[:], ALU.add, [tm_, tm2], [mg])
                for tt in range(4):
                    tok0 = sl * 512 + tt * 128
                    xt = xb[xi % 2]
                    xo = xn[xi % 2]
                    xi += 1
                    k.dma("sp", xt[:], X[s][tok0:tok0 + 128, :], [X[s]], [xt])
                    for hf in range(2):
                        ps = PS[5 + hf]
                        for ob in range(8):
                            k.mm(ps[:], mg[:, ob, tt * 128:(tt + 1) * 128], wo[:, ob, hf * 512:(hf + 1) * 512], ob == 0, ob == 7, [mg, wo], [ps])
                        k.tt("dve", xo[:, hf * 512:(hf + 1) * 512], ps[:], xt[:, hf * 512:(hf + 1) * 512], ALU.add, [ps, xt], [xo])
                    if not last:
                        k.dma("sp", X[s][tok0:tok0 + 128, :], xo[:], [xo], [X[s]])
                    else:
                        k.act(junk[:], xo[:], AF.Square, [xo], [junk, st4], accum_out=st4[:, 0:1])
                        k.ts("dve", st4[:, 1:2], st4[:, 0:1], 1.0 / D, 1e-6, ALU.mult, ALU.add, [st4], [st4])
                        k.act(st4[:, 2:3], st4[:, 1:2], AF.Sqrt, [st4], [st4])
                        k.recip(st4[:, 3:4], st4[:, 2:3], [st4], [st4])
                        k.stt(xo[:], xo[:], st4[:, 3:4], fgb[:], ALU.mult, ALU.mult, [xo, st4, fgb], [xo])
                        k.dma("sp", out_d[s, tok0:tok0 + 128, :], xo[:], [xo], [out_d])
        k.barrier()


def _consts():
    c = np.zeros((128, NCST), np.float32)
    c[:, C_ID:C_ID + 128] = np.eye(128)
    j = np.arange(128)[:, None]
    t = np.arange(128)[None, :]
    c[:, C_SF:C_SF + 128] = j < t
    c[:, C_IF:C_IF + 128] = j <= t
    c[:, C_SB:C_SB + 128] = j > t
    c[:, C_IB:C_IB + 128] = j >= t
    c[:, C_OB:C_OB + 128] = (j // 64) == (t // 64)
    c[:, C_OA:C_OA + 128] = 1.0
    seg = np.ones(512, np.float32)
    seg[::128] = 0.0
    c[:, C_SEG:C_SEG + 512] = seg[None, :]
    a = np.arange(64)
    ang = 2 * np.pi * np.outer(a, a) / 64.0
    c[0:64, C_C64:C_C64 + 64] = np.cos(ang)
    c[0:64, C_S64:C_S64 + 64] = np.sin(ang)
    inv = (10000.0 ** (-np.arange(0, 32, 2, dtype=np.float32) / np.float32(32))).astype(np.float32)
    for p in range(64, 96):
        c[p, C_INVF] = inv[(p - 64) % 16]
        c[p, C_SGN] = -1.0 if p < 80 else 1.0
    c[0:96, C_E96 + 96] = 1.0
    return c


def _dft(T):
    n = np.arange(T, dtype=np.int64)
    m = (np.outer(n, n) % T).astype(np.float64) * (2 * np.pi / T)
    return np.cos(m).astype(ml_dtypes.bfloat16), np.sin(m).astype(ml_dtypes.bfloat16)


def host_inputs(inp, T, NL, NS, ncores):
    f = lambda a: np.ascontiguousarray(np.asarray(a), dtype=np.float32)
    w_in = f(inp["w_in"])
    wmix = np.concatenate([
        w_in[:, :, O_ZR:O_ZR + 1408], w_in[:, :, O_ZQ:O_ZQ + 256], w_in[:, :, O_ZKV:O_ZKV + 128],
        w_in[:, :, O_ZKV:O_ZKV + 64], w_in[:, :, O_ZKR:O_ZKR + 32],
        w_in[:, :, O_ZKV:O_ZKV + 64], w_in[:, :, O_ZKR + 16:O_ZKR + 32], w_in[:, :, O_ZKR:O_ZKR + 16],
        w_in[:, :, O_ZA:O_ZA + 256]], axis=2)
    assert wmix.shape[2] == CM
    wgate = np.concatenate([w_in[:, :, O_ZAG:O_ZAG + 256], w_in[:, :, O_ZBG:O_ZBG + 384],
                            w_in[:, :, O_ZCG:O_ZCG + 384], w_in[:, :, O_ZM:O_ZM + 3072]], axis=2)
    wproj = np.concatenate([f(inp["proj_a"]), f(inp["proj_b"]), f(inp["proj_c"])], axis=1)
    pp = np.zeros((NL, 128, NPP), np.float32)

    def blk(v, n):
        return np.transpose(v.reshape(NL, n, 128), (0, 2, 1))

    pp[:, :, 0:8] = blk(f(inp["norm_g"]), 8)
    pp[:, :, 8:19] = blk(f(inp["shift_mu_prev"]), 11)
    pp[:, :, 19:30] = blk(f(inp["shift_mu_next"]), 11)
    pp[:, :, 30:36] = blk(f(inp["decay_w0"]).reshape(NL, 768), 6)
    pp[:, :, 36:42] = blk(f(inp["iclr_a0"]).reshape(NL, 768), 6)
    pp[:, :, 42:45] = blk(f(inp["key_k"]), 3)
    pp[:, :, 45:48] = blk(f(inp["key_a"]), 3)
    pp[:, :, 48:51] = blk(f(inp["bonus_r_k"]).reshape(NL, 384), 3)
    pp[:, :, 51:54] = blk(f(inp["lnx_g"]), 3)
    pp[:, :, 54:57] = blk(f(inp["lnx_b"]), 3)
    pp[:, :, 57:59] = blk(f(inp["q_norm_g"]), 2)
    pp[:, :, 59:60] = blk(f(inp["kv_norm_g"]), 1)
    fw = np.transpose(f(inp["fourier_w"]), (0, 2, 1, 3)).reshape(NL, 64, 256)
    w2 = f(inp["decay_w2"]).reshape(NL, 128, 384)
    a2 = f(inp["iclr_a2"]).reshape(NL, 128, 384)
    wuq = f(inp["w_uq"])
    wq4 = wuq.reshape(NL, 256, 6, 96)
    wuqs = np.concatenate([wq4[..., 0:64], wq4[..., 80:96], wq4[..., 64:80]], axis=-1).reshape(NL, 256, 576)
    wkv4 = f(inp["w_ukv"]).reshape(NL, 128, 6, 128)
    wukvk = np.ascontiguousarray(wkv4[..., 0:64]).reshape(NL, 128, 384)
    wukvv = np.ascontiguousarray(wkv4[..., 64:128]).reshape(NL, 128, 384)
    dc, ds = _dft(T)
    shared = dict(cst=_consts(), dftc=dc, dfts=ds, wmix=np.ascontiguousarray(wmix), wgate=np.ascontiguousarray(wgate),
                  wproj=np.ascontiguousarray(wproj), wout=f(inp["w_out"]), pp=pp, fw=np.ascontiguousarray(fw), w2=w2, a2=a2,
                  wuq=np.ascontiguousarray(wuq), wuqs=np.ascontiguousarray(wuqs), wukvk=wukvk, wukvv=wukvv,
                  fg=f(inp["final_g"]).reshape(1, D))
    x = f(inp["x"])
    pos = np.ascontiguousarray(np.asarray(inp["positions"]), dtype=np.int32)
    maps = []
    for c in range(ncores):
        m = dict(shared)
        m["x"] = np.ascontiguousarray(x[c * NS:(c + 1) * NS])
        m["pos"] = np.ascontiguousarray(pos[c * NS:(c + 1) * NS]).reshape(NS, 1, T)
        maps.append(m)
    return maps


def kernel(**inputs):
    x = np.asarray(inputs["x"])
    B, T, _ = x.shape
    NL = np.asarray(inputs["w_in"]).shape[0]
    NS = B // NCORES
    nc, _ = build(T, NL, NS)
    maps = host_inputs(inputs, T, NL, NS, NCORES)
    res = run_bass_kernel_spmd(nc, maps, core_ids=list(range(NCORES)))
    return np.concatenate([np.asarray(r["out"], dtype=np.float32) for r in res.results], axis=0)
```
